# Optimizing a Trainium2 kernel written in Bass

```python
import math
import jax, jax.numpy as jnp
from jax import lax
import numpy as np

D_MODEL = 1024
BATCH = 16
SEQ = 2048
DEPTH = 2

GRID_W = 64
CTX_LEN = 256
N_MIXERS = 2
N_ATTN_LAYERS = (DEPTH + 1) // 2
N_SSM_LAYERS = DEPTH // 2
HEAD_DIM = 128
N_HEADS = D_MODEL // HEAD_DIM
N_KV_HEADS = 2
KV_REP = N_HEADS // N_KV_HEADS
Q_BLOCK = 128
ROPE_THETA = 10000.0
SSM_GROUP = 16
SSM_GROUPS = D_MODEL // SSM_GROUP
SSM_STATE = 64
STEP_MIN = 1e-3
STEP_MAX = 1e-1
D_FF = 2816
CONV_W = 3
N_MOD = 6
EPS = 1e-6

kernel_name = "hybrid_attn_s5_convffn_dit"


def rms_norm(x, g):
    x32 = x.astype(jnp.float32)
    y = x32 * lax.rsqrt(jnp.mean(x32 * x32, axis=-1, keepdims=True) + EPS) * g.astype(jnp.float32)
    return y.astype(x.dtype)


def modulate(h, shift, scale):
    return h * (1 + scale) + shift


def axial_rope_tables(n_tokens, dtype):
    rows = n_tokens // GRID_W
    row = jnp.repeat(jnp.arange(rows, dtype=jnp.float32), GRID_W)
    col = jnp.tile(jnp.arange(GRID_W, dtype=jnp.float32), rows)
    pairs_per_axis = HEAD_DIM // 4
    freq = ROPE_THETA ** (-jnp.arange(pairs_per_axis, dtype=jnp.float32) / pairs_per_axis)
    ang = jnp.concatenate([row[:, None] * freq, col[:, None] * freq], axis=-1)
    return jnp.cos(ang)[:, None, :].astype(dtype), jnp.sin(ang)[:, None, :].astype(dtype)


def apply_rope(x, cos, sin):
    half = HEAD_DIM // 2
    x1, x2 = x[..., :half], x[..., half:]
    return jnp.concatenate([x1 * cos - x2 * sin, x2 * cos + x1 * sin], axis=-1)


def attend(q, k, v):
    s = jnp.einsum('bqgrd,bkgd->bgrqk', q, k).astype(jnp.float32) * (HEAD_DIM ** -0.5)
    p = jax.nn.softmax(s, axis=-1).astype(v.dtype)
    return jnp.einsum('bgrqk,bkgd->bqgrd', p, v)


def attention_mixer(h_lat, h_ctx, w_qkv, q_g, k_g, w_o, need_ctx):
    B, L, _ = h_lat.shape
    Lc = h_ctx.shape[1]

    def project(h):
        n = h.shape[1]
        qkv = h @ w_qkv
        q, k, v = jnp.split(qkv, [N_HEADS * HEAD_DIM, (N_HEADS + N_KV_HEADS) * HEAD_DIM], axis=-1)
        q = rms_norm(q.reshape(B, n, N_HEADS, HEAD_DIM), q_g)
        k = rms_norm(k.reshape(B, n, N_KV_HEADS, HEAD_DIM), k_g)
        v = v.reshape(B, n, N_KV_HEADS, HEAD_DIM)
        return q, k, v

    q_l, k_l, v_l = project(h_lat)
    q_c, k_c, v_c = project(h_ctx)
    cos, sin = axial_rope_tables(L, h_lat.dtype)
    q_l = apply_rope(q_l, cos, sin)
    k_l = apply_rope(k_l, cos, sin)
    k_all = jnp.concatenate([k_c, k_l], axis=1)
    v_all = jnp.concatenate([v_c, v_l], axis=1)

    nb = L // Q_BLOCK
    q_blocks = q_l.reshape(B, nb, Q_BLOCK, N_KV_HEADS, KV_REP, HEAD_DIM).transpose(1, 0, 2, 3, 4, 5)
    o_blocks = lax.map(lambda qb: attend(qb, k_all, v_all), q_blocks)
    o_l = o_blocks.transpose(1, 0, 2, 3, 4, 5).reshape(B, L, N_HEADS * HEAD_DIM) @ w_o
    o_c = None
    if need_ctx:
        q_c = q_c.reshape(B, Lc, N_KV_HEADS, KV_REP, HEAD_DIM)
        o_c = attend(q_c, k_c, v_c).reshape(B, Lc, N_HEADS * HEAD_DIM) @ w_o
    return o_l, o_c


def s5_discretise(lam_re, lam_im, log_step, b_re, b_im):
    dt = jnp.exp(log_step.astype(jnp.float32))[:, None]
    lam_re = lam_re.astype(jnp.float32)
    lam_im = lam_im.astype(jnp.float32)
    mag = jnp.exp(lam_re * dt)
    lb_re = mag * jnp.cos(lam_im * dt)
    lb_im = mag * jnp.sin(lam_im * dt)
    den = lam_re * lam_re + lam_im * lam_im
    nr, ni = lb_re - 1.0, lb_im
    f_re = ((nr * lam_re + ni * lam_im) / den)[..., None]
    f_im = ((ni * lam_re - nr * lam_im) / den)[..., None]
    b_re = b_re.astype(jnp.float32)
    b_im = b_im.astype(jnp.float32)
    bb_re = f_re * b_re - f_im * b_im
    bb_im = f_re * b_im + f_im * b_re
    return lb_re, lb_im, bb_re, bb_im


def _complex_combine(e1, e2):
    a1r, a1i, b1r, b1i = e1
    a2r, a2i, b2r, b2i = e2
    return (a2r * a1r - a2i * a1i,
            a2r * a1i + a2i * a1r,
            a2r * b1r - a2i * b1i + b2r,
            a2r * b1i + a2i * b1r + b2i)


def s5_scan(u, lb_re, lb_im, bb_re, bb_im, h0, reverse):
    n = u.shape[1]
    bu_re = jnp.einsum('blgc,gnc->blgn', u, bb_re)
    bu_im = jnp.einsum('blgc,gnc->blgn', u, bb_im)
    if h0 is not None:
        hr, hi = h0
        idx = -1 if reverse else 0
        bu_re = bu_re.at[:, idx].add(lb_re * hr - lb_im * hi)
        bu_im = bu_im.at[:, idx].add(lb_re * hi + lb_im * hr)
    a_re = jnp.broadcast_to(lb_re, (1, n) + lb_re.shape)
    a_im = jnp.broadcast_to(lb_im, (1, n) + lb_im.shape)
    _, _, h_re, h_im = lax.associative_scan(_complex_combine, (a_re, a_im, bu_re, bu_im), reverse=reverse, axis=1)
    return h_re, h_im


def s5_readout(h_re, h_im, c_re, c_im):
    return (jnp.einsum('blgn,gcn->blgc', h_re, c_re.astype(jnp.float32))
            - jnp.einsum('blgn,gcn->blgc', h_im, c_im.astype(jnp.float32)))


def s5_mixer(h_lat, h_ctx, lam_re, lam_im, log_step, b_re, b_im, c_re, c_im, d_skip, w_glu_a, w_glu_b, need_ctx):
    dtype = h_lat.dtype
    B, L, D = h_lat.shape
    Lc = h_ctx.shape[1]
    u_l = h_lat.astype(jnp.float32).reshape(B, L, SSM_GROUPS, SSM_GROUP)
    u_c = h_ctx.astype(jnp.float32).reshape(B, Lc, SSM_GROUPS, SSM_GROUP)
    y_l = jnp.zeros_like(u_l)
    y_c = jnp.zeros_like(u_c)
    for d in range(2):
        reverse = d == 1
        lb_re, lb_im, bb_re, bb_im = s5_discretise(lam_re[d], lam_im[d], log_step[d], b_re[d], b_im[d])
        hc_re, hc_im = s5_scan(u_c, lb_re, lb_im, bb_re, bb_im, None, reverse)
        fin = 0 if reverse else -1
        h0 = (hc_re[:, fin], hc_im[:, fin])
        if need_ctx:
            y_c = y_c + s5_readout(hc_re, hc_im, c_re[d], c_im[d])
        hl_re, hl_im = s5_scan(u_l, lb_re, lb_im, bb_re, bb_im, h0, reverse)
        y_l = y_l + s5_readout(hl_re, hl_im, c_re[d], c_im[d])
    d32 = d_skip.astype(jnp.float32)

    def glu_out(y, u, n):
        y = y.reshape(B, n, D) + d32 * u.reshape(B, n, D)
        g = jax.nn.gelu(y).astype(dtype)
        return (g @ w_glu_a) * jax.nn.sigmoid(g @ w_glu_b)

    o_l = glu_out(y_l, u_l, L)
    o_c = glu_out(y_c, u_c, Lc) if need_ctx else None
    return o_l, o_c


def conv_ffn(h, w_up, conv_w, conv_b, w_down):
    a = h @ w_up
    a = lax.conv_general_dilated(a, conv_w[:, None, :].astype(a.dtype), window_strides=(1,),
                                 padding=((CONV_W // 2, CONV_W // 2),),
                                 dimension_numbers=('NWC', 'WIO', 'NWC'),
                                 feature_group_count=2 * D_FF) + conv_b
    val, gate = jnp.split(a, 2, axis=-1)
    return (jax.nn.silu(gate) * val) @ w_down


def setup_inputs(seed: int = 0) -> dict:
    key = jax.random.key(seed)
    keys = jax.random.split(key, 32)
    counter = [0]

    def nxt():
        k = keys[counter[0]]
        counter[0] += 1
        return k

    def nrm(shape, scale):
        return jax.random.normal(nxt(), shape, jnp.float32) * scale

    D, F = D_MODEL, D_FF
    QKV = (N_HEADS + 2 * N_KV_HEADS) * HEAD_DIM
    G, N, C = SSM_GROUPS, SSM_STATE, SSM_GROUP
    n_idx = jnp.arange(N, dtype=jnp.float32)
    return {
        "x": nrm((BATCH, SEQ, D), 1.0),
        "c": nrm((BATCH, D), 1.0),
        "ctx": nrm((BATCH, CTX_LEN, D), 1.0),
        "c_ctx": nrm((D,), 1.0),
        "w_mod": nrm((DEPTH, D, N_MOD * D), 0.5 * D ** -0.5),
        "b_mod": nrm((DEPTH, N_MOD * D), 0.01),
        "g_pre_mix": 1.0 + nrm((DEPTH, D), 0.05),
        "g_post_mix": 1.0 + nrm((DEPTH, D), 0.05),
        "g_pre_ffn": 1.0 + nrm((DEPTH, D), 0.05),
        "g_post_ffn": 1.0 + nrm((DEPTH, D), 0.05),
        "w_qkv": nrm((N_ATTN_LAYERS, D, QKV), D ** -0.5),
        "q_norm_g": 1.0 + nrm((N_ATTN_LAYERS, HEAD_DIM), 0.05),
        "k_norm_g": 1.0 + nrm((N_ATTN_LAYERS, HEAD_DIM), 0.05),
        "w_o": nrm((N_ATTN_LAYERS, N_HEADS * HEAD_DIM, D), (N_HEADS * HEAD_DIM) ** -0.5),
        "ssm_lambda_re": -0.5 + nrm((N_SSM_LAYERS, 2, G, N), 0.01),
        "ssm_lambda_im": math.pi * n_idx + nrm((N_SSM_LAYERS, 2, G, N), 0.01),
        "ssm_log_step": jax.random.uniform(nxt(), (N_SSM_LAYERS, 2, G), jnp.float32,
                                           minval=math.log(STEP_MIN), maxval=math.log(STEP_MAX)),
        "ssm_b_re": nrm((N_SSM_LAYERS, 2, G, N, C), (2 * C) ** -0.5),
        "ssm_b_im": nrm((N_SSM_LAYERS, 2, G, N, C), (2 * C) ** -0.5),
        "ssm_c_re": nrm((N_SSM_LAYERS, 2, G, C, N), (2 * N) ** -0.5),
        "ssm_c_im": nrm((N_SSM_LAYERS, 2, G, C, N), (2 * N) ** -0.5),
        "ssm_d": nrm((N_SSM_LAYERS, D), 1.0),
        "w_glu_a": nrm((N_SSM_LAYERS, D, D), D ** -0.5),
        "w_glu_b": nrm((N_SSM_LAYERS, D, D), D ** -0.5),
        "w_up": nrm((DEPTH, D, 2 * F), D ** -0.5),
        "conv_w": nrm((DEPTH, CONV_W, 2 * F), CONV_W ** -0.5),
        "conv_b": nrm((DEPTH, 2 * F), 0.01),
        "w_down": nrm((DEPTH, F, D), F ** -0.5),
    }


def reference(x, c, ctx, c_ctx, w_mod, b_mod, g_pre_mix, g_post_mix, g_pre_ffn, g_post_ffn,
              w_qkv, q_norm_g, k_norm_g, w_o, ssm_lambda_re, ssm_lambda_im, ssm_log_step,
              ssm_b_re, ssm_b_im, ssm_c_re, ssm_c_im, ssm_d, w_glu_a, w_glu_b,
              w_up, conv_w, conv_b, w_down):
    x_lat, x_ctx = x, ctx
    silu_c = jax.nn.silu(c)
    silu_cc = jax.nn.silu(c_ctx)
    for i in range(DEPTH):
        last = i == DEPTH - 1
        m_l = [m[:, None, :] for m in jnp.split(silu_c @ w_mod[i] + b_mod[i], N_MOD, axis=-1)]
        m_c = jnp.split(silu_cc @ w_mod[i] + b_mod[i], N_MOD, axis=-1)
        h_l = modulate(rms_norm(x_lat, g_pre_mix[i]), m_l[0], m_l[1])
        h_c = modulate(rms_norm(x_ctx, g_pre_mix[i]), m_c[0], m_c[1])
        j = i // N_MIXERS
        if i % N_MIXERS == 0:
            o_l, o_c = attention_mixer(h_l, h_c, w_qkv[j], q_norm_g[j], k_norm_g[j], w_o[j], not last)
        else:
            o_l, o_c = s5_mixer(h_l, h_c, ssm_lambda_re[j], ssm_lambda_im[j], ssm_log_step[j],
                                ssm_b_re[j], ssm_b_im[j], ssm_c_re[j], ssm_c_im[j], ssm_d[j],
                                w_glu_a[j], w_glu_b[j], not last)
        x_lat = x_lat + m_l[2] * rms_norm(o_l, g_post_mix[i])
        h_l = modulate(rms_norm(x_lat, g_pre_ffn[i]), m_l[3], m_l[4])
        x_lat = x_lat + m_l[5] * rms_norm(conv_ffn(h_l, w_up[i], conv_w[i], conv_b[i], w_down[i]), g_post_ffn[i])
        if not last:
            x_ctx = x_ctx + m_c[2] * rms_norm(o_c, g_post_mix[i])
            h_c = modulate(rms_norm(x_ctx, g_pre_ffn[i]), m_c[3], m_c[4])
            x_ctx = x_ctx + m_c[5] * rms_norm(conv_ffn(h_c, w_up[i], conv_w[i], conv_b[i], w_down[i]), g_post_ffn[i])
    return x_lat
```

```python
import contextlib
import math
import numpy as np
import concourse.bass as bass
import concourse.mybir as mybir
from concourse.bass_utils import run_bass_kernel_spmd

F32 = mybir.dt.float32
F32R = mybir.dt.float32r
AF = mybir.ActivationFunctionType
ALU = mybir.AluOpType

D = 1024
NT = 2304
LC = 256
LL = 2048
DFF = 2816
NFC = 22
EPS = 1e-6


class Buf:
    __slots__ = ("name", "w", "r")

    def __init__(self, name=""):
        self.name = name
        self.w = None
        self.r = {}


class DmaSem:
    def __init__(self, sched, name):
        self.key = ("dma", name)
        self.count = 0
        sched.dma_sems.append(self)


class Sched:
    COMPUTE = ("pe", "act", "dve", "pool")
    QUEUES = ("pe", "act", "dve", "pool", "sp")
    EPOCH = 12000

    def __init__(self):
        self.q = {e: [] for e in self.QUEUES}
        self.known = {e: {} for e in self.QUEUES}
        self.dma_sems = []
        self.pending = {e: {} for e in self.QUEUES}
        self.ec = {e: 0 for e in self.QUEUES}
        self.eng_keys = set()

    def _ekey(self, eng, count):
        ep = (count - 1) // self.EPOCH
        return ("eng", eng, ep), (count - 1) % self.EPOCH + 1

    def barrier(self):
        snap = {}
        for e in self.COMPUTE:
            if self.ec[e] > 0:
                key, val = self._ekey(e, self.ec[e])
                snap[key] = val
        for d in self.dma_sems:
            snap[d.key] = d.count
        for e in self.QUEUES:
            p = self.pending[e]
            for k, v in snap.items():
                if v > p.get(k, 0):
                    p[k] = v

    def op(self, eng, fn, reads=(), writes=(), dsem=None):
        q = self.q[eng]
        idx = len(q) + 1
        deps = dict(self.pending[eng])
        self.pending[eng] = {}
        for key in [kk for kk in deps if kk[0] == "eng" and kk[1] == eng]:
            del deps[key]

        def add(tok):
            if tok is None:
                return
            key, val, teng, tidx = tok
            if teng == eng and key[0] == "eng":
                if eng == "pe" or idx - tidx > 2:
                    return
            if deps.get(key, 0) < val:
                deps[key] = val

        for b in reads:
            add(b.w)
        for b in writes:
            add(b.w)
            for t in b.r.values():
                add(t)
        kn = self.known[eng]
        waits = []
        for key, val in deps.items():
            if val <= 0:
                continue
            if kn.get(key, 0) >= val:
                continue
            kn[key] = val
            waits.append((key, val))
        if dsem is not None:
            dsem.count += 16
            tok = (dsem.key, dsem.count, eng, idx)
            inc = (dsem.key, 16)
        else:
            self.ec[eng] += 1
            key, val = self._ekey(eng, self.ec[eng])
            self.eng_keys.add(key)
            tok = (key, val, eng, idx)
            inc = (key, 1)
        q.append((waits, fn, inc))
        for b in reads:
            b.r[(tok[0][0], tok[0][1])] = tok
        for b in writes:
            b.w = tok
            b.r = {}
        return tok

    def emit(self, nc, final_wait_tokens=()):
        with contextlib.ExitStack() as st:
            sems = {}
            for key in sorted(self.eng_keys):
                sems[key] = st.enter_context(nc.semaphore("s_%s_%d" % (key[1], key[2])))
            for d in self.dma_sems:
                sems[d.key] = st.enter_context(nc.semaphore("d_" + str(d.key[1])))
            block = st.enter_context(nc.Block())
            q = self.q

            def replay(eng_obj, name, extra_waits=()):
                for waits, fn, inc in q[name]:
                    for key, val in waits:
                        eng_obj.wait_ge(sems[key], val)
                    ins = fn(eng_obj)
                    ins.then_inc(sems[inc[0]], inc[1])
                for key, val in extra_waits:
                    eng_obj.wait_ge(sems[key], val)

            fw = [(t[0], t[1]) for t in final_wait_tokens]

            @block.tensor
            def _(e):
                replay(e, "pe")

            @block.scalar
            def _(e):
                replay(e, "act")

            @block.vector
            def _(e):
                replay(e, "dve")

            @block.gpsimd
            def _(e):
                replay(e, "pool")

            @block.sync
            def _(e):
                replay(e, "sp", fw)


def rope_tables():
    grid_w, head_dim, theta = 64, 128, 10000.0
    rows = LL // grid_w
    row = np.repeat(np.arange(rows, dtype=np.float32), grid_w)
    col = np.tile(np.arange(grid_w, dtype=np.float32), rows)
    ppa = head_dim // 4
    freq = (np.float32(theta) ** (-np.arange(ppa, dtype=np.float32) / np.float32(ppa))).astype(np.float32)
    ang = np.concatenate([row[:, None] * freq, col[:, None] * freq], axis=-1).astype(np.float32)
    cos = np.cos(ang).astype(np.float32).T
    sin = np.sin(ang).astype(np.float32).T
    return np.ascontiguousarray(np.concatenate([cos, cos], 0)), np.ascontiguousarray(np.concatenate([sin, sin], 0))


def pk(v):
    v = np.asarray(v, np.float32)
    lead = v.shape[:-1]
    m = v.shape[-1] // 128
    r = v.reshape(lead + (m, 128))
    r = np.moveaxis(r, -1, 0)
    return np.ascontiguousarray(r)


def shared_inputs(inp):
    s = {}
    s["w_mod"] = np.ascontiguousarray(inp["w_mod"], np.float32)
    s["bmT"] = pk(inp["b_mod"])
    g = np.stack([inp["g_pre_mix"], inp["g_post_mix"], inp["g_pre_ffn"], inp["g_post_ffn"]], 1)
    s["gT"] = pk(g)
    wqkv = np.asarray(inp["w_qkv"][0], np.float32)
    s["wqk"] = np.ascontiguousarray(wqkv[:, :1280].reshape(8, 128, 10, 128).transpose(2, 1, 0, 3))
    s["wv"] = np.ascontiguousarray(wqkv[:, 1280:].reshape(8, 128, 256).transpose(1, 0, 2))
    wo = np.asarray(inp["w_o"][0], np.float32)
    s["wo"] = np.ascontiguousarray(wo.reshape(8, 128, 8, 128).transpose(2, 1, 0, 3))
    s["qkg"] = np.ascontiguousarray(np.stack([inp["q_norm_g"][0], inp["k_norm_g"][0]], 1), np.float32)
    cos, sin = rope_tables()
    s["cos"], s["sin"] = cos, sin
    rot = np.zeros((128, 128), np.float32)
    for m in range(64):
        rot[m + 64, m] = -1.0
        rot[m, m + 64] = 1.0
    s["rotm"] = rot
    s["ones"] = np.ones((128, 128), np.float32)
    wup = np.asarray(inp["w_up"], np.float32)
    wu = wup.reshape(2, 8, 128, 2, NFC, 128)
    s["wup"] = np.ascontiguousarray(wu.transpose(0, 4, 2, 1, 3, 5)).reshape(2, NFC, 128, 8, 256)
    wdn = np.asarray(inp["w_down"], np.float32)
    s["wdn"] = np.ascontiguousarray(wdn.reshape(2, NFC, 128, 8, 128).transpose(0, 3, 2, 1, 4))
    s["cw"] = pk(inp["conv_w"])
    s["cb"] = pk(inp["conv_b"])
    lre = np.asarray(inp["ssm_lambda_re"][0], np.float32)
    lim = np.asarray(inp["ssm_lambda_im"][0], np.float32)
    lst = np.asarray(inp["ssm_log_step"][0], np.float32)

    def en_pd(a):
        return a.reshape(2, 32, 2, 64).transpose(2, 3, 1, 0).reshape(128, 64)

    lam = np.stack([en_pd(lre), en_pd(lim), en_pd(np.broadcast_to(lst[:, :, None], (2, 64, 64)))], 1)
    s["lam"] = np.ascontiguousarray(lam, np.float32)
    Bf = np.zeros((4, 2, 16, 2, 32, 2, 2, 64), np.float32)
    Cf = np.zeros((2, 64, 2, 32, 2, 4, 2, 16), np.float32)
    for ri, (bsrc, csrc) in enumerate(((inp["ssm_b_re"], inp["ssm_c_re"]), (inp["ssm_b_im"], inp["ssm_c_im"]))):
        bsrc = np.asarray(bsrc[0], np.float32)
        csrc = np.asarray(csrc[0], np.float32)
        for p_ in range(32):
            r_ = p_ % 4
            for e_ in range(2):
                g_ = 2 * p_ + e_
                Bf[r_, e_, :, :, p_, ri, e_, :] = bsrc[:, g_, :, :].transpose(2, 0, 1)
                Cf[e_, :, :, p_, ri, r_, e_, :] = csrc[:, g_, :, :].transpose(2, 0, 1)
    s["Bf"] = np.ascontiguousarray(Bf.reshape(128, 2, 32, 2, 128))
    s["Cf"] = np.ascontiguousarray(Cf.reshape(128, 2, 32, 2, 128))
    s["dsk"] = pk(inp["ssm_d"][0])
    wg = np.stack([inp["w_glu_a"][0], inp["w_glu_b"][0]]).astype(np.float32)
    s["wglu"] = np.ascontiguousarray(wg.reshape(2, 8, 128, 8, 128).transpose(0, 3, 2, 1, 4))
    return s


def core_inputs(inp, core):
    b0 = 2 * core
    m = {}
    xs = []
    for b in (b0, b0 + 1):
        xs.append(np.concatenate([inp["ctx"][b], inp["x"][b]], 0).T)
    m["xT"] = np.ascontiguousarray(np.stack(xs), np.float32)
    c3 = np.stack([inp["c"][b0], inp["c"][b0 + 1], inp["c_ctx"]], -1)
    m["cT"] = np.ascontiguousarray(c3.reshape(8, 128, 3).transpose(1, 0, 2), np.float32)
    return m


NAR = 24576
NAF = 7744
TCH = 64
TWO_PI = 2.0 * math.pi
import os as _os
IMENG = _os.environ.get('IMENG', 'pool')


class K:
    pass


class XB:
    def __init__(self):
        self.b = [Buf(f"X{i}") for i in range(NT // 256)]

    def __call__(self, c0, c1):
        return self.b[c0 // 256:(c1 + 255) // 256]


class Arena:
    def __init__(self, t, n, dt):
        self.t, self.n, self.dt, self.o = t, n, dt, 0

    def reset(self):
        self.o = 0

    def take(self, shape):
        n = int(np.prod(shape))
        v = self.t[:, self.o:self.o + n]
        self.o += n
        assert self.o <= self.n, (self.o, self.n)
        if len(shape) == 2:
            v = v.rearrange("p (a b) -> p a b", b=shape[1])
        elif len(shape) == 3:
            v = v.rearrange("p (a b c) -> p a b c", b=shape[1], c=shape[2])
        elif len(shape) == 4:
            v = v.rearrange("p (a b c d) -> p a b c d", b=shape[1], c=shape[2], d=shape[3])
        return v


def build(stage=99, nb=2):
    nc = bass.Bass("TRN2", target_bir_lowering=False)
    S = Sched()
    k = K()
    k.nc, k.S = nc, S

    def din(name, shape, dt=F32):
        return nc.dram_tensor(name, list(shape), dt, kind="ExternalInput").ap()

    d = K()
    d.xT = din("xT", [2, D, NT])
    d.cT = din("cT", [128, 8, 3])
    d.w_mod = din("w_mod", [2, D, 6 * D])
    d.bmT = din("bmT", [128, 2, 48])
    d.gT = din("gT", [128, 2, 4, 8])
    d.wqk = din("wqk", [10, 128, 8, 128])
    d.wv = din("wv", [128, 8, 256])
    d.wo = din("wo", [8, 128, 8, 128])
    d.qkg = din("qkg", [128, 2])
    d.cos = din("cos", [128, LL])
    d.sin = din("sin", [128, LL])
    d.rotm = din("rotm", [128, 128])
    d.ones = din("ones", [128, 128])
    d.wup = din("wup", [2, NFC, 128, 8, 256])
    d.wdn = din("wdn", [2, 8, 128, NFC, 128])
    d.cw = din("cw", [128, 2, 3, 44])
    d.cb = din("cb", [128, 2, 44])
    d.lam = din("lam", [128, 3, 64])
    d.Bf = din("Bf", [128, 2, 32, 2, 128])
    d.Cf = din("Cf", [128, 2, 32, 2, 128])
    d.dsk = din("dsk", [128, 8])
    d.wglu = din("wglu", [2, 8, 128, 8, 128])
    d.out = nc.dram_tensor("outT", [2, D, LL], F32, kind="ExternalOutput").ap()
    if stage != 99:
        d.dbg = nc.dram_tensor("dbg", [128, 8, NT], F32, kind="ExternalOutput").ap()
    d.xsp = nc.dram_tensor("xspill", [128, 8, LL], F32, kind="Internal").ap()
    k.d = d

    with contextlib.ExitStack() as st:
        def sb(name, shape, dt=F32):
            return st.enter_context(nc.sbuf_tensor("sb_" + name, list(shape), dt))

        k.Xt = sb("X", [128, 8 * NT])
        k.ARt = sb("AR", [128, NAR], F32R)
        k.AFt = sb("AF", [128, NAF])
        k.AR = Arena(k.ARt, NAR, F32R)
        k.AF = Arena(k.AFt, NAF, F32)
        k.ones = sb("ones", [128, 128], F32R)
        k.rotm = sb("rotm", [128, 128], F32R)
        k.modv = sb("modv", [128, 2, 48, 3])
        k.gT = sb("gT", [128, 2, 4, 8])
        k.sA = sb("sA", [128, 2, 2, 3, 8])
        k.sG = sb("sG", [128, 2, 2, 3, 8])
        k.qkg = sb("qkg", [128, 2])
        k.cw = sb("cw", [128, 2, 3, 44])
        k.cb = sb("cb", [128, 2, 44])
        k.epsD = sb("epsD", [128, 1])
        k.eps128 = sb("eps128", [128, 1])
        k.bmT = sb("bmT", [128, 2, 48])
        k.dsk = sb("dsk", [128, 8])
        k.stab = sb("stab", [128, 6, 64])
        k.ps = [st.enter_context(nc.psum_tensor(f"ps{i}", [128, 512], F32)) for i in range(8)]
        k.bps = [Buf(f"ps{i}") for i in range(8)]
        k.bconst = Buf("const")
        k.dsem_c = DmaSem(S, "const")
        k.dsem_o = DmaSem(S, "out")
        k.dsem_x = DmaSem(S, "x")
        k.dsem_w = [DmaSem(S, f"w{i}") for i in range(6)]
        k.dsem_m = [DmaSem(S, f"m{i}") for i in range(2)]
        k.dsem_cs = DmaSem(S, "const_sw")

        prologue(k)
        fin = []
        for b in range(nb):
            fin += batch_element(k, b, stage)
        S.emit(nc, fin)
    k.ninstr = {e: len(q) for e, q in S.q.items()}
    print("instr counts", k.ninstr)
    return nc


def prologue(k):
    nc, S, d = k.nc, k.S, k.d
    bc = k.bconst
    dc = k.dsem_c
    S.op("pool", lambda e: e.dma_start(out=k.ones[:], in_=d.ones), writes=[bc], dsem=k.dsem_cs)
    S.op("pool", lambda e: e.dma_start(out=k.rotm[:], in_=d.rotm), writes=[bc], dsem=k.dsem_cs)
    for t, src in ((k.gT, d.gT), (k.qkg, d.qkg), (k.cw, d.cw), (k.cb, d.cb), (k.bmT, d.bmT), (k.dsk, d.dsk)):
        S.op("sp", (lambda e, t=t, src=src: e.dma_start(out=t[:], in_=src)), writes=[bc], dsem=dc)
    S.op("dve", lambda e: e.memset(k.epsD[:], EPS), writes=[bc])
    S.op("dve", lambda e: e.memset(k.eps128[:], EPS), writes=[bc])
    AFa = k.AF
    AFa.reset()
    cT = AFa.take([8, 3])
    sc = AFa.take([8, 3])
    bcT, bsc = Buf("cT"), Buf("sc")
    S.op("sp", lambda e: e.dma_start(out=cT, in_=d.cT), writes=[bcT], dsem=dc)
    S.op("act", lambda e: e.activation(out=sc, in_=cT, func=AF.Silu), reads=[bcT], writes=[bsc])
    import os
    PS = int(os.environ.get("PRO_STOP", "9"))
    if PS <= 1:
        S.barrier()
        return
    NP = 384
    wt = [(AFa.take([8, NP]), Buf(f"wm{i}")) for i in range(2)]
    bmod = Buf("modv")
    it = 0
    for l in range(2):
        for j in range(6 * D // NP):
            wv_, wb = wt[it % 2]
            ds = k.dsem_m[it % 2]
            src = d.w_mod[l, :, j * NP:(j + 1) * NP].rearrange("(kk p) n -> p kk n", p=128)
            S.op("sp", (lambda e, wv_=wv_, src=src: e.dma_start(out=wv_, in_=src)), writes=[wb], dsem=ds)
            for c in range(NP // 128):
                oc = j * (NP // 128) + c
                pst = k.ps[l][:, oc * 3:oc * 3 + 3]
                for kk in range(8):
                    S.op("pe", (lambda e, pst=pst, wv_=wv_, c=c, kk=kk: e.matmul(
                        pst, wv_[:, kk, c * 128:(c + 1) * 128], sc[:, kk, :], start=(kk == 0), stop=(kk == 7))),
                        reads=[wb, bsc], writes=[k.bps[l]])
            it += 1
        S.op("dve", (lambda e, l=l: e.tensor_tensor(
            out=k.modv[:, l, :, :], in0=k.ps[l][:, 0:144].rearrange("p (a b) -> p a b", b=3),
            in1=k.bmT[:, l, :].unsqueeze(2).broadcast_to([128, 48, 3]), op=ALU.add)),
            reads=[k.bps[l], bc], writes=[bmod])
    if PS <= 2:
        S.barrier()
        return
    for l in range(2):
        for j, (gi, si) in enumerate(((0, 1), (2, 4))):
            for r in range(3):
                S.op("dve", (lambda e, l=l, j=j, gi=gi, si=si, r=r: e.scalar_tensor_tensor(
                    out=k.sA[:, l, j, r, :], in0=k.modv[:, l, si * 8:(si + 1) * 8, r], scalar=1.0,
                    in1=k.gT[:, l, gi, :], op0=ALU.add, op1=ALU.mult)), reads=[bmod, bc], writes=[bc])
        for j, (gi, si) in enumerate(((1, 2), (3, 5))):
            for r in range(3):
                S.op("dve", (lambda e, l=l, j=j, gi=gi, si=si, r=r: e.tensor_tensor(
                    out=k.sG[:, l, j, r, :], in0=k.modv[:, l, si * 8:(si + 1) * 8, r],
                    in1=k.gT[:, l, gi, :], op=ALU.mult)), reads=[bmod, bc], writes=[bc])
    S.barrier()
    if PS <= 3:
        return
    s5_tables(k)
    S.barrier()


def s5_tables(k):
    S, d = k.S, k.d
    AFa = k.AF
    AFa.reset()
    lam = AFa.take([3, 64])
    w = [AFa.take([64]) for _ in range(10)]
    bt = Buf("s5t")
    S.op("sp", lambda e: e.dma_start(out=lam, in_=d.lam), writes=[bt], dsem=k.dsem_c)
    lre, lim, lst = lam[:, 0, :], lam[:, 1, :], lam[:, 2, :]
    dt_, xr, xi, mag, tq, tr, cs, sn, den, t2 = w
    st = k.stab

    def dve(fn):
        S.op("dve", fn, reads=[bt], writes=[bt])

    def act(fn):
        S.op("act", fn, reads=[bt], writes=[bt])

    act(lambda e: e.activation(out=dt_, in_=lst, func=AF.Exp))
    dve(lambda e: e.tensor_tensor(out=xr, in0=lre, in1=dt_, op=ALU.mult))
    dve(lambda e: e.tensor_tensor(out=xi, in0=lim, in1=dt_, op=ALU.mult))
    MAGIC = 12582912.0

    def sincos(mult, out_c, out_s):
        for off, dst in ((0.0, out_s), (math.pi / 2, out_c)):
            dve(lambda e, off=off: e.tensor_scalar(out=tq, in0=xi, scalar1=float(mult), scalar2=float(off),
                                                    op0=ALU.mult, op1=ALU.add))
            dve(lambda e: e.tensor_scalar(out=tr, in0=tq, scalar1=1.0 / TWO_PI, scalar2=MAGIC,
                                          op0=ALU.mult, op1=ALU.add))
            dve(lambda e: e.tensor_scalar(out=tr, in0=tr, scalar1=MAGIC, scalar2=None, op0=ALU.subtract))
            dve(lambda e: e.scalar_tensor_tensor(out=tq, in0=tr, scalar=-TWO_PI, in1=tq, op0=ALU.mult, op1=ALU.add))
            dve(lambda e: e.tensor_scalar(out=tq, in0=tq, scalar1=3.1415925, scalar2=-3.1415925,
                                          op0=ALU.min, op1=ALU.max))
            act(lambda e, dst=dst: e.activation(out=dst, in_=tq, func=AF.Sin))

    for mult, ia, ib in ((1.0, 0, 1), (float(TCH), 2, 3)):
        sincos(mult, cs, sn)
        act(lambda e, mult=mult: e.activation(out=mag, in_=xr, func=AF.Exp, scale=float(mult)))
        dve(lambda e, ia=ia: e.tensor_tensor(out=st[:, ia, :], in0=mag, in1=cs, op=ALU.mult))
        dve(lambda e, ib=ib: e.tensor_tensor(out=st[:, ib, :], in0=mag, in1=sn, op=ALU.mult))
    dve(lambda e: e.tensor_tensor(out=den, in0=lre, in1=lre, op=ALU.mult))
    dve(lambda e: e.tensor_tensor(out=t2, in0=lim, in1=lim, op=ALU.mult))
    dve(lambda e: e.tensor_tensor(out=den, in0=den, in1=t2, op=ALU.add))
    dve(lambda e: e.reciprocal(out=den, in_=den))
    dve(lambda e: e.tensor_scalar(out=tq, in0=st[:, 0, :], scalar1=-1.0, scalar2=None, op0=ALU.add))
    dve(lambda e: e.tensor_tensor(out=tr, in0=tq, in1=lre, op=ALU.mult))
    dve(lambda e: e.tensor_tensor(out=t2, in0=st[:, 1, :], in1=lim, op=ALU.mult))
    dve(lambda e: e.tensor_tensor(out=tr, in0=tr, in1=t2, op=ALU.add))
    dve(lambda e: e.tensor_tensor(out=st[:, 4, :], in0=tr, in1=den, op=ALU.mult))
    dve(lambda e: e.tensor_tensor(out=tr, in0=st[:, 1, :], in1=lre, op=ALU.mult))
    dve(lambda e: e.tensor_tensor(out=t2, in0=tq, in1=lim, op=ALU.mult))
    dve(lambda e: e.tensor_tensor(out=tr, in0=tr, in1=t2, op=ALU.subtract))
    dve(lambda e: e.tensor_tensor(out=st[:, 5, :], in0=tr, in1=den, op=ALU.mult))
    S.op("dve", lambda e: e.tensor_copy(out=cs, in_=st[:, 5, :]), reads=[bt], writes=[bt, k.bconst])


def segs_rows(c0, c1, b):
    out = []
    if c0 < LC:
        out.append((c0, min(c1, LC), 2))
    if c1 > LC:
        out.append((max(c0, LC), c1, b))
    return out


def norm_stats(k, src_fn, reads, W, scr):
    S = k.S
    pi = scr["ps"]
    pst = k.ps[pi][:, 0:W]
    for kk in range(8):
        sq, bsq = scr["sq"][kk % 2]
        S.op("act", (lambda e, sq=sq, kk=kk: e.activation(out=sq[:, 0:W], in_=src_fn(kk), func=AF.Square)),
             reads=reads, writes=[bsq])
        S.op("pe", (lambda e, sq=sq, kk=kk: e.matmul(pst, k.ones[:], sq[:, 0:W], start=(kk == 0), stop=(kk == 7))),
             reads=[bsq, k.bconst], writes=[k.bps[pi]])
    rstd, brs = scr["rstd"]
    S.op("act", lambda e: e.activation(out=rstd[:, 0:W], in_=pst, func=AF.Sqrt, scale=1.0 / D, bias=k.epsD[:]),
         reads=[k.bps[pi], k.bconst], writes=[brs])
    S.op("dve", lambda e: e.reciprocal(out=rstd[:, 0:W], in_=rstd[:, 0:W]), reads=[brs], writes=[brs])
    return rstd, brs


def modnorm(k, X, bX, c0, c1, l, j, b, hout, bh, scr, ho=0):
    S = k.S
    W = c1 - c0
    Ws = W + (W % 2)
    assert c0 + Ws <= NT
    rstd, brs = norm_stats(k, lambda kk: X[:, kk, c0:c0 + Ws], bX(c0, c0 + Ws), Ws, scr)
    shift_i = 0 if j == 0 else 3
    for kk in range(8):
        tmp, btmp = scr["tmp"][kk % 2]
        S.op("dve", (lambda e, tmp=tmp, kk=kk: e.tensor_tensor(out=tmp[:, 0:W], in0=X[:, kk, c0:c1], in1=rstd[:, 0:W],
                                                               op=ALU.mult)), reads=bX(c0, c1) + [brs], writes=[btmp])
        for (lo, hi, r) in segs_rows(c0, c1, b):
            S.op("act", (lambda e, tmp=tmp, kk=kk, lo=lo, hi=hi, r=r: e.activation(
                out=hout[:, kk, ho + lo - c0:ho + hi - c0], in_=tmp[:, lo - c0:hi - c0], func=AF.Identity,
                scale=k.sA[:, l, j, r, kk:kk + 1], bias=k.modv[:, l, shift_i * 8 + kk, r:r + 1])),
                reads=[btmp, k.bconst], writes=(bh if isinstance(bh, list) else [bh]))


def postnorm_update(k, X, bX, c0, c1, l, j, b, ost, bost, scr):
    S = k.S
    W = c1 - c0
    rstd, brs = norm_stats(k, lambda kk: ost[:, kk, 0:W], [bost], W, scr)
    for kk in range(8):
        tmp, btmp = scr["tmp"][kk % 2]
        S.op("dve", (lambda e, tmp=tmp, kk=kk: e.tensor_tensor(out=tmp[:, 0:W], in0=ost[:, kk, 0:W], in1=rstd[:, 0:W],
                                                               op=ALU.mult)), reads=[bost, brs], writes=[btmp])
        for (lo, hi, r) in segs_rows(c0, c1, b):
            S.op("dve", (lambda e, tmp=tmp, kk=kk, lo=lo, hi=hi, r=r: e.scalar_tensor_tensor(
                out=X[:, kk, lo:hi], in0=tmp[:, lo - c0:hi - c0], scalar=k.sG[:, l, j, r, kk:kk + 1],
                in1=X[:, kk, lo:hi], op0=ALU.mult, op1=ALU.add)), reads=[btmp, k.bconst] + bX(lo, hi), writes=bX(lo, hi))


def dump_x(k, X, bX, fin):
    S, d = k.S, k.d
    for kk in range(8):
        fin.append(S.op("sp", (lambda e, kk=kk: e.dma_start(out=d.dbg[:, kk, :], in_=X[:, kk, :])),
                        reads=bX(0, NT), dsem=k.dsem_o))


def batch_element(k, b, stage):
    nc, S, d = k.nc, k.S, k.d
    X = k.Xt[:, :].rearrange("p (a t) -> p a t", t=NT)
    bX = XB()
    for kk in range(8):
        S.op("sp", (lambda e, kk=kk: e.dma_start(out=X[:, kk, :], in_=d.xT[b, kk * 128:(kk + 1) * 128, :])),
             writes=bX(0, NT), dsem=k.dsem_x)
    fin = []
    if stage == 0:
        dump_x(k, X, bX, fin)
        S.barrier()
        return fin
    import os
    if os.environ.get("S5_ONLY") == "1":
        s5_layer(k, b, X, bX)
        S.barrier()
        dump_x(k, X, bX, fin)
        S.barrier()
        return fin
    attention_layer(k, b, X, bX)
    S.barrier()
    if stage == 1:
        if b == 0:
            dump_x(k, X, bX, fin)
        S.barrier()
        return fin
    ffn_layer(k, b, X, bX, 0, 0, NT)
    S.barrier()
    if stage == 2:
        if b == 0:
            dump_x(k, X, bX, fin)
        S.barrier()
        return fin
    s5_layer(k, b, X, bX)
    S.barrier()
    if stage == 3:
        if b == 0:
            dump_x(k, X, bX, fin)
        S.barrier()
        return fin
    ffn_layer(k, b, X, bX, 1, LC, NT)
    S.barrier()
    for kk in range(8):
        fin.append(S.op("sp", (lambda e, kk=kk: e.dma_start(out=d.out[b, kk * 128:(kk + 1) * 128, :], in_=X[:, kk, LC:NT])),
                        reads=bX(LC, NT), dsem=k.dsem_o))
    S.barrier()
    return fin


def make_scr(k, W, ps):
    AR, AFa = k.AR, k.AF
    scr = {"ps": ps, "sq": [], "tmp": []}
    for i in range(2):
        scr["sq"].append((AR.take([W]), Buf(f"sq{i}")))
    for i in range(2):
        scr["tmp"].append((AFa.take([W]), Buf(f"tmp{i}")))
    scr["rstd"] = (AFa.take([W]), Buf("rstd"))
    return scr


def attention_layer(k, b, X, bX):
    nc, S, d = k.nc, k.S, k.d
    l = 0
    WQ = 256
    AR, AFa = k.AR, k.AF
    AR.reset()
    AFa.reset()
    KT = AR.take([2, NT])
    V = AR.take([18, 256])
    AT = AR.take([8, WQ])
    HW = AR.take([8, WQ])
    QT = AR.take([8, WQ])
    bKT, bV, bAT, bHW, bQT = Buf("KT"), Buf("V"), Buf("AT"), Buf("HW"), Buf("QT")
    wq = [(AR.take([8, 128]), Buf(f"wq{i}"), k.dsem_w[i]) for i in range(2)]
    wo = [(AR.take([8, 128]), Buf(f"wo{i}"), k.dsem_w[2 + i]) for i in range(2)]
    WV = AR.take([8, 256]); bWV = Buf("WV")
    PT = [(AR.take([WQ]), Buf(f"PT{i}")) for i in range(4)]
    KN = AR.take([WQ]); bKN = Buf("KN")
    SQ2 = AR.take([WQ]); bSQ2 = Buf("SQ2")
    COS = AFa.take([LL])
    SIN = AFa.take([LL])
    OST = AFa.take([8, WQ]); bOST = Buf("OST")
    RS2 = AFa.take([WQ]); bRS2 = Buf("RS2")
    T2 = AFa.take([WQ]); bT2 = Buf("T2")
    RINV = AFa.take([WQ]); bRINV = Buf("RINV")
    scr = make_scr(k, WQ, 7)
    brope = Buf("rope")
    S.op("sp", lambda e: e.dma_start(out=COS, in_=d.cos), writes=[brope], dsem=k.dsem_c)
    S.op("sp", lambda e: e.dma_start(out=SIN, in_=d.sin), writes=[brope], dsem=k.dsem_c)
    S.op("pool", lambda e: e.dma_start(out=WV, in_=d.wv), writes=[bWV], dsem=k.dsem_w[4])
    bS = {(4, 0): Buf("S40"), (4, 1): Buf("S41"), (5, 0): Buf("S50"), (5, 1): Buf("S51")}
    wq_it = [0]

    def load_wq(ci):
        v, bw, ds = wq[wq_it[0] % 2]
        wq_it[0] += 1
        S.op("pool", (lambda e, v=v, ci=ci: e.dma_start(out=v, in_=d.wqk[ci])), writes=[bw], dsem=ds)
        return v, bw

    def qk_head(ci, gcol, c0, lat, dst, bdst):
        wv_, bw = load_wq(ci)
        pi = 6
        pst = k.ps[pi][:, 0:WQ]
        for kk in range(8):
            S.op("pe", (lambda e, kk=kk: e.matmul(pst, wv_[:, kk, :], HW[:, kk, :], start=(kk == 0), stop=(kk == 7))),
                 reads=[bw, bHW], writes=[k.bps[pi]])
        S.op("act", lambda e: e.activation(out=SQ2, in_=pst, func=AF.Square), reads=[k.bps[pi]], writes=[bSQ2])
        p2 = k.ps[pi][:, 256:256 + WQ]
        S.op("pe", lambda e: e.matmul(p2, k.ones[:], SQ2, start=True, stop=True), reads=[bSQ2, k.bconst],
             writes=[k.bps[pi]])
        S.op("act", lambda e: e.activation(out=RS2, in_=p2, func=AF.Sqrt, scale=1.0 / 128, bias=k.eps128[:]),
             reads=[k.bps[pi], k.bconst], writes=[bRS2])
        S.op("dve", lambda e: e.reciprocal(out=RS2, in_=RS2), reads=[bRS2], writes=[bRS2])
        if not lat:
            S.op("dve", lambda e: e.scalar_tensor_tensor(out=dst, in0=pst, scalar=k.qkg[:, gcol:gcol + 1], in1=RS2,
                                                         op0=ALU.mult, op1=ALU.mult),
                 reads=[k.bps[pi], bRS2, k.bconst], writes=[bdst])
            return
        S.op("dve", lambda e: e.scalar_tensor_tensor(out=KN, in0=pst, scalar=k.qkg[:, gcol:gcol + 1], in1=RS2,
                                                     op0=ALU.mult, op1=ALU.mult),
             reads=[k.bps[pi], bRS2, k.bconst], writes=[bKN])
        S.op("pe", lambda e: e.matmul(p2, k.rotm[:], KN, start=True, stop=True), reads=[bKN, k.bconst, bRS2],
             writes=[k.bps[pi]])
        t0 = c0 - LC
        S.op("dve", lambda e: e.tensor_tensor(out=T2, in0=p2, in1=SIN[:, t0:t0 + WQ], op=ALU.mult),
             reads=[k.bps[pi], brope], writes=[bT2])
        S.op("dve", lambda e: e.tensor_tensor(out=RS2, in0=KN.bitcast(F32), in1=COS[:, t0:t0 + WQ], op=ALU.mult),
             reads=[bKN, brope], writes=[bRS2])
        S.op("dve", lambda e: e.tensor_tensor(out=dst, in0=RS2, in1=T2, op=ALU.add), reads=[bRS2, bT2], writes=[bdst])

    for w in range(NT // WQ):
        c0 = w * WQ
        lat = c0 >= LC
        modnorm(k, X, bX, c0, c0 + WQ, l, 0, b, HW, bHW, scr)
        for jk in range(2):
            qk_head(8 + jk, 1, c0, lat, KT[:, jk, c0:c0 + WQ], bKT)
        for sub in range(2):
            pi = 5
            pst = k.ps[pi][:, sub * 256:(sub + 1) * 256]
            for kk in range(8):
                S.op("pe", (lambda e, kk=kk, sub=sub, pst=pst: e.matmul(
                    pst, HW[:, kk, sub * 128:(sub + 1) * 128], WV[:, kk, :], start=(kk == 0), stop=(kk == 7))),
                    reads=[bHW, bWV], writes=[k.bps[pi]])
            S.op("act", (lambda e, sub=sub, pst=pst, w=w: e.activation(out=V[:, 2 * w + sub, :], in_=pst, func=AF.Identity)),
                 reads=[k.bps[pi]], writes=[bV])
    scale = 1.0 / math.sqrt(128.0)
    wo_it = [0]
    pt_it = [0]
    for w in range(NT // WQ):
        c0 = w * WQ
        lat = c0 >= LC
        nkt = 18 if lat else 2
        modnorm(k, X, bX, c0, c0 + WQ, l, 0, b, HW, bHW, scr)
        for h in range(8):
            qk_head(h, 0, c0, lat, QT[:, h, :], bQT)
        for h in range(8):
            jk = h // 4
            po, pr = (0, 1) if h % 2 == 0 else (2, 3)
            for kt in range(nkt):
                sp_i = 4 if (kt % 2 == 0) else 5
                half = (kt // 2) % 2
                pss = k.ps[sp_i][:, half * 256:half * 256 + WQ]
                S.op("pe", (lambda e, pss=pss, jk=jk, kt=kt, h=h: e.matmul(
                    pss, KT[:, jk, kt * 128:(kt + 1) * 128], QT[:, h, :], start=True, stop=True)),
                    reads=[bKT, bQT], writes=[bS[(sp_i, half)]])
                pt, bpt = PT[pt_it[0] % 4]
                pt_it[0] += 1
                S.op("act", (lambda e, pt=pt, pss=pss: e.activation(out=pt, in_=pss, func=AF.Exp, scale=scale)),
                     reads=[bS[(sp_i, half)]], writes=[bpt])
                S.op("pe", (lambda e, pt=pt, jk=jk, kt=kt, po=po, nkt=nkt: e.matmul(
                    k.ps[po][:, 0:WQ], V[:, kt, jk * 128:(jk + 1) * 128], pt, start=(kt == 0), stop=(kt == nkt - 1))),
                    reads=[bV, bpt], writes=[k.bps[po]])
                S.op("pe", (lambda e, pt=pt, kt=kt, pr=pr, nkt=nkt: e.matmul(
                    k.ps[pr][:, 0:WQ], k.ones[:], pt, start=(kt == 0), stop=(kt == nkt - 1))),
                    reads=[bpt, k.bconst], writes=[k.bps[pr]])
            S.op("dve", (lambda e, pr=pr: e.reciprocal(out=RINV, in_=k.ps[pr][:, 0:WQ])), reads=[k.bps[pr]], writes=[bRINV])
            S.op("dve", (lambda e, po=po, h=h: e.tensor_tensor(out=AT[:, h, :], in0=k.ps[po][:, 0:WQ], in1=RINV, op=ALU.mult)),
                 reads=[k.bps[po], bRINV], writes=[bAT])
        for dcn in range(8):
            v, bw, ds = wo[wo_it[0] % 2]
            wo_it[0] += 1
            S.op("pool", (lambda e, v=v, dcn=dcn: e.dma_start(out=v, in_=d.wo[dcn])), writes=[bw], dsem=ds)
            pi = 6
            pst = k.ps[pi][:, 0:WQ]
            for h in range(8):
                S.op("pe", (lambda e, v=v, h=h, pst=pst: e.matmul(pst, v[:, h, :], AT[:, h, :], start=(h == 0), stop=(h == 7))),
                     reads=[bw, bAT], writes=[k.bps[pi]])
            S.op("act", (lambda e, dcn=dcn, pst=pst: e.activation(out=OST[:, dcn, :], in_=pst, func=AF.Identity)),
                 reads=[k.bps[pi]], writes=[bOST])
        postnorm_update(k, X, bX, c0, c0 + WQ, l, 0, b, OST, bOST, scr)


def excl_segs(lo, hi, bad):
    if hi <= lo:
        return []
    if bad is None or bad < lo or bad >= hi:
        return [(lo, hi)]
    out = []
    if bad > lo:
        out.append((lo, bad))
    if bad + 1 < hi:
        out.append((bad + 1, hi))
    return out


def ffn_windows(cs, ce):
    n = ce - cs
    if n == NT:
        sizes = [460, 462, 460, 462, 460]
    else:
        sizes = [410, 410, 408, 410, 410]
    assert sum(sizes) == n
    out = []
    s = cs
    for w in sizes:
        out.append((s, s + w))
        s += w
    return out


def ffn_layer(k, b, X, bX, l, cs, ce):
    nc, S, d = k.nc, k.S, k.d
    AR, AFa = k.AR, k.AF
    AR.reset()
    AFa.reset()
    WM = 464
    HW = AR.take([8, WM]); bHW = Buf("HWf")
    G = AR.take([NFC, 462]); bG = [Buf(f"G{i}") for i in range(NFC)]
    wu = [(AR.take([8, 256]), Buf(f"wu{i}"), k.dsem_w[i]) for i in range(2)]
    wd = [(AR.take([NFC, 128]), Buf(f"wd{i}"), k.dsem_w[2 + i]) for i in range(2)]
    HALO = AR.take([8, 2]); bHALO = Buf("halo")
    OST = AFa.take([8, 462]); bOST = Buf("OSTf")
    yv = [(AFa.take([WM]), Buf(f"yv{i}")) for i in range(2)]
    yg = [(AFa.take([WM]), Buf(f"yg{i}")) for i in range(2)]
    scr = make_scr(k, WM, 7)
    wins = ffn_windows(cs, ce)
    wu_it = [0]
    wd_it = [0]
    for wi, (s, e_) in enumerate(wins):
        W = e_ - s
        ms, me = max(s - 1, cs), min(e_ + 1, ce)
        if (me - ms) % 2 == 1:
            if me < ce:
                me += 1
            else:
                ms -= 1
        Wm = me - ms
        nleft = s - ms
        modnorm(k, X, bX, s, me, l, 1, b, HW, bHW, scr, ho=nleft)
        if nleft > 0:
            S.op("dve", (lambda e, nleft=nleft, s=s, e_=e_, ms=ms, W=W, Wm=Wm: e.tensor_copy(out=HW[:, :, 0:nleft], in_=HALO[:, :, 2 - nleft:2])),
                 reads=[bHALO], writes=[bHW])
        S.op("dve", (lambda e, s=s, e_=e_, ms=ms: e.tensor_copy(out=HALO[:, :, :], in_=HW[:, :, e_ - 2 - ms:e_ - ms])),
             reads=[bHW], writes=[bHALO])
        badl = LC if l == 0 else None
        badr = LC - 1 if l == 0 else None
        for fp in range(NFC):
            wv_, bw, ds = wu[wu_it[0] % 2]
            wu_it[0] += 1
            S.op("pool", (lambda e, wv_=wv_, fp=fp, s=s, e_=e_, ms=ms, W=W, Wm=Wm: e.dma_start(out=wv_, in_=d.wup[l, fp])), writes=[bw], dsem=ds)
            par = fp % 2
            pv, pg = (0, 1) if par == 0 else (2, 3)
            for (pi, co) in ((pv, 0), (pg, 128)):
                for kk in range(8):
                    S.op("pe", (lambda e, pi=pi, co=co, kk=kk, wv_=wv_, s=s, e_=e_, ms=ms, W=W, Wm=Wm: e.matmul(
                        k.ps[pi][:, 0:Wm], wv_[:, kk, co:co + 128], HW[:, kk, 0:Wm], start=(kk == 0), stop=(kk == 7))),
                        reads=[bw, bHW], writes=[k.bps[pi]])
            for (pi, ci, (yt, byt)) in ((pv, fp, yv[par]), (pg, NFC + fp, yg[par])):
                psr = k.ps[pi]
                S.op("act", (lambda e, psr=psr, ci=ci, yt=yt, s=s, e_=e_, ms=ms, W=W, Wm=Wm: e.activation(
                    out=yt[:, 0:W], in_=psr[:, s - ms:e_ - ms], func=AF.Identity,
                    scale=k.cw[:, l, 1, ci:ci + 1], bias=k.cb[:, l, ci:ci + 1])),
                    reads=[k.bps[pi], k.bconst], writes=[byt])
                for (lo, hi) in excl_segs(max(s, cs + 1), e_, badl):
                    S.op("dve", (lambda e, psr=psr, ci=ci, yt=yt, lo=lo, hi=hi, s=s, e_=e_, ms=ms, W=W, Wm=Wm: e.scalar_tensor_tensor(
                        out=yt[:, lo - s:hi - s], in0=psr[:, lo - 1 - ms:hi - 1 - ms], scalar=k.cw[:, l, 0, ci:ci + 1],
                        in1=yt[:, lo - s:hi - s], op0=ALU.mult, op1=ALU.add)),
                        reads=[k.bps[pi], k.bconst, byt], writes=[byt])
                for (lo, hi) in excl_segs(s, min(e_, ce - 1), badr):
                    S.op("dve", (lambda e, psr=psr, ci=ci, yt=yt, lo=lo, hi=hi, s=s, e_=e_, ms=ms, W=W, Wm=Wm: e.scalar_tensor_tensor(
                        out=yt[:, lo - s:hi - s], in0=psr[:, lo + 1 - ms:hi + 1 - ms], scalar=k.cw[:, l, 2, ci:ci + 1],
                        in1=yt[:, lo - s:hi - s], op0=ALU.mult, op1=ALU.add)),
                        reads=[k.bps[pi], k.bconst, byt], writes=[byt])
            ygt, bygt = yg[par]
            yvt, byvt = yv[par]
            S.op("act", (lambda e, ygt=ygt, s=s, e_=e_, ms=ms, W=W, Wm=Wm: e.activation(out=ygt[:, 0:W], in_=ygt[:, 0:W], func=AF.Silu)),
                 reads=[bygt], writes=[bygt])
            S.op("pool", (lambda e, ygt=ygt, yvt=yvt, fp=fp, s=s, e_=e_, ms=ms, W=W, Wm=Wm: e.tensor_tensor(out=G[:, fp, 0:W], in0=yvt[:, 0:W], in1=ygt[:, 0:W],
                                                                                op=ALU.mult)),
                 reads=[bygt, byvt], writes=[bG[fp]])
        for dcn in range(8):
            wv_, bw, ds = wd[wd_it[0] % 2]
            wd_it[0] += 1
            S.op("pool", (lambda e, wv_=wv_, dcn=dcn, s=s, e_=e_, ms=ms, W=W, Wm=Wm: e.dma_start(out=wv_, in_=d.wdn[l, dcn])), writes=[bw], dsem=ds)
            pi = 4 + dcn % 2
            for fc in range(NFC):
                S.op("pe", (lambda e, pi=pi, fc=fc, wv_=wv_, s=s, e_=e_, ms=ms, W=W, Wm=Wm: e.matmul(
                    k.ps[pi][:, 0:W], wv_[:, fc, :], G[:, fc, 0:W], start=(fc == 0), stop=(fc == NFC - 1))),
                    reads=[bw, bG[fc]], writes=[k.bps[pi]])
            S.op("act", (lambda e, pi=pi, dcn=dcn, s=s, e_=e_, ms=ms, W=W, Wm=Wm: e.activation(out=OST[:, dcn, 0:W], in_=k.ps[pi][:, 0:W], func=AF.Identity)),
                 reads=[k.bps[pi]], writes=[bOST])
        postnorm_update(k, X, bX, s, e_, l, 1, b, OST, bOST, scr)


def s5_layer(k, b, X, bX):
    nc, S, d = k.nc, k.S, k.d
    l = 1
    AR, AFa = k.AR, k.AF
    AR.reset()
    AFa.reset()
    st = k.stab
    bsp = Buf("xsp")
    for kk in range(8):
        S.op("sp", (lambda e, kk=kk: e.dma_start(out=d.xsp[:, kk, :], in_=X[:, kk, LC:NT])),
             reads=bX(LC, NT), writes=[bsp], dsem=k.dsem_x)
    Y = AR.take([8, LL]); bY = Buf("Y")
    o_ar = AR.o
    scr = make_scr(k, 256, 7)
    for w in range(NT // 256):
        c0 = w * 256
        modnorm(k, X, bX, c0, c0 + 256, l, 0, b, X, bX(c0, c0 + 256), scr, ho=c0)
    S.barrier()
    import os
    PART = int(os.environ.get("S5_PART", "99"))
    if PART <= 1:
        return
    AR.o = o_ar
    BpT = AR.take([16, 2, 128]); bBp = Buf("Bp")
    Cp = AR.take([16, 2, 128]); bCp = Buf("Cp")
    BpF = BpT.bitcast(F32)
    CpF = Cp.bitcast(F32)
    Hre = [AFa.take([16, 32]) for _ in range(2)]
    Him = [AFa.take([16, 32]) for _ in range(2)]
    bH = [Buf("H0"), Buf("H1")]
    XimS = AFa.take([16, 32]); bXimS = Buf("XimS")
    t2 = AFa.take([16, 32]); bt2 = Buf("t2")
    t4 = AFa.take([16, 32]); bt4 = Buf("t4")
    HinR = AFa.take([16, 32]); HinI = AFa.take([16, 32]); bHin = Buf("Hin")
    h0R = AFa.take([16]); h0I = AFa.take([16]); bh0 = Buf("h0")
    hAR = AFa.take([16]); hAI = AFa.take([16]); bhA = Buf("hA")
    sm = [AFa.take([16]) for _ in range(4)]; bsm = [Buf(f"sm{i}") for i in range(4)]
    tC = AFa.take([8, 128]); btC = Buf("tC")
    tD = AFa.take([8, 128]); btD = Buf("tD")
    bXps = [Buf("Xps0"), Buf("Xps1")]
    bXips = [Buf("Xips0"), Buf("Xips1")]
    bYps = [Buf("Yps0"), Buf("Yps1")]
    bU = bX(0, NT)
    first_y = [True] * 2

    def cstep_small(oR, oI, hR, hI, eR, eI, aT, bT, reads, writes):
        S.op("dve", lambda e: e.tensor_tensor(out=sm[0], in0=aT, in1=hR, op=ALU.mult), reads=reads, writes=[bsm[0]])
        S.op("dve", lambda e: e.tensor_tensor(out=sm[1], in0=bT, in1=hI, op=ALU.mult), reads=reads, writes=[bsm[1]])
        S.op("dve", lambda e: e.tensor_tensor(out=sm[0], in0=sm[0], in1=sm[1], op=ALU.subtract), reads=[bsm[0], bsm[1]], writes=[bsm[0]])
        S.op("pool", lambda e: e.tensor_tensor(out=sm[2], in0=bT, in1=hR, op=ALU.mult), reads=reads, writes=[bsm[2]])
        S.op("pool", lambda e: e.tensor_tensor(out=sm[3], in0=aT, in1=hI, op=ALU.mult), reads=reads, writes=[bsm[3]])
        S.op("pool", lambda e: e.tensor_tensor(out=sm[2], in0=sm[2], in1=sm[3], op=ALU.add), reads=[bsm[2], bsm[3]], writes=[bsm[2]])
        S.op("dve", lambda e: e.tensor_tensor(out=oR, in0=sm[0], in1=eR, op=ALU.add), reads=[bsm[0]] + reads, writes=writes)
        S.op("pool", lambda e: e.tensor_tensor(out=oI, in0=sm[2], in1=eI, op=ALU.add), reads=[bsm[2]] + reads, writes=writes)

    for dd in range(2):
        for hs in range(2):
            S.op("pool", (lambda e, dd=dd, hs=hs: e.dma_start(out=BpT, in_=d.Bf[:, dd, hs * 16:hs * 16 + 16])),
                 writes=[bBp], dsem=k.dsem_w[0])
            S.op("pool", (lambda e, dd=dd, hs=hs: e.dma_start(out=Cp, in_=d.Cf[:, dd, hs * 16:hs * 16 + 16])),
                 writes=[bCp], dsem=k.dsem_w[1])
            for hp in range(2):
                ps_ = slice(hp * 8, hp * 8 + 8)
                f0 = hs * 32 + hp * 16 + dd
                fre = st[:, 4, f0:f0 + 15:2].unsqueeze(2).broadcast_to([128, 8, 128])
                fim = st[:, 5, f0:f0 + 15:2].unsqueeze(2).broadcast_to([128, 8, 128])
                cre, cim = CpF[:, ps_, 0, :], CpF[:, ps_, 1, :]
                S.op("dve", (lambda e, cim=cim, fim=fim: e.tensor_tensor(out=tC, in0=cim, in1=fim, op=ALU.mult)), reads=[bCp, k.bconst], writes=[btC])
                S.op("dve", (lambda e, cre=cre, fre=fre: e.tensor_tensor(out=tD, in0=cre, in1=fre, op=ALU.mult)), reads=[bCp, k.bconst], writes=[btD])
                S.op("dve", lambda e: e.tensor_tensor(out=tD, in0=tD, in1=tC, op=ALU.subtract), reads=[btC, btD], writes=[btD])
                S.op("dve", (lambda e, cre=cre, fim=fim: e.tensor_tensor(out=tC, in0=cre, in1=fim, op=ALU.mult)), reads=[bCp, k.bconst, btD], writes=[btC])
                S.op("dve", (lambda e, cim=cim, fre=fre, ps_=ps_: e.tensor_tensor(out=Cp[:, ps_, 1, :], in0=cim, in1=fre, op=ALU.mult)), reads=[bCp, k.bconst], writes=[bCp])
                S.op("dve", (lambda e, cim=cim, ps_=ps_: e.scalar_tensor_tensor(out=Cp[:, ps_, 1, :], in0=cim, scalar=-1.0, in1=tC,
                                                                            op0=ALU.mult, op1=ALU.subtract)), reads=[btC, bCp], writes=[bCp])
                S.op("dve", (lambda e, ps_=ps_: e.tensor_copy(out=Cp[:, ps_, 0, :], in_=tD)), reads=[btD, bCp], writes=[bCp])
            if PART <= 2:
                continue
            sl = slice(hs * 32 + dd, hs * 32 + 32, 2)
            a16, b16 = st[:, 0, sl], st[:, 1, sl]
            aT16, bT16 = st[:, 2, sl], st[:, 3, sl]

            def sweep(cb0, nchk, init, contract):
                a_bc = a16.unsqueeze(2).broadcast_to([128, 16, nchk])
                b_bc = b16.unsqueeze(2).broadcast_to([128, 16, nchk])
                cur = 0
                if init is None:
                    S.op("dve", lambda e: e.memset(Hre[0][:, :, 0:nchk], 0.0), writes=[bH[0]])
                    S.op("pool", lambda e: e.memset(Him[0][:, :, 0:nchk], 0.0), writes=[bH[0]])
                else:
                    S.op("dve", lambda e: e.tensor_copy(out=Hre[0][:, :, 0:nchk], in_=init[0][:, :, 0:nchk]), reads=[bHin], writes=[bH[0]])
                    S.op("pool", lambda e: e.tensor_copy(out=Him[0][:, :, 0:nchk], in_=init[1][:, :, 0:nchk]), reads=[bHin], writes=[bH[0]])
                import os
                for s_ in range(int(os.environ.get("S5_STEPS", TCH))):
                    j = s_ if dd == 0 else TCH - 1 - s_
                    par = s_ % 2
                    xr = k.ps[0 + 2 * par][:, :].rearrange("p (q c) -> p q c", c=32)
                    xi = k.ps[1 + 2 * par][:, :].rearrange("p (q c) -> p q c", c=32)
                    c_lo = cb0 + j
                    c_hi = cb0 + j + TCH * (nchk - 1) + 1
                    for q in range(0 if os.environ.get("S5_NOMM") != "1" else 16, 16):
                        p = hs * 16 + q
                        dcn, r = p // 4, p % 4
                        rhs = X[:, dcn, c_lo:c_hi:TCH]
                        S.op("pe", (lambda e, q=q, rhs=rhs, xr=xr: e.matmul(
                            xr[:, q, 0:nchk], BpF[:, q, 0, :], rhs, start=True, stop=True)),
                            reads=[bBp] + bU, writes=[bXps[par]])
                        S.op("pe", (lambda e, q=q, rhs=rhs, xi=xi: e.matmul(
                            xi[:, q, 0:nchk], BpF[:, q, 1, :], rhs, start=True, stop=True)),
                            reads=[bBp] + bU, writes=[bXips[par]])
                    if os.environ.get("S5_NOEW") == "1":
                        continue
                    nxt = 1 - cur
                    hr, hi_ = Hre[cur][:, :, 0:nchk], Him[cur][:, :, 0:nchk]
                    nr, ni = Hre[nxt][:, :, 0:nchk], Him[nxt][:, :, 0:nchk]
                    S.op("act", (lambda e, xi=xi: e.activation(out=XimS[:, :, 0:nchk], in_=xi[:, :, 0:nchk], func=AF.Identity)),
                         reads=[bXips[par]], writes=[bXimS])
                    S.op("dve", (lambda e, nr=nr, hr=hr: e.tensor_tensor(out=nr, in0=a_bc, in1=hr, op=ALU.mult)),
                         reads=[bH[cur], k.bconst], writes=[bH[nxt]])
                    S.op("dve", (lambda e, hi_=hi_: e.tensor_tensor(out=t2[:, :, 0:nchk], in0=b_bc, in1=hi_, op=ALU.mult)),
                         reads=[bH[cur], k.bconst], writes=[bt2])
                    S.op("dve", (lambda e, nr=nr: e.tensor_tensor(out=nr, in0=nr, in1=t2[:, :, 0:nchk], op=ALU.subtract)),
                         reads=[bt2, bH[nxt]], writes=[bH[nxt]])
                    S.op("dve", (lambda e, nr=nr, xr=xr: e.tensor_tensor(out=nr, in0=nr, in1=xr[:, :, 0:nchk], op=ALU.add)),
                         reads=[bXps[par], bH[nxt]], writes=[bH[nxt]])
                    S.op(IMENG, (lambda e, ni=ni, hr=hr: e.tensor_tensor(out=ni, in0=b_bc, in1=hr, op=ALU.mult)),
                         reads=[bH[cur], k.bconst], writes=[bH[nxt]])
                    S.op(IMENG, (lambda e, hi_=hi_: e.tensor_tensor(out=t4[:, :, 0:nchk], in0=a_bc, in1=hi_, op=ALU.mult)),
                         reads=[bH[cur], k.bconst], writes=[bt4])
                    S.op(IMENG, (lambda e, ni=ni: e.tensor_tensor(out=ni, in0=ni, in1=t4[:, :, 0:nchk], op=ALU.add)),
                         reads=[bt4, bH[nxt]], writes=[bH[nxt]])
                    S.op(IMENG, (lambda e, ni=ni: e.tensor_tensor(out=ni, in0=ni, in1=XimS[:, :, 0:nchk], op=ALU.add)),
                         reads=[bXimS, bH[nxt]], writes=[bH[nxt]])
                    cur = nxt
                    if contract:
                        yps = k.ps[4 + par][:, 0:128].rearrange("p (a c) -> p a c", c=32)
                        for q in range(16):
                            p = hs * 16 + q
                            dcl, r = p // 4 - hs * 4, p % 4
                            S.op("pe", (lambda e, q=q, dcl=dcl, r=r, yps=yps, cur=cur: e.matmul(
                                yps[:, dcl, 0:nchk], CpF[:, q, 0, :], Hre[cur][:, q, 0:nchk],
                                start=(r == 0), stop=False)), reads=[bCp, bH[cur]], writes=[bYps[par]])
                            S.op("pe", (lambda e, q=q, dcl=dcl, r=r, yps=yps, cur=cur: e.matmul(
                                yps[:, dcl, 0:nchk], CpF[:, q, 1, :], Him[cur][:, q, 0:nchk],
                                start=False, stop=(r == 3))), reads=[bCp, bH[cur]], writes=[bYps[par]])
                        yv = Y[:, hs * 4:hs * 4 + 4, c_lo - LC:c_hi - LC:TCH]
                        if first_y[hs]:
                            S.op("act", (lambda e, yv=yv, yps=yps: e.activation(out=yv, in_=yps[:, :, 0:nchk], func=AF.Identity)),
                                 reads=[bYps[par]], writes=[bY])
                        else:
                            S.op("dve", (lambda e, yv=yv, yps=yps: e.tensor_tensor(out=yv, in0=yv.bitcast(F32), in1=yps[:, :, 0:nchk],
                                                                                    op=ALU.add)),
                                 reads=[bYps[par], bY], writes=[bY])
                return cur

            cur = sweep(0, LC // TCH, None, False)
            if PART <= 3:
                continue
            order = list(range(LC // TCH)) if dd == 0 else list(range(LC // TCH - 1, -1, -1))
            S.op("dve", lambda e: e.memset(h0R, 0.0), writes=[bh0])
            S.op("dve", lambda e: e.memset(h0I, 0.0), writes=[bh0])
            for c in order:
                cstep_small(hAR, hAI, h0R, h0I, Hre[cur][:, :, c], Him[cur][:, :, c], aT16, bT16,
                            [bh0, bH[cur], k.bconst], [bhA])
                S.op("dve", lambda e: e.tensor_copy(out=h0R, in_=hAR), reads=[bhA], writes=[bh0])
                S.op("dve", lambda e: e.tensor_copy(out=h0I, in_=hAI), reads=[bhA], writes=[bh0])
            if PART <= 4:
                continue
            nl = int(os.environ.get("S5_NL", LL // TCH))
            cur = sweep(LC, nl, None, False)
            if PART <= 5:
                continue
            order = list(range(nl)) if dd == 0 else list(range(nl - 1, -1, -1))
            c_first = order[0]
            S.op("dve", (lambda e, c_first=c_first: e.tensor_copy(out=HinR[:, :, c_first], in_=h0R)), reads=[bh0], writes=[bHin])
            S.op("dve", (lambda e, c_first=c_first: e.tensor_copy(out=HinI[:, :, c_first], in_=h0I)), reads=[bh0], writes=[bHin])
            for ci in range(nl - 1):
                c, cn = order[ci], order[ci + 1]
                cstep_small(HinR[:, :, cn], HinI[:, :, cn], HinR[:, :, c], HinI[:, :, c],
                            Hre[cur][:, :, c], Him[cur][:, :, c], aT16, bT16, [bHin, bH[cur], k.bconst], [bHin])
            if PART <= 6:
                continue
            sweep(LC, nl, (HinR, HinI), True)
            first_y[hs] = False
    S.barrier()
    if PART <= 7:
        return
    for kk in range(8):
        S.op("dve", (lambda e, kk=kk: e.scalar_tensor_tensor(out=Y[:, kk, :], in0=X[:, kk, LC:NT], scalar=k.dsk[:, kk:kk + 1],
                                                            in1=Y[:, kk, :].bitcast(F32), op0=ALU.mult, op1=ALU.add)),
             reads=bU + [bY, k.bconst], writes=[bY])
        S.op("act", (lambda e, kk=kk: e.activation(out=Y[:, kk, :], in_=Y[:, kk, :].bitcast(F32), func=AF.Gelu_apprx_tanh)),
             reads=[bY], writes=[bY])
    S.barrier()
    for kk in range(8):
        S.op("sp", (lambda e, kk=kk: e.dma_start(out=X[:, kk, LC:NT], in_=d.xsp[:, kk, :])),
             reads=[bsp], writes=bX(LC, NT), dsem=k.dsem_x)
    AR.o = o_ar
    AFa.reset()
    WG = 512
    wa = [(AR.take([8, 128]), Buf(f"wa{i}"), k.dsem_w[i]) for i in range(2)]
    wb = [(AR.take([8, 128]), Buf(f"wb{i}"), k.dsem_w[2 + i]) for i in range(2)]
    OST = AFa.take([8, WG]); bOST = Buf("OSTg")
    sg = [(AFa.take([WG]), Buf(f"sg{i}")) for i in range(2)]
    scr = make_scr(k, WG, 7)
    it = 0
    for w in range(LL // WG):
        c0 = w * WG
        for dcn in range(8):
            wa_, bwa, dsa = wa[it % 2]
            wb_, bwb, dsb = wb[it % 2]
            sg_, bsg = sg[it % 2]
            pa, pb = (0, 1) if it % 2 == 0 else (2, 3)
            it += 1
            S.op("pool", (lambda e, wa_=wa_, dcn=dcn: e.dma_start(out=wa_, in_=d.wglu[0, dcn])), writes=[bwa], dsem=dsa)
            S.op("pool", (lambda e, wb_=wb_, dcn=dcn: e.dma_start(out=wb_, in_=d.wglu[1, dcn])), writes=[bwb], dsem=dsb)
            for (pi, wt_, bw_) in ((pa, wa_, bwa), (pb, wb_, bwb)):
                for kk in range(8):
                    S.op("pe", (lambda e, pi=pi, wt_=wt_, kk=kk, c0=c0: e.matmul(
                        k.ps[pi][:, 0:WG], wt_[:, kk, :], Y[:, kk, c0:c0 + WG], start=(kk == 0), stop=(kk == 7))),
                        reads=[bw_, bY], writes=[k.bps[pi]])
            S.op("act", (lambda e, pb=pb, sg_=sg_: e.activation(out=sg_, in_=k.ps[pb][:, 0:WG], func=AF.Sigmoid)),
                 reads=[k.bps[pb]], writes=[bsg])
            S.op("dve", (lambda e, pa=pa, sg_=sg_, dcn=dcn: e.tensor_tensor(out=OST[:, dcn, :], in0=k.ps[pa][:, 0:WG], in1=sg_,
                                                                            op=ALU.mult)),
                 reads=[k.bps[pa], bsg], writes=[bOST])
        postnorm_update(k, X, bX, LC + c0, LC + c0 + WG, l, 0, b, OST, bOST, scr)


N_CORES = 8


def kernel(**inputs):
    inp = {k_: np.asarray(v) for k_, v in inputs.items()}
    sh = shared_inputs(inp)
    nc = build(stage=99, nb=2)
    in_maps = []
    for c in range(N_CORES):
        m = dict(sh)
        m.update(core_inputs(inp, c))
        in_maps.append(m)
    res = run_bass_kernel_spmd(nc, in_maps, core_ids=list(range(N_CORES)))
    out = np.empty((2 * N_CORES, LL, D), np.float32)
    for c in range(N_CORES):
        o = np.asarray(res.results[c]["outT"])
        out[2 * c] = o[0].T
        out[2 * c + 1] = o[1].T
    return out
```

```python
import contextlib
import math
import numpy as np
import concourse.bass as bass
import concourse.mybir as mybir
from concourse.bass_utils import run_bass_kernel_spmd

F32 = mybir.dt.float32
F32R = mybir.dt.float32r
AF = mybir.ActivationFunctionType
ALU = mybir.AluOpType

D = 1024
NT = 2304
LC = 256
LL = 2048
DFF = 2816
NFC = 22
EPS = 1e-6


class Buf:
    __slots__ = ("name", "w", "r")

    def __init__(self, name=""):
        self.name = name
        self.w = None
        self.r = {}


class DmaSem:
    def __init__(self, sched, name):
        self.key = ("dma", name)
        self.count = 0
        sched.dma_sems.append(self)


class Sched:
    COMPUTE = ("pe", "act", "dve", "pool")
    QUEUES = ("pe", "act", "dve", "pool", "sp")
    EPOCH = 12000

    def __init__(self):
        self.q = {e: [] for e in self.QUEUES}
        self.known = {e: {} for e in self.QUEUES}
        self.dma_sems = []
        self.pending = {e: {} for e in self.QUEUES}
        self.ec = {e: 0 for e in self.QUEUES}
        self.eng_keys = set()

    def _ekey(self, eng, count):
        ep = (count - 1) // self.EPOCH
        return ("eng", eng, ep), (count - 1) % self.EPOCH + 1

    def barrier(self):
        snap = {}
        for e in self.COMPUTE:
            if self.ec[e] > 0:
                key, val = self._ekey(e, self.ec[e])
                snap[key] = val
        for d in self.dma_sems:
            snap[d.key] = d.count
        for e in self.QUEUES:
            p = self.pending[e]
            for k, v in snap.items():
                if v > p.get(k, 0):
                    p[k] = v

    def op(self, eng, fn, reads=(), writes=(), dsem=None):
        q = self.q[eng]
        idx = len(q) + 1
        deps = dict(self.pending[eng])
        self.pending[eng] = {}
        for key in [kk for kk in deps if kk[0] == "eng" and kk[1] == eng]:
            del deps[key]

        def add(tok):
            if tok is None:
                return
            key, val, teng, tidx = tok
            if teng == eng and key[0] == "eng":
                if eng == "pe" or idx - tidx > 1:
                    return
            if deps.get(key, 0) < val:
                deps[key] = val

        for b in reads:
            add(b.w)
        for b in writes:
            add(b.w)
            for t in b.r.values():
                add(t)
        kn = self.known[eng]
        waits = []
        for key, val in deps.items():
            if val <= 0:
                continue
            if kn.get(key, 0) >= val:
                continue
            kn[key] = val
            waits.append((key, val))
        if dsem is not None:
            dsem.count += 16
            tok = (dsem.key, dsem.count, eng, idx)
            inc = (dsem.key, 16)
        else:
            self.ec[eng] += 1
            key, val = self._ekey(eng, self.ec[eng])
            self.eng_keys.add(key)
            tok = (key, val, eng, idx)
            inc = (key, 1)
        q.append((waits, fn, inc))
        for b in reads:
            b.r[(tok[0][0], tok[0][1])] = tok
        for b in writes:
            b.w = tok
            b.r = {}
        return tok

    def emit(self, nc, final_wait_tokens=()):
        with contextlib.ExitStack() as st:
            sems = {}
            for key in sorted(self.eng_keys):
                sems[key] = st.enter_context(nc.semaphore("s_%s_%d" % (key[1], key[2])))
            for d in self.dma_sems:
                sems[d.key] = st.enter_context(nc.semaphore("d_" + str(d.key[1])))
            block = st.enter_context(nc.Block())
            q = self.q

            def replay(eng_obj, name, extra_waits=()):
                for waits, fn, inc in q[name]:
                    for key, val in waits:
                        eng_obj.wait_ge(sems[key], val)
                    ins = fn(eng_obj)
                    ins.then_inc(sems[inc[0]], inc[1])
                for key, val in extra_waits:
                    eng_obj.wait_ge(sems[key], val)

            fw = [(t[0], t[1]) for t in final_wait_tokens]

            @block.tensor
            def _(e):
                replay(e, "pe")

            @block.scalar
            def _(e):
                replay(e, "act")

            @block.vector
            def _(e):
                replay(e, "dve")

            @block.gpsimd
            def _(e):
                replay(e, "pool")

            @block.sync
            def _(e):
                replay(e, "sp", fw)


def rope_tables():
    grid_w, head_dim, theta = 64, 128, 10000.0
    rows = LL // grid_w
    row = np.repeat(np.arange(rows, dtype=np.float32), grid_w)
    col = np.tile(np.arange(grid_w, dtype=np.float32), rows)
    ppa = head_dim // 4
    freq = (np.float32(theta) ** (-np.arange(ppa, dtype=np.float32) / np.float32(ppa))).astype(np.float32)
    ang = np.concatenate([row[:, None] * freq, col[:, None] * freq], axis=-1).astype(np.float32)
    cos = np.cos(ang).astype(np.float32).T
    sin = np.sin(ang).astype(np.float32).T
    return np.ascontiguousarray(np.concatenate([cos, cos], 0)), np.ascontiguousarray(np.concatenate([sin, sin], 0))


def pk(v):
    v = np.asarray(v, np.float32)
    lead = v.shape[:-1]
    m = v.shape[-1] // 128
    r = v.reshape(lead + (m, 128))
    r = np.moveaxis(r, -1, 0)
    return np.ascontiguousarray(r)


def shared_inputs(inp):
    s = {}
    s["w_mod"] = np.ascontiguousarray(inp["w_mod"], np.float32)
    s["bmT"] = pk(inp["b_mod"])
    g = np.stack([inp["g_pre_mix"], inp["g_post_mix"], inp["g_pre_ffn"], inp["g_post_ffn"]], 1)
    s["gT"] = pk(g)
    wqkv = np.asarray(inp["w_qkv"][0], np.float32)
    s["wqk"] = np.ascontiguousarray(wqkv[:, :1280].reshape(8, 128, 10, 128).transpose(2, 1, 0, 3))
    s["wv"] = np.ascontiguousarray(wqkv[:, 1280:].reshape(8, 128, 256).transpose(1, 0, 2))
    wo = np.asarray(inp["w_o"][0], np.float32)
    s["wo"] = np.ascontiguousarray(wo.reshape(8, 128, 8, 128).transpose(2, 1, 0, 3))
    s["qkg"] = np.ascontiguousarray(np.stack([inp["q_norm_g"][0], inp["k_norm_g"][0]], 1), np.float32)
    cos, sin = rope_tables()
    s["cos"], s["sin"] = cos, sin
    rot = np.zeros((128, 128), np.float32)
    for m in range(64):
        rot[m + 64, m] = -1.0
        rot[m, m + 64] = 1.0
    s["rotm"] = rot
    s["ones"] = np.ones((128, 128), np.float32)
    wup = np.asarray(inp["w_up"], np.float32)
    wu = wup.reshape(2, 8, 128, 2, NFC, 128)
    s["wup"] = np.ascontiguousarray(wu.transpose(0, 4, 2, 1, 3, 5)).reshape(2, NFC, 128, 8, 256)
    wdn = np.asarray(inp["w_down"], np.float32)
    s["wdn"] = np.ascontiguousarray(wdn.reshape(2, NFC, 128, 8, 128).transpose(0, 3, 2, 1, 4))
    s["cw"] = pk(inp["conv_w"])
    s["cb"] = pk(inp["conv_b"])
    lre = np.asarray(inp["ssm_lambda_re"][0], np.float32)
    lim = np.asarray(inp["ssm_lambda_im"][0], np.float32)
    lst = np.asarray(inp["ssm_log_step"][0], np.float32)

    def en_pd(a):
        return a.reshape(2, 32, 2, 64).transpose(2, 3, 1, 0).reshape(128, 64)

    lam = np.stack([en_pd(lre), en_pd(lim), en_pd(np.broadcast_to(lst[:, :, None], (2, 64, 64)))], 1)
    s["lam"] = np.ascontiguousarray(lam, np.float32)
    Bf = np.zeros((4, 2, 16, 2, 32, 2, 2, 64), np.float32)
    Cf = np.zeros((2, 64, 2, 32, 2, 4, 2, 16), np.float32)
    for ri, (bsrc, csrc) in enumerate(((inp["ssm_b_re"], inp["ssm_c_re"]), (inp["ssm_b_im"], inp["ssm_c_im"]))):
        bsrc = np.asarray(bsrc[0], np.float32)
        csrc = np.asarray(csrc[0], np.float32)
        for p_ in range(32):
            r_ = p_ % 4
            for e_ in range(2):
                g_ = 2 * p_ + e_
                Bf[r_, e_, :, :, p_, ri, e_, :] = bsrc[:, g_, :, :].transpose(2, 0, 1)
                Cf[e_, :, :, p_, ri, r_, e_, :] = csrc[:, g_, :, :].transpose(2, 0, 1)
    s["Bf"] = np.ascontiguousarray(Bf.reshape(128, 2, 32, 2, 128))
    s["Cf"] = np.ascontiguousarray(Cf.reshape(128, 2, 32, 2, 128))
    s["dsk"] = pk(inp["ssm_d"][0])
    wg = np.stack([inp["w_glu_a"][0], inp["w_glu_b"][0]]).astype(np.float32)
    s["wglu"] = np.ascontiguousarray(wg.reshape(2, 8, 128, 8, 128).transpose(0, 3, 2, 1, 4))
    return s


def core_inputs(inp, core):
    b0 = 2 * core
    m = {}
    xs = []
    for b in (b0, b0 + 1):
        xs.append(np.concatenate([inp["ctx"][b], inp["x"][b]], 0).T)
    m["xT"] = np.ascontiguousarray(np.stack(xs), np.float32)
    c3 = np.stack([inp["c"][b0], inp["c"][b0 + 1], inp["c_ctx"]], -1)
    m["cT"] = np.ascontiguousarray(c3.reshape(8, 128, 3).transpose(1, 0, 2), np.float32)
    return m


NAR = 24576
NAF = 7744
TCH = 64
TWO_PI = 2.0 * math.pi
import os as _os
IMENG = _os.environ.get('IMENG', 'pool')


class K:
    pass


class XB:
    def __init__(self):
        self.b = [Buf(f"X{i}") for i in range(NT // 256)]

    def __call__(self, c0, c1):
        return self.b[c0 // 256:(c1 + 255) // 256]


class Arena:
    def __init__(self, t, n, dt):
        self.t, self.n, self.dt, self.o = t, n, dt, 0

    def reset(self):
        self.o = 0

    def take(self, shape):
        n = int(np.prod(shape))
        v = self.t[:, self.o:self.o + n]
        self.o += n
        assert self.o <= self.n, (self.o, self.n)
        if len(shape) == 2:
            v = v.rearrange("p (a b) -> p a b", b=shape[1])
        elif len(shape) == 3:
            v = v.rearrange("p (a b c) -> p a b c", b=shape[1], c=shape[2])
        elif len(shape) == 4:
            v = v.rearrange("p (a b c d) -> p a b c d", b=shape[1], c=shape[2], d=shape[3])
        return v


def build(stage=99, nb=2):
    nc = bass.Bass("TRN2", target_bir_lowering=False)
    S = Sched()
    k = K()
    k.nc, k.S = nc, S

    def din(name, shape, dt=F32):
        return nc.dram_tensor(name, list(shape), dt, kind="ExternalInput").ap()

    d = K()
    d.xT = din("xT", [2, D, NT])
    d.cT = din("cT", [128, 8, 3])
    d.w_mod = din("w_mod", [2, D, 6 * D])
    d.bmT = din("bmT", [128, 2, 48])
    d.gT = din("gT", [128, 2, 4, 8])
    d.wqk = din("wqk", [10, 128, 8, 128])
    d.wv = din("wv", [128, 8, 256])
    d.wo = din("wo", [8, 128, 8, 128])
    d.qkg = din("qkg", [128, 2])
    d.cos = din("cos", [128, LL])
    d.sin = din("sin", [128, LL])
    d.rotm = din("rotm", [128, 128])
    d.ones = din("ones", [128, 128])
    d.wup = din("wup", [2, NFC, 128, 8, 256])
    d.wdn = din("wdn", [2, 8, 128, NFC, 128])
    d.cw = din("cw", [128, 2, 3, 44])
    d.cb = din("cb", [128, 2, 44])
    d.lam = din("lam", [128, 3, 64])
    d.Bf = din("Bf", [128, 2, 32, 2, 128])
    d.Cf = din("Cf", [128, 2, 32, 2, 128])
    d.dsk = din("dsk", [128, 8])
    d.wglu = din("wglu", [2, 8, 128, 8, 128])
    d.out = nc.dram_tensor("outT", [2, D, LL], F32, kind="ExternalOutput").ap()
    if stage != 99:
        d.dbg = nc.dram_tensor("dbg", [128, 8, NT], F32, kind="ExternalOutput").ap()
    d.xsp = nc.dram_tensor("xspill", [128, 8, LL], F32, kind="Internal").ap()
    k.d = d

    with contextlib.ExitStack() as st:
        def sb(name, shape, dt=F32):
            return st.enter_context(nc.sbuf_tensor("sb_" + name, list(shape), dt))

        k.Xt = sb("X", [128, 8 * NT])
        k.ARt = sb("AR", [128, NAR], F32R)
        k.AFt = sb("AF", [128, NAF])
        k.AR = Arena(k.ARt, NAR, F32R)
        k.AF = Arena(k.AFt, NAF, F32)
        k.ones = sb("ones", [128, 128], F32R)
        k.rotm = sb("rotm", [128, 128], F32R)
        k.modv = sb("modv", [128, 2, 48, 3])
        k.gT = sb("gT", [128, 2, 4, 8])
        k.sA = sb("sA", [128, 2, 2, 3, 8])
        k.sG = sb("sG", [128, 2, 2, 3, 8])
        k.qkg = sb("qkg", [128, 2])
        k.cw = sb("cw", [128, 2, 3, 44])
        k.cb = sb("cb", [128, 2, 44])
        k.epsD = sb("epsD", [128, 1])
        k.eps128 = sb("eps128", [128, 1])
        k.bmT = sb("bmT", [128, 2, 48])
        k.dsk = sb("dsk", [128, 8])
        k.stab = sb("stab", [128, 6, 64])
        k.ps01 = st.enter_context(nc.psum_tensor("ps01", [128, 1024], F32))
        k.ps23 = st.enter_context(nc.psum_tensor("ps23", [128, 1024], F32))
        k.ps = [k.ps01[:, 0:512], k.ps01[:, 512:1024], k.ps23[:, 0:512], k.ps23[:, 512:1024]]
        k.ps += [st.enter_context(nc.psum_tensor(f"ps{i}", [128, 512], F32)) for i in range(4, 8)]
        k.bps = [Buf(f"ps{i}") for i in range(8)]
        k.bconst = Buf("const")
        k.dsem_c = DmaSem(S, "const")
        k.dsem_o = DmaSem(S, "out")
        k.dsem_x = DmaSem(S, "x")
        k.dsem_w = [DmaSem(S, f"w{i}") for i in range(6)]
        k.dsem_m = [DmaSem(S, f"m{i}") for i in range(2)]
        k.dsem_cs = DmaSem(S, "const_sw")

        prologue(k)
        fin = []
        for b in range(nb):
            fin += batch_element(k, b, stage)
        S.emit(nc, fin)
    k.ninstr = {e: len(q) for e, q in S.q.items()}
    print("instr counts", k.ninstr)
    return nc


def prologue(k):
    nc, S, d = k.nc, k.S, k.d
    bc = k.bconst
    dc = k.dsem_c
    S.op("pool", lambda e: e.dma_start(out=k.ones[:], in_=d.ones), writes=[bc], dsem=k.dsem_cs)
    S.op("pool", lambda e: e.dma_start(out=k.rotm[:], in_=d.rotm), writes=[bc], dsem=k.dsem_cs)
    for t, src in ((k.gT, d.gT), (k.qkg, d.qkg), (k.cw, d.cw), (k.cb, d.cb), (k.bmT, d.bmT), (k.dsk, d.dsk)):
        S.op("sp", (lambda e, t=t, src=src: e.dma_start(out=t[:], in_=src)), writes=[bc], dsem=dc)
    S.op("dve", lambda e: e.memset(k.epsD[:], EPS), writes=[bc])
    S.op("dve", lambda e: e.memset(k.eps128[:], EPS), writes=[bc])
    AFa = k.AF
    AFa.reset()
    cT = AFa.take([8, 3])
    sc = AFa.take([8, 3])
    bcT, bsc = Buf("cT"), Buf("sc")
    S.op("sp", lambda e: e.dma_start(out=cT, in_=d.cT), writes=[bcT], dsem=dc)
    S.op("act", lambda e: e.activation(out=sc, in_=cT, func=AF.Silu), reads=[bcT], writes=[bsc])
    import os
    PS = int(os.environ.get("PRO_STOP", "9"))
    if PS <= 1:
        S.barrier()
        return
    NP = 384
    wt = [(AFa.take([8, NP]), Buf(f"wm{i}")) for i in range(2)]
    bmod = Buf("modv")
    it = 0
    for l in range(2):
        for j in range(6 * D // NP):
            wv_, wb = wt[it % 2]
            ds = k.dsem_m[it % 2]
            src = d.w_mod[l, :, j * NP:(j + 1) * NP].rearrange("(kk p) n -> p kk n", p=128)
            S.op("sp", (lambda e, wv_=wv_, src=src: e.dma_start(out=wv_, in_=src)), writes=[wb], dsem=ds)
            for c in range(NP // 128):
                oc = j * (NP // 128) + c
                pst = k.ps[l][:, oc * 3:oc * 3 + 3]
                for kk in range(8):
                    S.op("pe", (lambda e, pst=pst, wv_=wv_, c=c, kk=kk: e.matmul(
                        pst, wv_[:, kk, c * 128:(c + 1) * 128], sc[:, kk, :], start=(kk == 0), stop=(kk == 7))),
                        reads=[wb, bsc], writes=[k.bps[l]])
            it += 1
        S.op("dve", (lambda e, l=l: e.tensor_tensor(
            out=k.modv[:, l, :, :], in0=k.ps[l][:, 0:144].rearrange("p (a b) -> p a b", b=3),
            in1=k.bmT[:, l, :].unsqueeze(2).broadcast_to([128, 48, 3]), op=ALU.add)),
            reads=[k.bps[l], bc], writes=[bmod])
    if PS <= 2:
        S.barrier()
        return
    for l in range(2):
        for j, (gi, si) in enumerate(((0, 1), (2, 4))):
            for r in range(3):
                S.op("dve", (lambda e, l=l, j=j, gi=gi, si=si, r=r: e.scalar_tensor_tensor(
                    out=k.sA[:, l, j, r, :], in0=k.modv[:, l, si * 8:(si + 1) * 8, r], scalar=1.0,
                    in1=k.gT[:, l, gi, :], op0=ALU.add, op1=ALU.mult)), reads=[bmod, bc], writes=[bc])
        for j, (gi, si) in enumerate(((1, 2), (3, 5))):
            for r in range(3):
                S.op("dve", (lambda e, l=l, j=j, gi=gi, si=si, r=r: e.tensor_tensor(
                    out=k.sG[:, l, j, r, :], in0=k.modv[:, l, si * 8:(si + 1) * 8, r],
                    in1=k.gT[:, l, gi, :], op=ALU.mult)), reads=[bmod, bc], writes=[bc])
    S.barrier()
    if PS <= 3:
        return
    s5_tables(k)
    S.barrier()


def s5_tables(k):
    S, d = k.S, k.d
    AFa = k.AF
    AFa.reset()
    lam = AFa.take([3, 64])
    w = [AFa.take([64]) for _ in range(10)]
    bt = Buf("s5t")
    S.op("sp", lambda e: e.dma_start(out=lam, in_=d.lam), writes=[bt], dsem=k.dsem_c)
    lre, lim, lst = lam[:, 0, :], lam[:, 1, :], lam[:, 2, :]
    dt_, xr, xi, mag, tq, tr, cs, sn, den, t2 = w
    st = k.stab

    def dve(fn):
        S.op("dve", fn, reads=[bt], writes=[bt])

    def act(fn):
        S.op("act", fn, reads=[bt], writes=[bt])

    act(lambda e: e.activation(out=dt_, in_=lst, func=AF.Exp))
    dve(lambda e: e.tensor_tensor(out=xr, in0=lre, in1=dt_, op=ALU.mult))
    dve(lambda e: e.tensor_tensor(out=xi, in0=lim, in1=dt_, op=ALU.mult))
    MAGIC = 12582912.0

    def sincos(mult, out_c, out_s):
        for off, dst in ((0.0, out_s), (math.pi / 2, out_c)):
            dve(lambda e, off=off: e.tensor_scalar(out=tq, in0=xi, scalar1=float(mult), scalar2=float(off),
                                                    op0=ALU.mult, op1=ALU.add))
            dve(lambda e: e.tensor_scalar(out=tr, in0=tq, scalar1=1.0 / TWO_PI, scalar2=MAGIC,
                                          op0=ALU.mult, op1=ALU.add))
            dve(lambda e: e.tensor_scalar(out=tr, in0=tr, scalar1=MAGIC, scalar2=None, op0=ALU.subtract))
            dve(lambda e: e.scalar_tensor_tensor(out=tq, in0=tr, scalar=-TWO_PI, in1=tq, op0=ALU.mult, op1=ALU.add))
            dve(lambda e: e.tensor_scalar(out=tq, in0=tq, scalar1=3.1415925, scalar2=-3.1415925,
                                          op0=ALU.min, op1=ALU.max))
            act(lambda e, dst=dst: e.activation(out=dst, in_=tq, func=AF.Sin))

    for mult, ia, ib in ((1.0, 0, 1), (float(TCH), 2, 3)):
        sincos(mult, cs, sn)
        act(lambda e, mult=mult: e.activation(out=mag, in_=xr, func=AF.Exp, scale=float(mult)))
        dve(lambda e, ia=ia: e.tensor_tensor(out=st[:, ia, :], in0=mag, in1=cs, op=ALU.mult))
        dve(lambda e, ib=ib: e.tensor_tensor(out=st[:, ib, :], in0=mag, in1=sn, op=ALU.mult))
    dve(lambda e: e.tensor_tensor(out=den, in0=lre, in1=lre, op=ALU.mult))
    dve(lambda e: e.tensor_tensor(out=t2, in0=lim, in1=lim, op=ALU.mult))
    dve(lambda e: e.tensor_tensor(out=den, in0=den, in1=t2, op=ALU.add))
    dve(lambda e: e.reciprocal(out=den, in_=den))
    dve(lambda e: e.tensor_scalar(out=tq, in0=st[:, 0, :], scalar1=-1.0, scalar2=None, op0=ALU.add))
    dve(lambda e: e.tensor_tensor(out=tr, in0=tq, in1=lre, op=ALU.mult))
    dve(lambda e: e.tensor_tensor(out=t2, in0=st[:, 1, :], in1=lim, op=ALU.mult))
    dve(lambda e: e.tensor_tensor(out=tr, in0=tr, in1=t2, op=ALU.add))
    dve(lambda e: e.tensor_tensor(out=st[:, 4, :], in0=tr, in1=den, op=ALU.mult))
    dve(lambda e: e.tensor_tensor(out=tr, in0=st[:, 1, :], in1=lre, op=ALU.mult))
    dve(lambda e: e.tensor_tensor(out=t2, in0=tq, in1=lim, op=ALU.mult))
    dve(lambda e: e.tensor_tensor(out=tr, in0=tr, in1=t2, op=ALU.subtract))
    dve(lambda e: e.tensor_tensor(out=st[:, 5, :], in0=tr, in1=den, op=ALU.mult))
    S.op("dve", lambda e: e.tensor_copy(out=cs, in_=st[:, 5, :]), reads=[bt], writes=[bt, k.bconst])


def segs_rows(c0, c1, b):
    out = []
    if c0 < LC:
        out.append((c0, min(c1, LC), 2))
    if c1 > LC:
        out.append((max(c0, LC), c1, b))
    return out


def norm_stats(k, src_fn, reads, W, scr):
    S = k.S
    pi = scr["ps"]
    pst = k.ps[pi][:, 0:W]
    for kk in range(8):
        sq, bsq = scr["sq"][kk % 2]
        S.op("act", (lambda e, sq=sq, kk=kk: e.activation(out=sq[:, 0:W], in_=src_fn(kk), func=AF.Square)),
             reads=reads, writes=[bsq])
        S.op("pe", (lambda e, sq=sq, kk=kk: e.matmul(pst, k.ones[:], sq[:, 0:W], start=(kk == 0), stop=(kk == 7))),
             reads=[bsq, k.bconst], writes=[k.bps[pi]])
    rstd, brs = scr["rstd"]
    S.op("act", lambda e: e.activation(out=rstd[:, 0:W], in_=pst, func=AF.Sqrt, scale=1.0 / D, bias=k.epsD[:]),
         reads=[k.bps[pi], k.bconst], writes=[brs])
    S.op("dve", lambda e: e.reciprocal(out=rstd[:, 0:W], in_=rstd[:, 0:W]), reads=[brs], writes=[brs])
    return rstd, brs


def modnorm(k, X, bX, c0, c1, l, j, b, hout, bh, scr, ho=0):
    S = k.S
    W = c1 - c0
    Ws = W + (W % 2)
    assert c0 + Ws <= NT
    rstd, brs = norm_stats(k, lambda kk: X[:, kk, c0:c0 + Ws], bX(c0, c0 + Ws), Ws, scr)
    shift_i = 0 if j == 0 else 3
    for kk in range(8):
        tmp, btmp = scr["tmp"][kk % 2]
        S.op("dve", (lambda e, tmp=tmp, kk=kk: e.tensor_tensor(out=tmp[:, 0:W], in0=X[:, kk, c0:c1], in1=rstd[:, 0:W],
                                                               op=ALU.mult)), reads=bX(c0, c1) + [brs], writes=[btmp])
        for (lo, hi, r) in segs_rows(c0, c1, b):
            S.op("act", (lambda e, tmp=tmp, kk=kk, lo=lo, hi=hi, r=r: e.activation(
                out=hout[:, kk, ho + lo - c0:ho + hi - c0], in_=tmp[:, lo - c0:hi - c0], func=AF.Identity,
                scale=k.sA[:, l, j, r, kk:kk + 1], bias=k.modv[:, l, shift_i * 8 + kk, r:r + 1])),
                reads=[btmp, k.bconst], writes=(bh if isinstance(bh, list) else [bh]))


def postnorm_update(k, X, bX, c0, c1, l, j, b, ost, bost, scr):
    S = k.S
    W = c1 - c0
    rstd, brs = norm_stats(k, lambda kk: ost[:, kk, 0:W], [bost], W, scr)
    for kk in range(8):
        tmp, btmp = scr["tmp"][kk % 2]
        S.op("dve", (lambda e, tmp=tmp, kk=kk: e.tensor_tensor(out=tmp[:, 0:W], in0=ost[:, kk, 0:W], in1=rstd[:, 0:W],
                                                               op=ALU.mult)), reads=[bost, brs], writes=[btmp])
        for (lo, hi, r) in segs_rows(c0, c1, b):
            S.op("dve", (lambda e, tmp=tmp, kk=kk, lo=lo, hi=hi, r=r: e.scalar_tensor_tensor(
                out=X[:, kk, lo:hi], in0=tmp[:, lo - c0:hi - c0], scalar=k.sG[:, l, j, r, kk:kk + 1],
                in1=X[:, kk, lo:hi], op0=ALU.mult, op1=ALU.add)), reads=[btmp, k.bconst] + bX(lo, hi), writes=bX(lo, hi))


def dump_x(k, X, bX, fin):
    S, d = k.S, k.d
    for kk in range(8):
        fin.append(S.op("sp", (lambda e, kk=kk: e.dma_start(out=d.dbg[:, kk, :], in_=X[:, kk, :])),
                        reads=bX(0, NT), dsem=k.dsem_o))


def batch_element(k, b, stage):
    nc, S, d = k.nc, k.S, k.d
    X = k.Xt[:, :].rearrange("p (a t) -> p a t", t=NT)
    bX = XB()
    for kk in range(8):
        S.op("sp", (lambda e, kk=kk: e.dma_start(out=X[:, kk, :], in_=d.xT[b, kk * 128:(kk + 1) * 128, :])),
             writes=bX(0, NT), dsem=k.dsem_x)
    fin = []
    if stage == 0:
        dump_x(k, X, bX, fin)
        S.barrier()
        return fin
    import os
    if os.environ.get("S5_ONLY") == "1":
        s5_layer(k, b, X, bX)
        S.barrier()
        dump_x(k, X, bX, fin)
        S.barrier()
        return fin
    attention_layer(k, b, X, bX)
    S.barrier()
    if stage == 1:
        if b == 0:
            dump_x(k, X, bX, fin)
        S.barrier()
        return fin
    ffn_layer(k, b, X, bX, 0, 0, NT)
    S.barrier()
    if stage == 2:
        if b == 0:
            dump_x(k, X, bX, fin)
        S.barrier()
        return fin
    s5_layer(k, b, X, bX)
    S.barrier()
    if stage == 3:
        if b == 0:
            dump_x(k, X, bX, fin)
        S.barrier()
        return fin
    ffn_layer(k, b, X, bX, 1, LC, NT)
    S.barrier()
    for kk in range(8):
        fin.append(S.op("sp", (lambda e, kk=kk: e.dma_start(out=d.out[b, kk * 128:(kk + 1) * 128, :], in_=X[:, kk, LC:NT])),
                        reads=bX(LC, NT), dsem=k.dsem_o))
    S.barrier()
    return fin


def make_scr(k, W, ps):
    AR, AFa = k.AR, k.AF
    scr = {"ps": ps, "sq": [], "tmp": []}
    for i in range(2):
        scr["sq"].append((AR.take([W]), Buf(f"sq{i}")))
    for i in range(2):
        scr["tmp"].append((AFa.take([W]), Buf(f"tmp{i}")))
    scr["rstd"] = (AFa.take([W]), Buf("rstd"))
    return scr


def attention_layer(k, b, X, bX):
    nc, S, d = k.nc, k.S, k.d
    l = 0
    WQ = 256
    AR, AFa = k.AR, k.AF
    AR.reset()
    AFa.reset()
    KT = AR.take([2, NT])
    V = AR.take([18, 256])
    AT = AR.take([8, WQ])
    HW = AR.take([8, WQ])
    QT = AR.take([8, WQ])
    bKT, bV, bAT, bHW, bQT = Buf("KT"), Buf("V"), Buf("AT"), Buf("HW"), Buf("QT")
    wq = [(AR.take([8, 128]), Buf(f"wq{i}"), k.dsem_w[i]) for i in range(2)]
    wo = [(AR.take([8, 128]), Buf(f"wo{i}"), k.dsem_w[2 + i]) for i in range(2)]
    WV = AR.take([8, 256]); bWV = Buf("WV")
    PT = [(AR.take([WQ]), Buf(f"PT{i}")) for i in range(4)]
    KN = AR.take([WQ]); bKN = Buf("KN")
    SQ2 = AR.take([WQ]); bSQ2 = Buf("SQ2")
    COS = AFa.take([LL])
    SIN = AFa.take([LL])
    OST = AFa.take([8, WQ]); bOST = Buf("OST")
    RS2 = AFa.take([WQ]); bRS2 = Buf("RS2")
    T2 = AFa.take([WQ]); bT2 = Buf("T2")
    RINV = AFa.take([WQ]); bRINV = Buf("RINV")
    scr = make_scr(k, WQ, 7)
    brope = Buf("rope")
    S.op("sp", lambda e: e.dma_start(out=COS, in_=d.cos), writes=[brope], dsem=k.dsem_c)
    S.op("sp", lambda e: e.dma_start(out=SIN, in_=d.sin), writes=[brope], dsem=k.dsem_c)
    S.op("pool", lambda e: e.dma_start(out=WV, in_=d.wv), writes=[bWV], dsem=k.dsem_w[4])
    bS = {(4, 0): Buf("S40"), (4, 1): Buf("S41"), (5, 0): Buf("S50"), (5, 1): Buf("S51")}
    wq_it = [0]

    def load_wq(ci):
        v, bw, ds = wq[wq_it[0] % 2]
        wq_it[0] += 1
        S.op("pool", (lambda e, v=v, ci=ci: e.dma_start(out=v, in_=d.wqk[ci])), writes=[bw], dsem=ds)
        return v, bw

    def qk_head(ci, gcol, c0, lat, dst, bdst):
        wv_, bw = load_wq(ci)
        pi = 6
        pst = k.ps[pi][:, 0:WQ]
        for kk in range(8):
            S.op("pe", (lambda e, kk=kk: e.matmul(pst, wv_[:, kk, :], HW[:, kk, :], start=(kk == 0), stop=(kk == 7))),
                 reads=[bw, bHW], writes=[k.bps[pi]])
        S.op("act", lambda e: e.activation(out=SQ2, in_=pst, func=AF.Square), reads=[k.bps[pi]], writes=[bSQ2])
        p2 = k.ps[pi][:, 256:256 + WQ]
        S.op("pe", lambda e: e.matmul(p2, k.ones[:], SQ2, start=True, stop=True), reads=[bSQ2, k.bconst],
             writes=[k.bps[pi]])
        S.op("act", lambda e: e.activation(out=RS2, in_=p2, func=AF.Sqrt, scale=1.0 / 128, bias=k.eps128[:]),
             reads=[k.bps[pi], k.bconst], writes=[bRS2])
        S.op("dve", lambda e: e.reciprocal(out=RS2, in_=RS2), reads=[bRS2], writes=[bRS2])
        if not lat:
            S.op("dve", lambda e: e.scalar_tensor_tensor(out=dst, in0=pst, scalar=k.qkg[:, gcol:gcol + 1], in1=RS2,
                                                         op0=ALU.mult, op1=ALU.mult),
                 reads=[k.bps[pi], bRS2, k.bconst], writes=[bdst])
            return
        S.op("dve", lambda e: e.scalar_tensor_tensor(out=KN, in0=pst, scalar=k.qkg[:, gcol:gcol + 1], in1=RS2,
                                                     op0=ALU.mult, op1=ALU.mult),
             reads=[k.bps[pi], bRS2, k.bconst], writes=[bKN])
        S.op("pe", lambda e: e.matmul(p2, k.rotm[:], KN, start=True, stop=True), reads=[bKN, k.bconst, bRS2],
             writes=[k.bps[pi]])
        t0 = c0 - LC
        S.op("dve", lambda e: e.tensor_tensor(out=T2, in0=p2, in1=SIN[:, t0:t0 + WQ], op=ALU.mult),
             reads=[k.bps[pi], brope], writes=[bT2])
        S.op("dve", lambda e: e.tensor_tensor(out=RS2, in0=KN.bitcast(F32), in1=COS[:, t0:t0 + WQ], op=ALU.mult),
             reads=[bKN, brope], writes=[bRS2])
        S.op("dve", lambda e: e.tensor_tensor(out=dst, in0=RS2, in1=T2, op=ALU.add), reads=[bRS2, bT2], writes=[bdst])

    for w in range(NT // WQ):
        c0 = w * WQ
        lat = c0 >= LC
        modnorm(k, X, bX, c0, c0 + WQ, l, 0, b, HW, bHW, scr)
        for jk in range(2):
            qk_head(8 + jk, 1, c0, lat, KT[:, jk, c0:c0 + WQ], bKT)
        for sub in range(2):
            pi = 5
            pst = k.ps[pi][:, sub * 256:(sub + 1) * 256]
            for kk in range(8):
                S.op("pe", (lambda e, kk=kk, sub=sub, pst=pst: e.matmul(
                    pst, HW[:, kk, sub * 128:(sub + 1) * 128], WV[:, kk, :], start=(kk == 0), stop=(kk == 7))),
                    reads=[bHW, bWV], writes=[k.bps[pi]])
            S.op("act", (lambda e, sub=sub, pst=pst, w=w: e.activation(out=V[:, 2 * w + sub, :], in_=pst, func=AF.Identity)),
                 reads=[k.bps[pi]], writes=[bV])
    scale = 1.0 / math.sqrt(128.0)
    wo_it = [0]
    pt_it = [0]
    for w in range(NT // WQ):
        c0 = w * WQ
        lat = c0 >= LC
        nkt = 18 if lat else 2
        modnorm(k, X, bX, c0, c0 + WQ, l, 0, b, HW, bHW, scr)
        for h in range(8):
            qk_head(h, 0, c0, lat, QT[:, h, :], bQT)
        for h in range(8):
            jk = h // 4
            po, pr = (0, 1) if h % 2 == 0 else (2, 3)
            for kt in range(nkt):
                sp_i = 4 if (kt % 2 == 0) else 5
                half = (kt // 2) % 2
                pss = k.ps[sp_i][:, half * 256:half * 256 + WQ]
                S.op("pe", (lambda e, pss=pss, jk=jk, kt=kt, h=h: e.matmul(
                    pss, KT[:, jk, kt * 128:(kt + 1) * 128], QT[:, h, :], start=True, stop=True)),
                    reads=[bKT, bQT], writes=[bS[(sp_i, half)]])
                pt, bpt = PT[pt_it[0] % 4]
                pt_it[0] += 1
                S.op("act", (lambda e, pt=pt, pss=pss: e.activation(out=pt, in_=pss, func=AF.Exp, scale=scale)),
                     reads=[bS[(sp_i, half)]], writes=[bpt])
                S.op("pe", (lambda e, pt=pt, jk=jk, kt=kt, po=po, nkt=nkt: e.matmul(
                    k.ps[po][:, 0:WQ], V[:, kt, jk * 128:(jk + 1) * 128], pt, start=(kt == 0), stop=(kt == nkt - 1))),
                    reads=[bV, bpt], writes=[k.bps[po]])
                S.op("pe", (lambda e, pt=pt, kt=kt, pr=pr, nkt=nkt: e.matmul(
                    k.ps[pr][:, 0:WQ], k.ones[:], pt, start=(kt == 0), stop=(kt == nkt - 1))),
                    reads=[bpt, k.bconst], writes=[k.bps[pr]])
            S.op("dve", (lambda e, pr=pr: e.reciprocal(out=RINV, in_=k.ps[pr][:, 0:WQ])), reads=[k.bps[pr]], writes=[bRINV])
            S.op("dve", (lambda e, po=po, h=h: e.tensor_tensor(out=AT[:, h, :], in0=k.ps[po][:, 0:WQ], in1=RINV, op=ALU.mult)),
                 reads=[k.bps[po], bRINV], writes=[bAT])
        for dcn in range(8):
            v, bw, ds = wo[wo_it[0] % 2]
            wo_it[0] += 1
            S.op("pool", (lambda e, v=v, dcn=dcn: e.dma_start(out=v, in_=d.wo[dcn])), writes=[bw], dsem=ds)
            pi = 6
            pst = k.ps[pi][:, 0:WQ]
            for h in range(8):
                S.op("pe", (lambda e, v=v, h=h, pst=pst: e.matmul(pst, v[:, h, :], AT[:, h, :], start=(h == 0), stop=(h == 7))),
                     reads=[bw, bAT], writes=[k.bps[pi]])
            S.op("act", (lambda e, dcn=dcn, pst=pst: e.activation(out=OST[:, dcn, :], in_=pst, func=AF.Identity)),
                 reads=[k.bps[pi]], writes=[bOST])
        postnorm_update(k, X, bX, c0, c0 + WQ, l, 0, b, OST, bOST, scr)


def excl_segs(lo, hi, bad):
    if hi <= lo:
        return []
    if bad is None or bad < lo or bad >= hi:
        return [(lo, hi)]
    out = []
    if bad > lo:
        out.append((lo, bad))
    if bad + 1 < hi:
        out.append((bad + 1, hi))
    return out


def ffn_windows(cs, ce):
    n = ce - cs
    if n == NT:
        sizes = [460, 462, 460, 462, 460]
    else:
        sizes = [410, 410, 408, 410, 410]
    assert sum(sizes) == n
    out = []
    s = cs
    for w in sizes:
        out.append((s, s + w))
        s += w
    return out


def ffn_layer(k, b, X, bX, l, cs, ce):
    nc, S, d = k.nc, k.S, k.d
    AR, AFa = k.AR, k.AF
    AR.reset()
    AFa.reset()
    WM = 464
    HW = AR.take([8, WM]); bHW = Buf("HWf")
    G = AR.take([NFC, 462]); bG = [Buf(f"G{i}") for i in range(NFC)]
    wu = [(AR.take([8, 256]), Buf(f"wu{i}"), k.dsem_w[i]) for i in range(2)]
    wd = [(AR.take([NFC, 128]), Buf(f"wd{i}"), k.dsem_w[2 + i]) for i in range(2)]
    HALO = AR.take([8, 2]); bHALO = Buf("halo")
    OST = AFa.take([8, 462]); bOST = Buf("OSTf")
    yv = [(AFa.take([WM]), Buf(f"yv{i}")) for i in range(2)]
    yg = [(AFa.take([WM]), Buf(f"yg{i}")) for i in range(2)]
    scr = make_scr(k, WM, 7)
    wins = ffn_windows(cs, ce)
    wu_it = [0]
    wd_it = [0]
    for wi, (s, e_) in enumerate(wins):
        W = e_ - s
        ms, me = max(s - 1, cs), min(e_ + 1, ce)
        if (me - ms) % 2 == 1:
            if me < ce:
                me += 1
            else:
                ms -= 1
        Wm = me - ms
        nleft = s - ms
        modnorm(k, X, bX, s, me, l, 1, b, HW, bHW, scr, ho=nleft)
        if nleft > 0:
            S.op("dve", (lambda e, nleft=nleft, s=s, e_=e_, ms=ms, W=W, Wm=Wm: e.tensor_copy(out=HW[:, :, 0:nleft], in_=HALO[:, :, 2 - nleft:2])),
                 reads=[bHALO], writes=[bHW])
        S.op("dve", (lambda e, s=s, e_=e_, ms=ms: e.tensor_copy(out=HALO[:, :, :], in_=HW[:, :, e_ - 2 - ms:e_ - ms])),
             reads=[bHW], writes=[bHALO])
        badl = LC if l == 0 else None
        badr = LC - 1 if l == 0 else None
        for fp in range(NFC):
            wv_, bw, ds = wu[wu_it[0] % 2]
            wu_it[0] += 1
            S.op("pool", (lambda e, wv_=wv_, fp=fp, s=s, e_=e_, ms=ms, W=W, Wm=Wm: e.dma_start(out=wv_, in_=d.wup[l, fp])), writes=[bw], dsem=ds)
            par = fp % 2
            pv, pg = (0, 1) if par == 0 else (2, 3)
            for (pi, co) in ((pv, 0), (pg, 128)):
                for kk in range(8):
                    S.op("pe", (lambda e, pi=pi, co=co, kk=kk, wv_=wv_, s=s, e_=e_, ms=ms, W=W, Wm=Wm: e.matmul(
                        k.ps[pi][:, 0:Wm], wv_[:, kk, co:co + 128], HW[:, kk, 0:Wm], start=(kk == 0), stop=(kk == 7))),
                        reads=[bw, bHW], writes=[k.bps[pi]])
            for (pi, ci, (yt, byt)) in ((pv, fp, yv[par]), (pg, NFC + fp, yg[par])):
                psr = k.ps[pi]
                S.op("act", (lambda e, psr=psr, ci=ci, yt=yt, s=s, e_=e_, ms=ms, W=W, Wm=Wm: e.activation(
                    out=yt[:, 0:W], in_=psr[:, s - ms:e_ - ms], func=AF.Identity,
                    scale=k.cw[:, l, 1, ci:ci + 1], bias=k.cb[:, l, ci:ci + 1])),
                    reads=[k.bps[pi], k.bconst], writes=[byt])
                for (lo, hi) in excl_segs(max(s, cs + 1), e_, badl):
                    S.op("dve", (lambda e, psr=psr, ci=ci, yt=yt, lo=lo, hi=hi, s=s, e_=e_, ms=ms, W=W, Wm=Wm: e.scalar_tensor_tensor(
                        out=yt[:, lo - s:hi - s], in0=psr[:, lo - 1 - ms:hi - 1 - ms], scalar=k.cw[:, l, 0, ci:ci + 1],
                        in1=yt[:, lo - s:hi - s], op0=ALU.mult, op1=ALU.add)),
                        reads=[k.bps[pi], k.bconst, byt], writes=[byt])
                for (lo, hi) in excl_segs(s, min(e_, ce - 1), badr):
                    S.op("dve", (lambda e, psr=psr, ci=ci, yt=yt, lo=lo, hi=hi, s=s, e_=e_, ms=ms, W=W, Wm=Wm: e.scalar_tensor_tensor(
                        out=yt[:, lo - s:hi - s], in0=psr[:, lo + 1 - ms:hi + 1 - ms], scalar=k.cw[:, l, 2, ci:ci + 1],
                        in1=yt[:, lo - s:hi - s], op0=ALU.mult, op1=ALU.add)),
                        reads=[k.bps[pi], k.bconst, byt], writes=[byt])
            ygt, bygt = yg[par]
            yvt, byvt = yv[par]
            S.op("act", (lambda e, ygt=ygt, s=s, e_=e_, ms=ms, W=W, Wm=Wm: e.activation(out=ygt[:, 0:W], in_=ygt[:, 0:W], func=AF.Silu)),
                 reads=[bygt], writes=[bygt])
            S.op("pool", (lambda e, ygt=ygt, yvt=yvt, fp=fp, s=s, e_=e_, ms=ms, W=W, Wm=Wm: e.tensor_tensor(out=G[:, fp, 0:W], in0=yvt[:, 0:W], in1=ygt[:, 0:W],
                                                                                op=ALU.mult)),
                 reads=[bygt, byvt], writes=[bG[fp]])
        for dcn in range(8):
            wv_, bw, ds = wd[wd_it[0] % 2]
            wd_it[0] += 1
            S.op("pool", (lambda e, wv_=wv_, dcn=dcn, s=s, e_=e_, ms=ms, W=W, Wm=Wm: e.dma_start(out=wv_, in_=d.wdn[l, dcn])), writes=[bw], dsem=ds)
            pi = 4 + dcn % 2
            for fc in range(NFC):
                S.op("pe", (lambda e, pi=pi, fc=fc, wv_=wv_, s=s, e_=e_, ms=ms, W=W, Wm=Wm: e.matmul(
                    k.ps[pi][:, 0:W], wv_[:, fc, :], G[:, fc, 0:W], start=(fc == 0), stop=(fc == NFC - 1))),
                    reads=[bw, bG[fc]], writes=[k.bps[pi]])
            S.op("act", (lambda e, pi=pi, dcn=dcn, s=s, e_=e_, ms=ms, W=W, Wm=Wm: e.activation(out=OST[:, dcn, 0:W], in_=k.ps[pi][:, 0:W], func=AF.Identity)),
                 reads=[k.bps[pi]], writes=[bOST])
        postnorm_update(k, X, bX, s, e_, l, 1, b, OST, bOST, scr)


def s5_layer(k, b, X, bX):
    nc, S, d = k.nc, k.S, k.d
    l = 1
    AR, AFa = k.AR, k.AF
    AR.reset()
    AFa.reset()
    st = k.stab
    bsp = Buf("xsp")
    for kk in range(8):
        S.op("sp", (lambda e, kk=kk: e.dma_start(out=d.xsp[:, kk, :], in_=X[:, kk, LC:NT])),
             reads=bX(LC, NT), writes=[bsp], dsem=k.dsem_x)
    Y = AR.take([8, LL]); bY = Buf("Y")
    o_ar = AR.o
    scr = make_scr(k, 256, 7)
    for w in range(NT // 256):
        c0 = w * 256
        modnorm(k, X, bX, c0, c0 + 256, l, 0, b, X, bX(c0, c0 + 256), scr, ho=c0)
    S.barrier()
    AR.o = o_ar
    AFa.reset()
    QN, SG = 4, 4
    NG = TCH // SG
    NS = 2
    bU = bX(0, NT)
    for kk in range(8):
        S.op("dve", (lambda e, kk=kk: e.tensor_scalar(out=Y[:, kk, :], in0=X[:, kk, LC:NT], scalar1=k.dsk[:, kk:kk + 1], scalar2=None,
                                                     op0=ALU.mult)), reads=bU + [k.bconst], writes=[bY])

    class Stream:
        pass

    sts = []
    for dd in range(NS):
        t = Stream()
        t.dd = dd
        t.BpT = AR.take([QN, 2, 128]); t.bBp = Buf(f"Bp{dd}")
        t.Cp = AR.take([QN, 2, 128]); t.bCp = Buf(f"Cp{dd}")
        t.BpF, t.CpF = t.BpT.bitcast(F32), t.Cp.bitcast(F32)
        t.XS = [AR.take([2 * QN * SG * 32]) for _ in range(2)]
        t.bXS = [Buf(f"XS{dd}0"), Buf(f"XS{dd}1")]
        t.XSF = [x.bitcast(F32) for x in t.XS]
        t.HsR = [AFa.take([SG + 1, QN, 32]) for _ in range(2)]
        t.HsI = [AFa.take([SG + 1, QN, 32]) for _ in range(2)]
        t.t2 = AFa.take([QN, 32]); t.bt2 = Buf(f"t2{dd}")
        t.t4 = AFa.take([QN, 32]); t.bt4 = Buf(f"t4{dd}")
        t.bHs = [Buf(f"Hs{dd}0"), Buf(f"Hs{dd}1")]
        t.HinR = AFa.take([QN, 32]); t.HinI = AFa.take([QN, 32]); t.bHin = Buf(f"Hin{dd}")
        t.h0R = AFa.take([QN]); t.h0I = AFa.take([QN]); t.bh0 = Buf(f"h0{dd}")
        t.hAR = AFa.take([QN]); t.hAI = AFa.take([QN]); t.bhA = Buf(f"hA{dd}")
        t.sm = [AFa.take([QN]) for _ in range(4)]; t.bsm = [Buf(f"sm{dd}{i}") for i in range(4)]
        t.XP = [k.ps[2 * dd], k.ps[2 * dd + 1]]
        t.bXP = [Buf(f"XPre{dd}"), Buf(f"XPim{dd}")]
        t.yps_i = 4 + dd
        t.bYps = Buf(f"Yps{dd}")
        t.gcount = 0
        t.islot = 0 if dd == 0 else SG
        t.eslot = SG if dd == 0 else 0
        t.lo_s = 1 if dd == 0 else 0
        sts.append(t)
    tC = AFa.take([QN, 128]); btC = Buf("tC")
    tD = AFa.take([QN, 128]); btD = Buf("tD")

    def cstep_small(t, oR, oI, hR, hI, eR, eI, reads, writes):
        sm, bsm, aT, bT = t.sm, t.bsm, t.aT16, t.bT16
        S.op("dve", lambda e: e.tensor_tensor(out=sm[0], in0=aT, in1=hR, op=ALU.mult), reads=reads, writes=[bsm[0]])
        S.op("dve", lambda e: e.tensor_tensor(out=sm[1], in0=bT, in1=hI, op=ALU.mult), reads=reads, writes=[bsm[1]])
        S.op("pool", lambda e: e.tensor_tensor(out=sm[2], in0=bT, in1=hR, op=ALU.mult), reads=reads, writes=[bsm[2]])
        S.op("pool", lambda e: e.tensor_tensor(out=sm[3], in0=aT, in1=hI, op=ALU.mult), reads=reads, writes=[bsm[3]])
        S.op("dve", lambda e: e.tensor_tensor(out=sm[0], in0=sm[0], in1=sm[1], op=ALU.subtract), reads=[bsm[0], bsm[1]], writes=[bsm[0]])
        S.op("pool", lambda e: e.tensor_tensor(out=sm[2], in0=sm[2], in1=sm[3], op=ALU.add), reads=[bsm[2], bsm[3]], writes=[bsm[2]])
        S.op("dve", lambda e: e.tensor_tensor(out=oR, in0=sm[0], in1=eR, op=ALU.add), reads=[bsm[0]] + reads, writes=writes)
        S.op("pool", lambda e: e.tensor_tensor(out=oI, in0=sm[2], in1=eI, op=ALU.add), reads=[bsm[2]] + reads, writes=writes)

    def sweep(dc, cb0, nchk, use_init, contract):
        Xl = X[:, dc, cb0:cb0 + TCH * nchk].rearrange("p (c t) -> p c t", t=TCH)
        Yl = Y[:, dc, :].rearrange("p (c t) -> p c t", t=TCH)

        def xsv(tt):
            return tt[:, 0:2 * QN * SG * nchk].rearrange("p (r q s c) -> p r q s c", r=2, q=QN, s=SG)

        for t in sts:
            hb = t.gcount % 2
            if not use_init:
                S.op("dve", (lambda e, t=t, hb=hb: e.memset(t.HsR[hb][:, t.islot, :, 0:nchk], 0.0)), writes=[t.bHs[hb]])
                S.op("pool", (lambda e, t=t, hb=hb: e.memset(t.HsI[hb][:, t.islot, :, 0:nchk], 0.0)), writes=[t.bHs[hb]])
            else:
                S.op("dve", (lambda e, t=t, hb=hb: e.tensor_copy(out=t.HsR[hb][:, t.islot, :, 0:nchk], in_=t.HinR[:, :, 0:nchk])),
                     reads=[t.bHin], writes=[t.bHs[hb]])
                S.op("pool", (lambda e, t=t, hb=hb: e.tensor_copy(out=t.HsI[hb][:, t.islot, :, 0:nchk], in_=t.HinI[:, :, 0:nchk])),
                     reads=[t.bHin], writes=[t.bHs[hb]])
        for g in range(NG):
            ctx_ = []
            for t in sts:
                hb = t.gcount % 2
                xb = t.gcount % 2
                t.gcount += 1
                j_lo = g * SG if t.dd == 0 else TCH - (g + 1) * SG
                rhs = Xl[:, :, j_lo:j_lo + SG].transpose([0, 2, 1])
                for ri in range(2):
                    xp = t.XP[ri][:, 0:QN * SG * nchk].rearrange("p (q s c) -> p q s c", s=SG, c=nchk)
                    for q in range(QN):
                        S.op("pe", (lambda e, t=t, xp=xp, q=q, ri=ri, rhs=rhs: e.matmul(
                            xp[:, q, :, :], t.BpF[:, q, ri, :], rhs, start=True, stop=True)),
                            reads=[t.bBp] + bU, writes=[t.bXP[ri]])
                    S.op("act", (lambda e, t=t, xp=xp, ri=ri, xb=xb: e.activation(
                        out=xsv(t.XS[xb])[:, ri, :, :, :], in_=xp[:, :, :, :], func=AF.Identity)),
                        reads=[t.bXP[ri]], writes=[t.bXS[xb]])
                ctx_.append((t, hb, xb, j_lo))
            for s_ in range(SG):
                ops = []
                for (t, hb, xb, j_lo) in ctx_:
                    rs, ws = (s_, s_ + 1) if t.dd == 0 else (SG - s_, SG - 1 - s_)
                    xs = s_ if t.dd == 0 else SG - 1 - s_
                    o = Stream()
                    o.t, o.hb, o.xb = t, hb, xb
                    o.a_bc = t.a16.unsqueeze(2).broadcast_to([128, QN, nchk])
                    o.b_bc = t.b16.unsqueeze(2).broadcast_to([128, QN, nchk])
                    o.hr, o.hi = t.HsR[hb][:, rs, :, 0:nchk], t.HsI[hb][:, rs, :, 0:nchk]
                    o.nr, o.ni = t.HsR[hb][:, ws, :, 0:nchk], t.HsI[hb][:, ws, :, 0:nchk]
                    o.xre, o.xim = xsv(t.XSF[xb])[:, 0, :, xs, :], xsv(t.XSF[xb])[:, 1, :, xs, :]
                    o.rd = [t.bHs[hb], k.bconst]
                    ops.append(o)
                for o in ops:
                    S.op("dve", (lambda e, o=o: e.tensor_tensor(out=o.nr, in0=o.a_bc, in1=o.hr, op=ALU.mult)), reads=o.rd, writes=[o.t.bHs[o.hb]])
                    S.op("dve", (lambda e, o=o: e.tensor_tensor(out=o.t.t2[:, :, 0:nchk], in0=o.b_bc, in1=o.hi, op=ALU.mult)), reads=o.rd, writes=[o.t.bt2])
                    S.op("pool", (lambda e, o=o: e.tensor_tensor(out=o.ni, in0=o.b_bc, in1=o.hr, op=ALU.mult)), reads=o.rd, writes=[o.t.bHs[o.hb]])
                    S.op("pool", (lambda e, o=o: e.tensor_tensor(out=o.t.t4[:, :, 0:nchk], in0=o.a_bc, in1=o.hi, op=ALU.mult)), reads=o.rd, writes=[o.t.bt4])
                for o in ops:
                    S.op("dve", (lambda e, o=o: e.tensor_tensor(out=o.nr, in0=o.nr, in1=o.t.t2[:, :, 0:nchk], op=ALU.subtract)),
                         reads=[o.t.bt2, o.t.bHs[o.hb]], writes=[o.t.bHs[o.hb]])
                    S.op("pool", (lambda e, o=o: e.tensor_tensor(out=o.ni, in0=o.ni, in1=o.t.t4[:, :, 0:nchk], op=ALU.add)),
                         reads=[o.t.bt4, o.t.bHs[o.hb]], writes=[o.t.bHs[o.hb]])
                for o in ops:
                    S.op("dve", (lambda e, o=o: e.tensor_tensor(out=o.nr, in0=o.nr, in1=o.xre, op=ALU.add)),
                         reads=[o.t.bXS[o.xb], o.t.bHs[o.hb]], writes=[o.t.bHs[o.hb]])
                    S.op("pool", (lambda e, o=o: e.tensor_tensor(out=o.ni, in0=o.ni, in1=o.xim, op=ALU.add)),
                         reads=[o.t.bXS[o.xb], o.t.bHs[o.hb]], writes=[o.t.bHs[o.hb]])
            for (t, hb, xb, j_lo) in ctx_:
                if contract:
                    yps = k.ps[t.yps_i][:, 0:SG * 32].rearrange("p (s c) -> p s c", c=32)
                    for q in range(QN):
                        S.op("pe", (lambda e, t=t, q=q, yps=yps, hb=hb: e.matmul(
                            yps[:, :, 0:nchk], t.CpF[:, q, 0, :], t.HsR[hb][:, t.lo_s:t.lo_s + SG, q, 0:nchk],
                            start=(q == 0), stop=False)), reads=[t.bCp, t.bHs[hb]], writes=[t.bYps])
                        S.op("pe", (lambda e, t=t, q=q, yps=yps, hb=hb: e.matmul(
                            yps[:, :, 0:nchk], t.CpF[:, q, 1, :], t.HsI[hb][:, t.lo_s:t.lo_s + SG, q, 0:nchk],
                            start=False, stop=(q == QN - 1))), reads=[t.bCp, t.bHs[hb]], writes=[t.bYps])
                    yv = Yl[:, :, j_lo:j_lo + SG].transpose([0, 2, 1])
                    S.op("dve", (lambda e, yv=yv, yps=yps: e.tensor_tensor(out=yv, in0=yv.bitcast(F32), in1=yps[:, :, 0:nchk],
                                                                            op=ALU.add)),
                         reads=[t.bYps, bY], writes=[bY])
                if g < NG - 1:
                    nb_ = t.gcount % 2
                    S.op("dve", (lambda e, t=t, hb=hb, nb_=nb_: e.tensor_copy(out=t.HsR[nb_][:, t.islot, :, 0:nchk], in_=t.HsR[hb][:, t.eslot, :, 0:nchk])),
                         reads=[t.bHs[hb]], writes=[t.bHs[nb_]])
                    S.op("pool", (lambda e, t=t, hb=hb, nb_=nb_: e.tensor_copy(out=t.HsI[nb_][:, t.islot, :, 0:nchk], in_=t.HsI[hb][:, t.eslot, :, 0:nchk])),
                         reads=[t.bHs[hb]], writes=[t.bHs[nb_]])
        return [(t.gcount - 1) % 2 for t in sts]

    for dc in range(8):
        for t in sts:
            dd = t.dd
            S.op("pool", (lambda e, t=t, dd=dd, dc=dc: e.dma_start(out=t.BpT, in_=d.Bf[:, dd, dc * QN:dc * QN + QN])),
                 writes=[t.bBp], dsem=k.dsem_w[2 * dd])
            S.op("pool", (lambda e, t=t, dd=dd, dc=dc: e.dma_start(out=t.Cp, in_=d.Cf[:, dd, dc * QN:dc * QN + QN])),
                 writes=[t.bCp], dsem=k.dsem_w[2 * dd + 1])
            f0 = 2 * dc * QN + dd
            sl = slice(f0, f0 + 2 * QN - 1, 2)
            t.a16, t.b16 = st[:, 0, sl], st[:, 1, sl]
            t.aT16, t.bT16 = st[:, 2, sl], st[:, 3, sl]
            fre = st[:, 4, sl].unsqueeze(2).broadcast_to([128, QN, 128])
            fim = st[:, 5, sl].unsqueeze(2).broadcast_to([128, QN, 128])
            cre, cim = t.CpF[:, :, 0, :], t.CpF[:, :, 1, :]
            Cp, bCp = t.Cp, t.bCp
            S.op("dve", (lambda e, fim=fim, cim=cim: e.tensor_tensor(out=tC, in0=cim, in1=fim, op=ALU.mult)), reads=[bCp, k.bconst], writes=[btC])
            S.op("dve", (lambda e, fre=fre, cre=cre: e.tensor_tensor(out=tD, in0=cre, in1=fre, op=ALU.mult)), reads=[bCp, k.bconst], writes=[btD])
            S.op("dve", lambda e: e.tensor_tensor(out=tD, in0=tD, in1=tC, op=ALU.subtract), reads=[btC, btD], writes=[btD])
            S.op("dve", (lambda e, fim=fim, cre=cre: e.tensor_tensor(out=tC, in0=cre, in1=fim, op=ALU.mult)), reads=[bCp, k.bconst, btD], writes=[btC])
            S.op("dve", (lambda e, fre=fre, cim=cim, Cp=Cp: e.tensor_tensor(out=Cp[:, :, 1, :], in0=cim, in1=fre, op=ALU.mult)), reads=[bCp, k.bconst], writes=[bCp])
            S.op("dve", (lambda e, cim=cim, Cp=Cp: e.scalar_tensor_tensor(out=Cp[:, :, 1, :], in0=cim, scalar=-1.0, in1=tC,
                                                                        op0=ALU.mult, op1=ALU.subtract)), reads=[btC, bCp], writes=[bCp])
            S.op("dve", (lambda e, Cp=Cp: e.tensor_copy(out=Cp[:, :, 0, :], in_=tD)), reads=[btD, bCp], writes=[bCp])
        hbs = sweep(dc, 0, LC // TCH, False, False)
        for t, hb in zip(sts, hbs):
            S.op("dve", (lambda e, t=t: e.memset(t.h0R, 0.0)), writes=[t.bh0])
            S.op("dve", (lambda e, t=t: e.memset(t.h0I, 0.0)), writes=[t.bh0])
        ncx = LC // TCH
        for ci in range(ncx):
            for t, hb in zip(sts, hbs):
                c = ci if t.dd == 0 else ncx - 1 - ci
                cstep_small(t, t.hAR, t.hAI, t.h0R, t.h0I, t.HsR[hb][:, t.eslot, :, c], t.HsI[hb][:, t.eslot, :, c],
                            [t.bh0, t.bHs[hb], k.bconst], [t.bhA])
            for t, hb in zip(sts, hbs):
                S.op("dve", (lambda e, t=t: e.tensor_copy(out=t.h0R, in_=t.hAR)), reads=[t.bhA], writes=[t.bh0])
                S.op("pool", (lambda e, t=t: e.tensor_copy(out=t.h0I, in_=t.hAI)), reads=[t.bhA], writes=[t.bh0])
        nl = LL // TCH
        hbs = sweep(dc, LC, nl, False, False)
        for t, hb in zip(sts, hbs):
            c_first = 0 if t.dd == 0 else nl - 1
            S.op("dve", (lambda e, t=t, c_first=c_first: e.tensor_copy(out=t.HinR[:, :, c_first], in_=t.h0R)), reads=[t.bh0], writes=[t.bHin])
            S.op("pool", (lambda e, t=t, c_first=c_first: e.tensor_copy(out=t.HinI[:, :, c_first], in_=t.h0I)), reads=[t.bh0], writes=[t.bHin])
        for ci in range(nl - 1):
            for t, hb in zip(sts, hbs):
                c, cn = (ci, ci + 1) if t.dd == 0 else (nl - 1 - ci, nl - 2 - ci)
                cstep_small(t, t.HinR[:, :, cn], t.HinI[:, :, cn], t.HinR[:, :, c], t.HinI[:, :, c],
                            t.HsR[hb][:, t.eslot, :, c], t.HsI[hb][:, t.eslot, :, c], [t.bHin, t.bHs[hb], k.bconst], [t.bHin])
        sweep(dc, LC, nl, True, True)
    S.barrier()
    for kk in range(8):
        S.op("act", (lambda e, kk=kk: e.activation(out=Y[:, kk, :], in_=Y[:, kk, :].bitcast(F32), func=AF.Gelu_apprx_tanh)),
             reads=[bY], writes=[bY])
    S.barrier()
    for kk in range(8):
        S.op("sp", (lambda e, kk=kk: e.dma_start(out=X[:, kk, LC:NT], in_=d.xsp[:, kk, :])),
             reads=[bsp], writes=bX(LC, NT), dsem=k.dsem_x)
    AR.o = o_ar
    AFa.reset()
    WG = 512
    wa = [(AR.take([8, 128]), Buf(f"wa{i}"), k.dsem_w[i]) for i in range(2)]
    wb = [(AR.take([8, 128]), Buf(f"wb{i}"), k.dsem_w[2 + i]) for i in range(2)]
    OST = AFa.take([8, WG]); bOST = Buf("OSTg")
    sg = [(AFa.take([WG]), Buf(f"sg{i}")) for i in range(2)]
    scr = make_scr(k, WG, 7)
    it = 0
    for w in range(LL // WG):
        c0 = w * WG
        for dcn in range(8):
            wa_, bwa, dsa = wa[it % 2]
            wb_, bwb, dsb = wb[it % 2]
            sg_, bsg = sg[it % 2]
            pa, pb = (0, 1) if it % 2 == 0 else (2, 3)
            it += 1
            S.op("pool", (lambda e, wa_=wa_, dcn=dcn: e.dma_start(out=wa_, in_=d.wglu[0, dcn])), writes=[bwa], dsem=dsa)
            S.op("pool", (lambda e, wb_=wb_, dcn=dcn: e.dma_start(out=wb_, in_=d.wglu[1, dcn])), writes=[bwb], dsem=dsb)
            for (pi, wt_, bw_) in ((pa, wa_, bwa), (pb, wb_, bwb)):
                for kk in range(8):
                    S.op("pe", (lambda e, pi=pi, wt_=wt_, kk=kk, c0=c0: e.matmul(
                        k.ps[pi][:, 0:WG], wt_[:, kk, :], Y[:, kk, c0:c0 + WG], start=(kk == 0), stop=(kk == 7))),
                        reads=[bw_, bY], writes=[k.bps[pi]])
            S.op("act", (lambda e, pb=pb, sg_=sg_: e.activation(out=sg_, in_=k.ps[pb][:, 0:WG], func=AF.Sigmoid)),
                 reads=[k.bps[pb]], writes=[bsg])
            S.op("dve", (lambda e, pa=pa, sg_=sg_, dcn=dcn: e.tensor_tensor(out=OST[:, dcn, :], in0=k.ps[pa][:, 0:WG], in1=sg_,
                                                                            op=ALU.mult)),
                 reads=[k.bps[pa], bsg], writes=[bOST])
        postnorm_update(k, X, bX, LC + c0, LC + c0 + WG, l, 0, b, OST, bOST, scr)


N_CORES = 8


def kernel(**inputs):
    inp = {k_: np.asarray(v) for k_, v in inputs.items()}
    sh = shared_inputs(inp)
    nc = build(stage=99, nb=2)
    in_maps = []
    for c in range(N_CORES):
        m = dict(sh)
        m.update(core_inputs(inp, c))
        in_maps.append(m)
    res = run_bass_kernel_spmd(nc, in_maps, core_ids=list(range(N_CORES)))
    out = np.empty((2 * N_CORES, LL, D), np.float32)
    for c in range(N_CORES):
        o = np.asarray(res.results[c]["outT"])
        out[2 * c] = o[0].T
        out[2 * c + 1] = o[1].T
    return out
```

```python
import contextlib
import math
import numpy as np
import concourse.bass as bass
import concourse.mybir as mybir
from concourse.bass_utils import run_bass_kernel_spmd

F32 = mybir.dt.float32
F32R = mybir.dt.float32r
AF = mybir.ActivationFunctionType
ALU = mybir.AluOpType

D = 1024
NT = 2304
LC = 256
LL = 2048
DFF = 2816
NFC = 22
EPS = 1e-6


class Buf:
    __slots__ = ("name", "w", "r")

    def __init__(self, name=""):
        self.name = name
        self.w = None
        self.r = {}


class DmaSem:
    def __init__(self, sched, name):
        self.key = ("dma", name)
        self.count = 0
        sched.dma_sems.append(self)


class Sched:
    COMPUTE = ("pe", "act", "dve", "pool")
    QUEUES = ("pe", "act", "dve", "pool", "sp")
    EPOCH = 12000

    def __init__(self):
        self.q = {e: [] for e in self.QUEUES}
        self.known = {e: {} for e in self.QUEUES}
        self.dma_sems = []
        self.pending = {e: {} for e in self.QUEUES}
        self.ec = {e: 0 for e in self.QUEUES}
        self.eng_keys = set()

    def _ekey(self, eng, count):
        ep = (count - 1) // self.EPOCH
        return ("eng", eng, ep), (count - 1) % self.EPOCH + 1

    def barrier(self):
        snap = {}
        for e in self.COMPUTE:
            if self.ec[e] > 0:
                key, val = self._ekey(e, self.ec[e])
                snap[key] = val
        for d in self.dma_sems:
            snap[d.key] = d.count
        for e in self.QUEUES:
            p = self.pending[e]
            for k, v in snap.items():
                if v > p.get(k, 0):
                    p[k] = v

    def op(self, eng, fn, reads=(), writes=(), dsem=None):
        q = self.q[eng]
        idx = len(q) + 1
        deps = dict(self.pending[eng])
        self.pending[eng] = {}
        for key in [kk for kk in deps if kk[0] == "eng" and kk[1] == eng]:
            del deps[key]

        def add(tok):
            if tok is None:
                return
            key, val, teng, tidx = tok
            if teng == eng and key[0] == "eng":
                if eng == "pe" or idx - tidx > 1:
                    return
            if deps.get(key, 0) < val:
                deps[key] = val

        for b in reads:
            add(b.w)
        for b in writes:
            add(b.w)
            for t in b.r.values():
                add(t)
        kn = self.known[eng]
        waits = []
        for key, val in deps.items():
            if val <= 0:
                continue
            if kn.get(key, 0) >= val:
                continue
            kn[key] = val
            waits.append((key, val))
        if dsem is not None:
            dsem.count += 16
            tok = (dsem.key, dsem.count, eng, idx)
            inc = (dsem.key, 16)
        else:
            self.ec[eng] += 1
            key, val = self._ekey(eng, self.ec[eng])
            self.eng_keys.add(key)
            tok = (key, val, eng, idx)
            inc = (key, 1)
        q.append((waits, fn, inc))
        for b in reads:
            b.r[(tok[0][0], tok[0][1])] = tok
        for b in writes:
            b.w = tok
            b.r = {}
        return tok

    def emit(self, nc, final_wait_tokens=()):
        with contextlib.ExitStack() as st:
            sems = {}
            for key in sorted(self.eng_keys):
                sems[key] = st.enter_context(nc.semaphore("s_%s_%d" % (key[1], key[2])))
            for d in self.dma_sems:
                sems[d.key] = st.enter_context(nc.semaphore("d_" + str(d.key[1])))
            block = st.enter_context(nc.Block())
            q = self.q

            def replay(eng_obj, name, extra_waits=()):
                for waits, fn, inc in q[name]:
                    for key, val in waits:
                        eng_obj.wait_ge(sems[key], val)
                    ins = fn(eng_obj)
                    ins.then_inc(sems[inc[0]], inc[1])
                for key, val in extra_waits:
                    eng_obj.wait_ge(sems[key], val)

            fw = [(t[0], t[1]) for t in final_wait_tokens]

            @block.tensor
            def _(e):
                replay(e, "pe")

            @block.scalar
            def _(e):
                replay(e, "act")

            @block.vector
            def _(e):
                replay(e, "dve")

            @block.gpsimd
            def _(e):
                replay(e, "pool")

            @block.sync
            def _(e):
                replay(e, "sp", fw)


def rope_tables():
    grid_w, head_dim, theta = 64, 128, 10000.0
    rows = LL // grid_w
    row = np.repeat(np.arange(rows, dtype=np.float32), grid_w)
    col = np.tile(np.arange(grid_w, dtype=np.float32), rows)
    ppa = head_dim // 4
    freq = (np.float32(theta) ** (-np.arange(ppa, dtype=np.float32) / np.float32(ppa))).astype(np.float32)
    ang = np.concatenate([row[:, None] * freq, col[:, None] * freq], axis=-1).astype(np.float32)
    cos = np.cos(ang).astype(np.float32).T
    sin = np.sin(ang).astype(np.float32).T
    return np.ascontiguousarray(np.concatenate([cos, cos], 0)), np.ascontiguousarray(np.concatenate([sin, sin], 0))


def pk(v):
    v = np.asarray(v, np.float32)
    lead = v.shape[:-1]
    m = v.shape[-1] // 128
    r = v.reshape(lead + (m, 128))
    r = np.moveaxis(r, -1, 0)
    return np.ascontiguousarray(r)


def shared_inputs(inp):
    s = {}
    s["w_mod"] = np.ascontiguousarray(inp["w_mod"], np.float32)
    s["bmT"] = pk(inp["b_mod"])
    g = np.stack([inp["g_pre_mix"], inp["g_post_mix"], inp["g_pre_ffn"], inp["g_post_ffn"]], 1)
    s["gT"] = pk(g)
    wqkv = np.asarray(inp["w_qkv"][0], np.float32)
    s["wqk"] = np.ascontiguousarray(wqkv[:, :1280].reshape(8, 128, 10, 128).transpose(2, 1, 0, 3))
    s["wv"] = np.ascontiguousarray(wqkv[:, 1280:].reshape(8, 128, 256).transpose(1, 0, 2))
    wo = np.asarray(inp["w_o"][0], np.float32)
    s["wo"] = np.ascontiguousarray(wo.reshape(8, 128, 8, 128).transpose(2, 1, 0, 3))
    s["qkg"] = np.ascontiguousarray(np.stack([inp["q_norm_g"][0], inp["k_norm_g"][0]], 1), np.float32)
    cos, sin = rope_tables()
    s["cos"], s["sin"] = cos, sin
    rot = np.zeros((128, 128), np.float32)
    for m in range(64):
        rot[m + 64, m] = -1.0
        rot[m, m + 64] = 1.0
    s["rotm"] = rot
    s["ones"] = np.ones((128, 128), np.float32)
    wup = np.asarray(inp["w_up"], np.float32)
    wu = wup.reshape(2, 8, 128, 2, NFC, 128)
    s["wup"] = np.ascontiguousarray(wu.transpose(0, 4, 2, 1, 3, 5)).reshape(2, NFC, 128, 8, 256)
    wdn = np.asarray(inp["w_down"], np.float32)
    s["wdn"] = np.ascontiguousarray(wdn.reshape(2, NFC, 128, 8, 128).transpose(0, 3, 2, 1, 4))
    s["cw"] = pk(inp["conv_w"])
    s["cb"] = pk(inp["conv_b"])
    lre = np.asarray(inp["ssm_lambda_re"][0], np.float32)
    lim = np.asarray(inp["ssm_lambda_im"][0], np.float32)
    lst = np.asarray(inp["ssm_log_step"][0], np.float32)

    def en_pd(a):
        return a.reshape(2, 32, 2, 64).transpose(2, 3, 1, 0).reshape(128, 64)

    lam = np.stack([en_pd(lre), en_pd(lim), en_pd(np.broadcast_to(lst[:, :, None], (2, 64, 64)))], 1)
    s["lam"] = np.ascontiguousarray(lam, np.float32)
    Bf = np.zeros((4, 2, 16, 2, 32, 2, 2, 64), np.float32)
    Cf = np.zeros((2, 64, 2, 32, 2, 4, 2, 16), np.float32)
    for ri, (bsrc, csrc) in enumerate(((inp["ssm_b_re"], inp["ssm_c_re"]), (inp["ssm_b_im"], inp["ssm_c_im"]))):
        bsrc = np.asarray(bsrc[0], np.float32)
        csrc = np.asarray(csrc[0], np.float32)
        for p_ in range(32):
            r_ = p_ % 4
            for e_ in range(2):
                g_ = 2 * p_ + e_
                Bf[r_, e_, :, :, p_, ri, e_, :] = bsrc[:, g_, :, :].transpose(2, 0, 1)
                Cf[e_, :, :, p_, ri, r_, e_, :] = csrc[:, g_, :, :].transpose(2, 0, 1)
    s["Bf"] = np.ascontiguousarray(Bf.reshape(128, 2, 32, 2, 128))
    s["Cf"] = np.ascontiguousarray(Cf.reshape(128, 2, 32, 2, 128))
    s["dsk"] = pk(inp["ssm_d"][0])
    wg = np.stack([inp["w_glu_a"][0], inp["w_glu_b"][0]]).astype(np.float32)
    s["wglu"] = np.ascontiguousarray(wg.reshape(2, 8, 128, 8, 128).transpose(0, 3, 2, 1, 4))
    return s


def core_inputs(inp, core):
    b0 = 2 * core
    m = {}
    xs = []
    for b in (b0, b0 + 1):
        xs.append(np.concatenate([inp["ctx"][b], inp["x"][b]], 0).T)
    m["xT"] = np.ascontiguousarray(np.stack(xs), np.float32)
    c3 = np.stack([inp["c"][b0], inp["c"][b0 + 1], inp["c_ctx"]], -1)
    m["cT"] = np.ascontiguousarray(c3.reshape(8, 128, 3).transpose(1, 0, 2), np.float32)
    return m


NAR = 24576
NAF = 7744
TCH = 64
TWO_PI = 2.0 * math.pi
import os as _os
IMENG = _os.environ.get('IMENG', 'pool')


class K:
    pass


class XB:
    def __init__(self):
        self.b = [Buf(f"X{i}") for i in range(NT // 256)]

    def __call__(self, c0, c1):
        return self.b[c0 // 256:(c1 + 255) // 256]


class Arena:
    def __init__(self, t, n, dt):
        self.t, self.n, self.dt, self.o = t, n, dt, 0

    def reset(self):
        self.o = 0

    def take(self, shape):
        n = int(np.prod(shape))
        v = self.t[:, self.o:self.o + n]
        self.o += n
        assert self.o <= self.n, (self.o, self.n)
        if len(shape) == 2:
            v = v.rearrange("p (a b) -> p a b", b=shape[1])
        elif len(shape) == 3:
            v = v.rearrange("p (a b c) -> p a b c", b=shape[1], c=shape[2])
        elif len(shape) == 4:
            v = v.rearrange("p (a b c d) -> p a b c d", b=shape[1], c=shape[2], d=shape[3])
        return v


def build(stage=99, nb=2):
    nc = bass.Bass("TRN2", target_bir_lowering=False)
    S = Sched()
    k = K()
    k.nc, k.S = nc, S

    def din(name, shape, dt=F32):
        return nc.dram_tensor(name, list(shape), dt, kind="ExternalInput").ap()

    d = K()
    d.xT = din("xT", [2, D, NT])
    d.cT = din("cT", [128, 8, 3])
    d.w_mod = din("w_mod", [2, D, 6 * D])
    d.bmT = din("bmT", [128, 2, 48])
    d.gT = din("gT", [128, 2, 4, 8])
    d.wqk = din("wqk", [10, 128, 8, 128])
    d.wv = din("wv", [128, 8, 256])
    d.wo = din("wo", [8, 128, 8, 128])
    d.qkg = din("qkg", [128, 2])
    d.cos = din("cos", [128, LL])
    d.sin = din("sin", [128, LL])
    d.rotm = din("rotm", [128, 128])
    d.ones = din("ones", [128, 128])
    d.wup = din("wup", [2, NFC, 128, 8, 256])
    d.wdn = din("wdn", [2, 8, 128, NFC, 128])
    d.cw = din("cw", [128, 2, 3, 44])
    d.cb = din("cb", [128, 2, 44])
    d.lam = din("lam", [128, 3, 64])
    d.Bf = din("Bf", [128, 2, 32, 2, 128])
    d.Cf = din("Cf", [128, 2, 32, 2, 128])
    d.dsk = din("dsk", [128, 8])
    d.wglu = din("wglu", [2, 8, 128, 8, 128])
    d.out = nc.dram_tensor("outT", [2, D, LL], F32, kind="ExternalOutput").ap()
    if stage != 99:
        d.dbg = nc.dram_tensor("dbg", [128, 8, NT], F32, kind="ExternalOutput").ap()
    d.xsp = nc.dram_tensor("xspill", [128, 8, LL], F32, kind="Internal").ap()
    k.d = d

    with contextlib.ExitStack() as st:
        def sb(name, shape, dt=F32):
            return st.enter_context(nc.sbuf_tensor("sb_" + name, list(shape), dt))

        k.Xt = sb("X", [128, 8 * NT])
        k.ARt = sb("AR", [128, NAR], F32R)
        k.AFt = sb("AF", [128, NAF])
        k.AR = Arena(k.ARt, NAR, F32R)
        k.AF = Arena(k.AFt, NAF, F32)
        k.ones = sb("ones", [128, 128], F32R)
        k.rotm = sb("rotm", [128, 128], F32R)
        k.modv = sb("modv", [128, 2, 48, 3])
        k.gT = sb("gT", [128, 2, 4, 8])
        k.sA = sb("sA", [128, 2, 2, 3, 8])
        k.sG = sb("sG", [128, 2, 2, 3, 8])
        k.qkg = sb("qkg", [128, 2])
        k.cw = sb("cw", [128, 2, 3, 44])
        k.cb = sb("cb", [128, 2, 44])
        k.epsD = sb("epsD", [128, 1])
        k.eps128 = sb("eps128", [128, 1])
        k.bmT = sb("bmT", [128, 2, 48])
        k.dsk = sb("dsk", [128, 8])
        k.stab = sb("stab", [128, 6, 64])
        k.ps01 = st.enter_context(nc.psum_tensor("ps01", [128, 1024], F32))
        k.ps23 = st.enter_context(nc.psum_tensor("ps23", [128, 1024], F32))
        k.ps = [k.ps01[:, 0:512], k.ps01[:, 512:1024], k.ps23[:, 0:512], k.ps23[:, 512:1024]]
        k.ps += [st.enter_context(nc.psum_tensor(f"ps{i}", [128, 512], F32)) for i in range(4, 8)]
        k.bps = [Buf(f"ps{i}") for i in range(8)]
        k.bconst = Buf("const")
        k.dsem_c = DmaSem(S, "const")
        k.dsem_o = DmaSem(S, "out")
        k.dsem_x = DmaSem(S, "x")
        k.dsem_w = [DmaSem(S, f"w{i}") for i in range(8)]
        k.dsem_m = [DmaSem(S, f"m{i}") for i in range(2)]
        k.dsem_cs = DmaSem(S, "const_sw")

        prologue(k)
        fin = []
        for b in range(nb):
            fin += batch_element(k, b, stage)
        S.emit(nc, fin)
    k.ninstr = {e: len(q) for e, q in S.q.items()}
    print("instr counts", k.ninstr)
    return nc


def prologue(k):
    nc, S, d = k.nc, k.S, k.d
    bc = k.bconst
    dc = k.dsem_c
    S.op("pool", lambda e: e.dma_start(out=k.ones[:], in_=d.ones), writes=[bc], dsem=k.dsem_cs)
    S.op("pool", lambda e: e.dma_start(out=k.rotm[:], in_=d.rotm), writes=[bc], dsem=k.dsem_cs)
    for t, src in ((k.gT, d.gT), (k.qkg, d.qkg), (k.cw, d.cw), (k.cb, d.cb), (k.bmT, d.bmT), (k.dsk, d.dsk)):
        S.op("sp", (lambda e, t=t, src=src: e.dma_start(out=t[:], in_=src)), writes=[bc], dsem=dc)
    S.op("dve", lambda e: e.memset(k.epsD[:], EPS), writes=[bc])
    S.op("dve", lambda e: e.memset(k.eps128[:], EPS), writes=[bc])
    AFa = k.AF
    AFa.reset()
    cT = AFa.take([8, 3])
    sc = AFa.take([8, 3])
    bcT, bsc = Buf("cT"), Buf("sc")
    S.op("sp", lambda e: e.dma_start(out=cT, in_=d.cT), writes=[bcT], dsem=dc)
    S.op("act", lambda e: e.activation(out=sc, in_=cT, func=AF.Silu), reads=[bcT], writes=[bsc])
    import os
    PS = int(os.environ.get("PRO_STOP", "9"))
    if PS <= 1:
        S.barrier()
        return
    NP = 384
    wt = [(AFa.take([8, NP]), Buf(f"wm{i}")) for i in range(2)]
    bmod = Buf("modv")
    it = 0
    for l in range(2):
        for j in range(6 * D // NP):
            wv_, wb = wt[it % 2]
            ds = k.dsem_m[it % 2]
            src = d.w_mod[l, :, j * NP:(j + 1) * NP].rearrange("(kk p) n -> p kk n", p=128)
            S.op("sp", (lambda e, wv_=wv_, src=src: e.dma_start(out=wv_, in_=src)), writes=[wb], dsem=ds)
            for c in range(NP // 128):
                oc = j * (NP // 128) + c
                pst = k.ps[l][:, oc * 3:oc * 3 + 3]
                for kk in range(8):
                    S.op("pe", (lambda e, pst=pst, wv_=wv_, c=c, kk=kk: e.matmul(
                        pst, wv_[:, kk, c * 128:(c + 1) * 128], sc[:, kk, :], start=(kk == 0), stop=(kk == 7))),
                        reads=[wb, bsc], writes=[k.bps[l]])
            it += 1
        S.op("dve", (lambda e, l=l: e.tensor_tensor(
            out=k.modv[:, l, :, :], in0=k.ps[l][:, 0:144].rearrange("p (a b) -> p a b", b=3),
            in1=k.bmT[:, l, :].unsqueeze(2).broadcast_to([128, 48, 3]), op=ALU.add)),
            reads=[k.bps[l], bc], writes=[bmod])
    if PS <= 2:
        S.barrier()
        return
    for l in range(2):
        for j, (gi, si) in enumerate(((0, 1), (2, 4))):
            for r in range(3):
                S.op("dve", (lambda e, l=l, j=j, gi=gi, si=si, r=r: e.scalar_tensor_tensor(
                    out=k.sA[:, l, j, r, :], in0=k.modv[:, l, si * 8:(si + 1) * 8, r], scalar=1.0,
                    in1=k.gT[:, l, gi, :], op0=ALU.add, op1=ALU.mult)), reads=[bmod, bc], writes=[bc])
        for j, (gi, si) in enumerate(((1, 2), (3, 5))):
            for r in range(3):
                S.op("dve", (lambda e, l=l, j=j, gi=gi, si=si, r=r: e.tensor_tensor(
                    out=k.sG[:, l, j, r, :], in0=k.modv[:, l, si * 8:(si + 1) * 8, r],
                    in1=k.gT[:, l, gi, :], op=ALU.mult)), reads=[bmod, bc], writes=[bc])
    S.barrier()
    if PS <= 3:
        return
    s5_tables(k)
    S.barrier()


def s5_tables(k):
    S, d = k.S, k.d
    AFa = k.AF
    AFa.reset()
    lam = AFa.take([3, 64])
    w = [AFa.take([64]) for _ in range(10)]
    bt = Buf("s5t")
    S.op("sp", lambda e: e.dma_start(out=lam, in_=d.lam), writes=[bt], dsem=k.dsem_c)
    lre, lim, lst = lam[:, 0, :], lam[:, 1, :], lam[:, 2, :]
    dt_, xr, xi, mag, tq, tr, cs, sn, den, t2 = w
    st = k.stab

    def dve(fn):
        S.op("dve", fn, reads=[bt], writes=[bt])

    def act(fn):
        S.op("act", fn, reads=[bt], writes=[bt])

    act(lambda e: e.activation(out=dt_, in_=lst, func=AF.Exp))
    dve(lambda e: e.tensor_tensor(out=xr, in0=lre, in1=dt_, op=ALU.mult))
    dve(lambda e: e.tensor_tensor(out=xi, in0=lim, in1=dt_, op=ALU.mult))
    MAGIC = 12582912.0

    def sincos(mult, out_c, out_s):
        for off, dst in ((0.0, out_s), (math.pi / 2, out_c)):
            dve(lambda e, off=off: e.tensor_scalar(out=tq, in0=xi, scalar1=float(mult), scalar2=float(off),
                                                    op0=ALU.mult, op1=ALU.add))
            dve(lambda e: e.tensor_scalar(out=tr, in0=tq, scalar1=1.0 / TWO_PI, scalar2=MAGIC,
                                          op0=ALU.mult, op1=ALU.add))
            dve(lambda e: e.tensor_scalar(out=tr, in0=tr, scalar1=MAGIC, scalar2=None, op0=ALU.subtract))
            dve(lambda e: e.scalar_tensor_tensor(out=tq, in0=tr, scalar=-TWO_PI, in1=tq, op0=ALU.mult, op1=ALU.add))
            dve(lambda e: e.tensor_scalar(out=tq, in0=tq, scalar1=3.1415925, scalar2=-3.1415925,
                                          op0=ALU.min, op1=ALU.max))
            act(lambda e, dst=dst: e.activation(out=dst, in_=tq, func=AF.Sin))

    for mult, ia, ib in ((1.0, 0, 1), (float(TCH), 2, 3)):
        sincos(mult, cs, sn)
        act(lambda e, mult=mult: e.activation(out=mag, in_=xr, func=AF.Exp, scale=float(mult)))
        dve(lambda e, ia=ia: e.tensor_tensor(out=st[:, ia, :], in0=mag, in1=cs, op=ALU.mult))
        dve(lambda e, ib=ib: e.tensor_tensor(out=st[:, ib, :], in0=mag, in1=sn, op=ALU.mult))
    dve(lambda e: e.tensor_tensor(out=den, in0=lre, in1=lre, op=ALU.mult))
    dve(lambda e: e.tensor_tensor(out=t2, in0=lim, in1=lim, op=ALU.mult))
    dve(lambda e: e.tensor_tensor(out=den, in0=den, in1=t2, op=ALU.add))
    dve(lambda e: e.reciprocal(out=den, in_=den))
    dve(lambda e: e.tensor_scalar(out=tq, in0=st[:, 0, :], scalar1=-1.0, scalar2=None, op0=ALU.add))
    dve(lambda e: e.tensor_tensor(out=tr, in0=tq, in1=lre, op=ALU.mult))
    dve(lambda e: e.tensor_tensor(out=t2, in0=st[:, 1, :], in1=lim, op=ALU.mult))
    dve(lambda e: e.tensor_tensor(out=tr, in0=tr, in1=t2, op=ALU.add))
    dve(lambda e: e.tensor_tensor(out=st[:, 4, :], in0=tr, in1=den, op=ALU.mult))
    dve(lambda e: e.tensor_tensor(out=tr, in0=st[:, 1, :], in1=lre, op=ALU.mult))
    dve(lambda e: e.tensor_tensor(out=t2, in0=tq, in1=lim, op=ALU.mult))
    dve(lambda e: e.tensor_tensor(out=tr, in0=tr, in1=t2, op=ALU.subtract))
    dve(lambda e: e.tensor_tensor(out=st[:, 5, :], in0=tr, in1=den, op=ALU.mult))
    S.op("dve", lambda e: e.tensor_copy(out=cs, in_=st[:, 5, :]), reads=[bt], writes=[bt, k.bconst])


def segs_rows(c0, c1, b):
    out = []
    if c0 < LC:
        out.append((c0, min(c1, LC), 2))
    if c1 > LC:
        out.append((max(c0, LC), c1, b))
    return out


def norm_stats(k, src_fn, reads, W, scr):
    S = k.S
    pi = scr["ps"]
    pst = k.ps[pi][:, 0:W]
    for kk in range(8):
        sq, bsq = scr["sq"][kk % 2]
        S.op("act", (lambda e, sq=sq, kk=kk: e.activation(out=sq[:, 0:W], in_=src_fn(kk), func=AF.Square)),
             reads=reads, writes=[bsq])
        S.op("pe", (lambda e, sq=sq, kk=kk: e.matmul(pst, k.ones[:], sq[:, 0:W], start=(kk == 0), stop=(kk == 7))),
             reads=[bsq, k.bconst], writes=[k.bps[pi]])
    rstd, brs = scr["rstd"]
    S.op("act", lambda e: e.activation(out=rstd[:, 0:W], in_=pst, func=AF.Sqrt, scale=1.0 / D, bias=k.epsD[:]),
         reads=[k.bps[pi], k.bconst], writes=[brs])
    S.op("dve", lambda e: e.reciprocal(out=rstd[:, 0:W], in_=rstd[:, 0:W]), reads=[brs], writes=[brs])
    return rstd, brs


def modnorm(k, X, bX, c0, c1, l, j, b, hout, bh, scr, ho=0):
    S = k.S
    W = c1 - c0
    Ws = W + (W % 2)
    assert c0 + Ws <= NT
    rstd, brs = norm_stats(k, lambda kk: X[:, kk, c0:c0 + Ws], bX(c0, c0 + Ws), Ws, scr)
    shift_i = 0 if j == 0 else 3
    for kk in range(8):
        tmp, btmp = scr["tmp"][kk % 2]
        S.op("dve", (lambda e, tmp=tmp, kk=kk: e.tensor_tensor(out=tmp[:, 0:W], in0=X[:, kk, c0:c1], in1=rstd[:, 0:W],
                                                               op=ALU.mult)), reads=bX(c0, c1) + [brs], writes=[btmp])
        for (lo, hi, r) in segs_rows(c0, c1, b):
            S.op("act", (lambda e, tmp=tmp, kk=kk, lo=lo, hi=hi, r=r: e.activation(
                out=hout[:, kk, ho + lo - c0:ho + hi - c0], in_=tmp[:, lo - c0:hi - c0], func=AF.Identity,
                scale=k.sA[:, l, j, r, kk:kk + 1], bias=k.modv[:, l, shift_i * 8 + kk, r:r + 1])),
                reads=[btmp, k.bconst], writes=(bh if isinstance(bh, list) else [bh]))


def postnorm_update(k, X, bX, c0, c1, l, j, b, ost, bost, scr):
    S = k.S
    W = c1 - c0
    rstd, brs = norm_stats(k, lambda kk: ost[:, kk, 0:W], [bost], W, scr)
    for kk in range(8):
        tmp, btmp = scr["tmp"][kk % 2]
        S.op("dve", (lambda e, tmp=tmp, kk=kk: e.tensor_tensor(out=tmp[:, 0:W], in0=ost[:, kk, 0:W], in1=rstd[:, 0:W],
                                                               op=ALU.mult)), reads=[bost, brs], writes=[btmp])
        for (lo, hi, r) in segs_rows(c0, c1, b):
            S.op("dve", (lambda e, tmp=tmp, kk=kk, lo=lo, hi=hi, r=r: e.scalar_tensor_tensor(
                out=X[:, kk, lo:hi], in0=tmp[:, lo - c0:hi - c0], scalar=k.sG[:, l, j, r, kk:kk + 1],
                in1=X[:, kk, lo:hi], op0=ALU.mult, op1=ALU.add)), reads=[btmp, k.bconst] + bX(lo, hi), writes=bX(lo, hi))


def dump_x(k, X, bX, fin):
    S, d = k.S, k.d
    for kk in range(8):
        fin.append(S.op("sp", (lambda e, kk=kk: e.dma_start(out=d.dbg[:, kk, :], in_=X[:, kk, :])),
                        reads=bX(0, NT), dsem=k.dsem_o))


def batch_element(k, b, stage):
    nc, S, d = k.nc, k.S, k.d
    X = k.Xt[:, :].rearrange("p (a t) -> p a t", t=NT)
    bX = XB()
    for kk in range(8):
        S.op("sp", (lambda e, kk=kk: e.dma_start(out=X[:, kk, :], in_=d.xT[b, kk * 128:(kk + 1) * 128, :])),
             writes=bX(0, NT), dsem=k.dsem_x)
    fin = []
    if stage == 0:
        dump_x(k, X, bX, fin)
        S.barrier()
        return fin
    import os
    if os.environ.get("S5_ONLY") == "1":
        s5_layer(k, b, X, bX)
        S.barrier()
        dump_x(k, X, bX, fin)
        S.barrier()
        return fin
    attention_layer(k, b, X, bX)
    S.barrier()
    if stage == 1:
        if b == 0:
            dump_x(k, X, bX, fin)
        S.barrier()
        return fin
    ffn_layer(k, b, X, bX, 0, 0, NT)
    S.barrier()
    if stage == 2:
        if b == 0:
            dump_x(k, X, bX, fin)
        S.barrier()
        return fin
    s5_layer(k, b, X, bX)
    S.barrier()
    if stage == 3:
        if b == 0:
            dump_x(k, X, bX, fin)
        S.barrier()
        return fin
    ffn_layer(k, b, X, bX, 1, LC, NT)
    S.barrier()
    for kk in range(8):
        fin.append(S.op("sp", (lambda e, kk=kk: e.dma_start(out=d.out[b, kk * 128:(kk + 1) * 128, :], in_=X[:, kk, LC:NT])),
                        reads=bX(LC, NT), dsem=k.dsem_o))
    S.barrier()
    return fin


def make_scr(k, W, ps):
    AR, AFa = k.AR, k.AF
    scr = {"ps": ps, "sq": [], "tmp": []}
    for i in range(2):
        scr["sq"].append((AR.take([W]), Buf(f"sq{i}")))
    for i in range(2):
        scr["tmp"].append((AFa.take([W]), Buf(f"tmp{i}")))
    scr["rstd"] = (AFa.take([W]), Buf("rstd"))
    return scr


def attention_layer(k, b, X, bX):
    nc, S, d = k.nc, k.S, k.d
    l = 0
    WQ = 256
    AR, AFa = k.AR, k.AF
    AR.reset()
    AFa.reset()
    KT = AR.take([2, NT])
    V = AR.take([18, 256])
    AT = AR.take([8, WQ])
    HW = AR.take([8, WQ])
    QT = AR.take([8, WQ])
    bKT, bV, bAT, bHW, bQT = Buf("KT"), Buf("V"), Buf("AT"), Buf("HW"), Buf("QT")
    wq = [(AR.take([8, 128]), Buf(f"wq{i}"), k.dsem_w[i]) for i in range(2)]
    wo = [(AR.take([8, 128]), Buf(f"wo{i}"), k.dsem_w[2 + i]) for i in range(2)]
    WV = AR.take([8, 256]); bWV = Buf("WV")
    PT = [(AR.take([WQ]), Buf(f"PT{i}")) for i in range(4)]
    KN = AR.take([WQ]); bKN = Buf("KN")
    SQ2 = AR.take([WQ]); bSQ2 = Buf("SQ2")
    COS = AFa.take([LL])
    SIN = AFa.take([LL])
    OST = AFa.take([8, WQ]); bOST = Buf("OST")
    RS2 = AFa.take([WQ]); bRS2 = Buf("RS2")
    T2 = AFa.take([WQ]); bT2 = Buf("T2")
    RINV = AFa.take([WQ]); bRINV = Buf("RINV")
    scr = make_scr(k, WQ, 7)
    brope = Buf("rope")
    S.op("sp", lambda e: e.dma_start(out=COS, in_=d.cos), writes=[brope], dsem=k.dsem_c)
    S.op("sp", lambda e: e.dma_start(out=SIN, in_=d.sin), writes=[brope], dsem=k.dsem_c)
    S.op("pool", lambda e: e.dma_start(out=WV, in_=d.wv), writes=[bWV], dsem=k.dsem_w[4])
    bS = {(4, 0): Buf("S40"), (4, 1): Buf("S41"), (5, 0): Buf("S50"), (5, 1): Buf("S51")}
    wq_it = [0]

    def load_wq(ci):
        v, bw, ds = wq[wq_it[0] % 2]
        wq_it[0] += 1
        S.op("pool", (lambda e, v=v, ci=ci: e.dma_start(out=v, in_=d.wqk[ci])), writes=[bw], dsem=ds)
        return v, bw

    def qk_head(ci, gcol, c0, lat, dst, bdst):
        wv_, bw = load_wq(ci)
        pi = 6
        pst = k.ps[pi][:, 0:WQ]
        for kk in range(8):
            S.op("pe", (lambda e, kk=kk: e.matmul(pst, wv_[:, kk, :], HW[:, kk, :], start=(kk == 0), stop=(kk == 7))),
                 reads=[bw, bHW], writes=[k.bps[pi]])
        S.op("act", lambda e: e.activation(out=SQ2, in_=pst, func=AF.Square), reads=[k.bps[pi]], writes=[bSQ2])
        p2 = k.ps[pi][:, 256:256 + WQ]
        S.op("pe", lambda e: e.matmul(p2, k.ones[:], SQ2, start=True, stop=True), reads=[bSQ2, k.bconst],
             writes=[k.bps[pi]])
        S.op("act", lambda e: e.activation(out=RS2, in_=p2, func=AF.Sqrt, scale=1.0 / 128, bias=k.eps128[:]),
             reads=[k.bps[pi], k.bconst], writes=[bRS2])
        S.op("dve", lambda e: e.reciprocal(out=RS2, in_=RS2), reads=[bRS2], writes=[bRS2])
        if not lat:
            S.op("dve", lambda e: e.scalar_tensor_tensor(out=dst, in0=pst, scalar=k.qkg[:, gcol:gcol + 1], in1=RS2,
                                                         op0=ALU.mult, op1=ALU.mult),
                 reads=[k.bps[pi], bRS2, k.bconst], writes=[bdst])
            return
        S.op("dve", lambda e: e.scalar_tensor_tensor(out=KN, in0=pst, scalar=k.qkg[:, gcol:gcol + 1], in1=RS2,
                                                     op0=ALU.mult, op1=ALU.mult),
             reads=[k.bps[pi], bRS2, k.bconst], writes=[bKN])
        S.op("pe", lambda e: e.matmul(p2, k.rotm[:], KN, start=True, stop=True), reads=[bKN, k.bconst, bRS2],
             writes=[k.bps[pi]])
        t0 = c0 - LC
        S.op("dve", lambda e: e.tensor_tensor(out=T2, in0=p2, in1=SIN[:, t0:t0 + WQ], op=ALU.mult),
             reads=[k.bps[pi], brope], writes=[bT2])
        S.op("dve", lambda e: e.tensor_tensor(out=RS2, in0=KN.bitcast(F32), in1=COS[:, t0:t0 + WQ], op=ALU.mult),
             reads=[bKN, brope], writes=[bRS2])
        S.op("dve", lambda e: e.tensor_tensor(out=dst, in0=RS2, in1=T2, op=ALU.add), reads=[bRS2, bT2], writes=[bdst])

    for w in range(NT // WQ):
        c0 = w * WQ
        lat = c0 >= LC
        modnorm(k, X, bX, c0, c0 + WQ, l, 0, b, HW, bHW, scr)
        for jk in range(2):
            qk_head(8 + jk, 1, c0, lat, KT[:, jk, c0:c0 + WQ], bKT)
        for sub in range(2):
            pi = 5
            pst = k.ps[pi][:, sub * 256:(sub + 1) * 256]
            for kk in range(8):
                S.op("pe", (lambda e, kk=kk, sub=sub, pst=pst: e.matmul(
                    pst, HW[:, kk, sub * 128:(sub + 1) * 128], WV[:, kk, :], start=(kk == 0), stop=(kk == 7))),
                    reads=[bHW, bWV], writes=[k.bps[pi]])
            S.op("act", (lambda e, sub=sub, pst=pst, w=w: e.activation(out=V[:, 2 * w + sub, :], in_=pst, func=AF.Identity)),
                 reads=[k.bps[pi]], writes=[bV])
    scale = 1.0 / math.sqrt(128.0)
    wo_it = [0]
    pt_it = [0]
    for w in range(NT // WQ):
        c0 = w * WQ
        lat = c0 >= LC
        nkt = 18 if lat else 2
        modnorm(k, X, bX, c0, c0 + WQ, l, 0, b, HW, bHW, scr)
        for h in range(8):
            qk_head(h, 0, c0, lat, QT[:, h, :], bQT)
        for h in range(8):
            jk = h // 4
            po, pr = (0, 1) if h % 2 == 0 else (2, 3)
            for kt in range(nkt):
                sp_i = 4 if (kt % 2 == 0) else 5
                half = (kt // 2) % 2
                pss = k.ps[sp_i][:, half * 256:half * 256 + WQ]
                S.op("pe", (lambda e, pss=pss, jk=jk, kt=kt, h=h: e.matmul(
                    pss, KT[:, jk, kt * 128:(kt + 1) * 128], QT[:, h, :], start=True, stop=True)),
                    reads=[bKT, bQT], writes=[bS[(sp_i, half)]])
                pt, bpt = PT[pt_it[0] % 4]
                pt_it[0] += 1
                S.op("act", (lambda e, pt=pt, pss=pss: e.activation(out=pt, in_=pss, func=AF.Exp, scale=scale)),
                     reads=[bS[(sp_i, half)]], writes=[bpt])
                S.op("pe", (lambda e, pt=pt, jk=jk, kt=kt, po=po, nkt=nkt: e.matmul(
                    k.ps[po][:, 0:WQ], V[:, kt, jk * 128:(jk + 1) * 128], pt, start=(kt == 0), stop=(kt == nkt - 1))),
                    reads=[bV, bpt], writes=[k.bps[po]])
                S.op("pe", (lambda e, pt=pt, kt=kt, pr=pr, nkt=nkt: e.matmul(
                    k.ps[pr][:, 0:WQ], k.ones[:], pt, start=(kt == 0), stop=(kt == nkt - 1))),
                    reads=[bpt, k.bconst], writes=[k.bps[pr]])
            S.op("dve", (lambda e, pr=pr: e.reciprocal(out=RINV, in_=k.ps[pr][:, 0:WQ])), reads=[k.bps[pr]], writes=[bRINV])
            S.op("dve", (lambda e, po=po, h=h: e.tensor_tensor(out=AT[:, h, :], in0=k.ps[po][:, 0:WQ], in1=RINV, op=ALU.mult)),
                 reads=[k.bps[po], bRINV], writes=[bAT])
        for dcn in range(8):
            v, bw, ds = wo[wo_it[0] % 2]
            wo_it[0] += 1
            S.op("pool", (lambda e, v=v, dcn=dcn: e.dma_start(out=v, in_=d.wo[dcn])), writes=[bw], dsem=ds)
            pi = 6
            pst = k.ps[pi][:, 0:WQ]
            for h in range(8):
                S.op("pe", (lambda e, v=v, h=h, pst=pst: e.matmul(pst, v[:, h, :], AT[:, h, :], start=(h == 0), stop=(h == 7))),
                     reads=[bw, bAT], writes=[k.bps[pi]])
            S.op("act", (lambda e, dcn=dcn, pst=pst: e.activation(out=OST[:, dcn, :], in_=pst, func=AF.Identity)),
                 reads=[k.bps[pi]], writes=[bOST])
        postnorm_update(k, X, bX, c0, c0 + WQ, l, 0, b, OST, bOST, scr)


def excl_segs(lo, hi, bad):
    if hi <= lo:
        return []
    if bad is None or bad < lo or bad >= hi:
        return [(lo, hi)]
    out = []
    if bad > lo:
        out.append((lo, bad))
    if bad + 1 < hi:
        out.append((bad + 1, hi))
    return out


def ffn_windows(cs, ce):
    n = ce - cs
    if n == NT:
        sizes = [460, 462, 460, 462, 460]
    else:
        sizes = [410, 410, 408, 410, 410]
    assert sum(sizes) == n
    out = []
    s = cs
    for w in sizes:
        out.append((s, s + w))
        s += w
    return out


def ffn_layer(k, b, X, bX, l, cs, ce):
    nc, S, d = k.nc, k.S, k.d
    AR, AFa = k.AR, k.AF
    AR.reset()
    AFa.reset()
    WM = 464
    HW = AR.take([8, WM]); bHW = Buf("HWf")
    G = AR.take([NFC, 462]); bG = [Buf(f"G{i}") for i in range(NFC)]
    NWU, NWD = 4, 3
    WB = AR.take([NWD * NFC * 128])
    bwu = [Buf(f"wu{i}") for i in range(NWU)]
    bwd = [Buf(f"wd{i}") for i in range(NWD)]
    wu = [(WB[:, i * 2048:(i + 1) * 2048].rearrange("p (a b) -> p a b", b=256), bwu[i], k.dsem_w[i]) for i in range(NWU)]
    wd = [(WB[:, i * 2816:(i + 1) * 2816].rearrange("p (a b) -> p a b", b=128), bwd[i], k.dsem_w[NWU + i]) for i in range(NWD)]

    def ovl(a0, a1, b0, b1):
        return a0 < b1 and b0 < a1
    wu_ov = [[bwd[j] for j in range(NWD) if ovl(i * 2048, (i + 1) * 2048, j * 2816, (j + 1) * 2816)] for i in range(NWU)]
    wd_ov = [[bwu[i] for i in range(NWU) if ovl(i * 2048, (i + 1) * 2048, j * 2816, (j + 1) * 2816)] for j in range(NWD)]
    HALO = AR.take([8, 2]); bHALO = Buf("halo")
    OST = AFa.take([8, 462]); bOST = Buf("OSTf")
    yv = [(AFa.take([WM]), Buf(f"yv{i}")) for i in range(2)]
    yg = [(AFa.take([WM]), Buf(f"yg{i}")) for i in range(2)]
    scr = make_scr(k, WM, 7)
    wins = ffn_windows(cs, ce)
    loads = []
    for wi_ in range(len(wins)):
        loads += [("u", fp_) for fp_ in range(NFC)] + [("d", dc_) for dc_ in range(8)]
    issued = [0]
    slots = {}
    cnt = {"u": 0, "d": 0}
    occ = {"u": [-1] * NWU, "d": [-1] * NWD}
    cons = [0]
    ovi_u = [[j for j in range(NWD) if ovl(i * 2048, (i + 1) * 2048, j * 2816, (j + 1) * 2816)] for i in range(NWU)]
    ovi_d = [[i for i in range(NWU) if ovl(i * 2048, (i + 1) * 2048, j * 2816, (j + 1) * 2816)] for j in range(NWD)]

    def issue(upto):
        while issued[0] < min(upto, len(loads)):
            kind, idx = loads[issued[0]]
            if kind == "u":
                i = cnt["u"] % NWU
                conf = [occ["u"][i]] + [occ["d"][j] for j in ovi_u[i]]
            else:
                i = cnt["d"] % NWD
                conf = [occ["d"][i]] + [occ["u"][j] for j in ovi_d[i]]
            if any(c_ >= cons[0] for c_ in conf):
                break
            cnt[kind] += 1
            occ[kind][i] = issued[0]
            if kind == "u":
                wv_, bw, ds = wu[i]
                S.op("pool", (lambda e, wv_=wv_, idx=idx: e.dma_start(out=wv_, in_=d.wup[l, idx])), writes=[bw] + wu_ov[i], dsem=ds)
            else:
                wv_, bw, ds = wd[i]
                S.op("pool", (lambda e, wv_=wv_, idx=idx: e.dma_start(out=wv_, in_=d.wdn[l, idx])), writes=[bw] + wd_ov[i], dsem=ds)
            slots[issued[0]] = (wv_, bw)
            issued[0] += 1
    PF = 4

    def next_w():
        issue(cons[0] + 1 + PF)
        assert cons[0] in slots, (cons[0], issued[0])
        r = slots.pop(cons[0])
        cons[0] += 1
        return r
    issue(PF)
    for wi, (s, e_) in enumerate(wins):
        W = e_ - s
        ms, me = max(s - 1, cs), min(e_ + 1, ce)
        if (me - ms) % 2 == 1:
            if me < ce:
                me += 1
            else:
                ms -= 1
        Wm = me - ms
        nleft = s - ms
        modnorm(k, X, bX, s, me, l, 1, b, HW, bHW, scr, ho=nleft)
        if nleft > 0:
            S.op("dve", (lambda e, nleft=nleft, s=s, e_=e_, ms=ms, W=W, Wm=Wm: e.tensor_copy(out=HW[:, :, 0:nleft], in_=HALO[:, :, 2 - nleft:2])),
                 reads=[bHALO], writes=[bHW])
        S.op("dve", (lambda e, s=s, e_=e_, ms=ms: e.tensor_copy(out=HALO[:, :, :], in_=HW[:, :, e_ - 2 - ms:e_ - ms])),
             reads=[bHW], writes=[bHALO])
        badl = LC if l == 0 else None
        badr = LC - 1 if l == 0 else None
        for fp in range(NFC):
            wv_, bw = next_w()
            par = fp % 2
            pv, pg = (0, 1) if par == 0 else (2, 3)
            for (pi, co) in ((pv, 0), (pg, 128)):
                for kk in range(8):
                    S.op("pe", (lambda e, pi=pi, co=co, kk=kk, wv_=wv_, s=s, e_=e_, ms=ms, W=W, Wm=Wm: e.matmul(
                        k.ps[pi][:, 0:Wm], wv_[:, kk, co:co + 128], HW[:, kk, 0:Wm], start=(kk == 0), stop=(kk == 7))),
                        reads=[bw, bHW], writes=[k.bps[pi]])
            for (pi, ci, (yt, byt)) in ((pv, fp, yv[par]), (pg, NFC + fp, yg[par])):
                psr = k.ps[pi]
                S.op("act", (lambda e, psr=psr, ci=ci, yt=yt, s=s, e_=e_, ms=ms, W=W, Wm=Wm: e.activation(
                    out=yt[:, 0:W], in_=psr[:, s - ms:e_ - ms], func=AF.Identity,
                    scale=k.cw[:, l, 1, ci:ci + 1], bias=k.cb[:, l, ci:ci + 1])),
                    reads=[k.bps[pi], k.bconst], writes=[byt])
                for (lo, hi) in excl_segs(max(s, cs + 1), e_, badl):
                    S.op("dve", (lambda e, psr=psr, ci=ci, yt=yt, lo=lo, hi=hi, s=s, e_=e_, ms=ms, W=W, Wm=Wm: e.scalar_tensor_tensor(
                        out=yt[:, lo - s:hi - s], in0=psr[:, lo - 1 - ms:hi - 1 - ms], scalar=k.cw[:, l, 0, ci:ci + 1],
                        in1=yt[:, lo - s:hi - s], op0=ALU.mult, op1=ALU.add)),
                        reads=[k.bps[pi], k.bconst, byt], writes=[byt])
                for (lo, hi) in excl_segs(s, min(e_, ce - 1), badr):
                    S.op("dve", (lambda e, psr=psr, ci=ci, yt=yt, lo=lo, hi=hi, s=s, e_=e_, ms=ms, W=W, Wm=Wm: e.scalar_tensor_tensor(
                        out=yt[:, lo - s:hi - s], in0=psr[:, lo + 1 - ms:hi + 1 - ms], scalar=k.cw[:, l, 2, ci:ci + 1],
                        in1=yt[:, lo - s:hi - s], op0=ALU.mult, op1=ALU.add)),
                        reads=[k.bps[pi], k.bconst, byt], writes=[byt])
            ygt, bygt = yg[par]
            yvt, byvt = yv[par]
            S.op("act", (lambda e, ygt=ygt, s=s, e_=e_, ms=ms, W=W, Wm=Wm: e.activation(out=ygt[:, 0:W], in_=ygt[:, 0:W], func=AF.Silu)),
                 reads=[bygt], writes=[bygt])
            S.op("pool", (lambda e, ygt=ygt, yvt=yvt, fp=fp, s=s, e_=e_, ms=ms, W=W, Wm=Wm: e.tensor_tensor(out=G[:, fp, 0:W], in0=yvt[:, 0:W], in1=ygt[:, 0:W],
                                                                                op=ALU.mult)),
                 reads=[bygt, byvt], writes=[bG[fp]])
        for dcn in range(8):
            wv_, bw = next_w()
            pi = 4 + dcn % 2
            for fc in range(NFC):
                S.op("pe", (lambda e, pi=pi, fc=fc, wv_=wv_, s=s, e_=e_, ms=ms, W=W, Wm=Wm: e.matmul(
                    k.ps[pi][:, 0:W], wv_[:, fc, :], G[:, fc, 0:W], start=(fc == 0), stop=(fc == NFC - 1))),
                    reads=[bw, bG[fc]], writes=[k.bps[pi]])
            S.op("act", (lambda e, pi=pi, dcn=dcn, s=s, e_=e_, ms=ms, W=W, Wm=Wm: e.activation(out=OST[:, dcn, 0:W], in_=k.ps[pi][:, 0:W], func=AF.Identity)),
                 reads=[k.bps[pi]], writes=[bOST])
        postnorm_update(k, X, bX, s, e_, l, 1, b, OST, bOST, scr)


def s5_layer(k, b, X, bX):
    nc, S, d = k.nc, k.S, k.d
    l = 1
    AR, AFa = k.AR, k.AF
    AR.reset()
    AFa.reset()
    st = k.stab
    bsp = Buf("xsp")
    for kk in range(8):
        S.op("sp", (lambda e, kk=kk: e.dma_start(out=d.xsp[:, kk, :], in_=X[:, kk, LC:NT])),
             reads=bX(LC, NT), writes=[bsp], dsem=k.dsem_x)
    Y = AR.take([8, LL]); bY = Buf("Y")
    o_ar = AR.o
    scr = make_scr(k, 256, 7)
    for w in range(NT // 256):
        c0 = w * 256
        modnorm(k, X, bX, c0, c0 + 256, l, 0, b, X, bX(c0, c0 + 256), scr, ho=c0)
    S.barrier()
    AR.o = o_ar
    AFa.reset()
    QN, SG = 4, 4
    NG = TCH // SG
    NS = 2
    bU = bX(0, NT)
    for kk in range(8):
        S.op("dve", (lambda e, kk=kk: e.tensor_scalar(out=Y[:, kk, :], in0=X[:, kk, LC:NT], scalar1=k.dsk[:, kk:kk + 1], scalar2=None,
                                                     op0=ALU.mult)), reads=bU + [k.bconst], writes=[bY])

    class Stream:
        pass

    sts = []
    for dd in range(NS):
        t = Stream()
        t.dd = dd
        t.BpT = AR.take([QN, 2, 128]); t.bBp = Buf(f"Bp{dd}")
        t.Cp = AR.take([QN, 2, 128]); t.bCp = Buf(f"Cp{dd}")
        t.BpF, t.CpF = t.BpT.bitcast(F32), t.Cp.bitcast(F32)
        t.XS = [AR.take([2 * QN * SG * 32]) for _ in range(2)]
        t.bXS = [Buf(f"XS{dd}0"), Buf(f"XS{dd}1")]
        t.XSF = [x.bitcast(F32) for x in t.XS]
        t.HsR = [AFa.take([SG + 1, QN, 32]) for _ in range(2)]
        t.HsI = [AFa.take([SG + 1, QN, 32]) for _ in range(2)]
        t.t2 = AFa.take([QN, 32]); t.bt2 = Buf(f"t2{dd}")
        t.t4 = AFa.take([QN, 32]); t.bt4 = Buf(f"t4{dd}")
        t.bHs = [Buf(f"Hs{dd}0"), Buf(f"Hs{dd}1")]
        t.HinR = AFa.take([QN, 32]); t.HinI = AFa.take([QN, 32]); t.bHin = Buf(f"Hin{dd}")
        t.h0R = AFa.take([QN]); t.h0I = AFa.take([QN]); t.bh0 = Buf(f"h0{dd}")
        t.hAR = AFa.take([QN]); t.hAI = AFa.take([QN]); t.bhA = Buf(f"hA{dd}")
        t.sm = [AFa.take([QN]) for _ in range(4)]; t.bsm = [Buf(f"sm{dd}{i}") for i in range(4)]
        t.XP = [k.ps[2 * dd], k.ps[2 * dd + 1]]
        t.bXP = [Buf(f"XPre{dd}"), Buf(f"XPim{dd}")]
        t.yps_i = 4 + dd
        t.bYps = Buf(f"Yps{dd}")
        t.gcount = 0
        t.islot = 0 if dd == 0 else SG
        t.eslot = SG if dd == 0 else 0
        t.lo_s = 1 if dd == 0 else 0
        sts.append(t)
    tC = AFa.take([QN, 128]); btC = Buf("tC")
    tD = AFa.take([QN, 128]); btD = Buf("tD")

    def cstep_small(t, oR, oI, hR, hI, eR, eI, reads, writes):
        sm, bsm, aT, bT = t.sm, t.bsm, t.aT16, t.bT16
        S.op("dve", lambda e: e.tensor_tensor(out=sm[0], in0=aT, in1=hR, op=ALU.mult), reads=reads, writes=[bsm[0]])
        S.op("dve", lambda e: e.tensor_tensor(out=sm[1], in0=bT, in1=hI, op=ALU.mult), reads=reads, writes=[bsm[1]])
        S.op("pool", lambda e: e.tensor_tensor(out=sm[2], in0=bT, in1=hR, op=ALU.mult), reads=reads, writes=[bsm[2]])
        S.op("pool", lambda e: e.tensor_tensor(out=sm[3], in0=aT, in1=hI, op=ALU.mult), reads=reads, writes=[bsm[3]])
        S.op("dve", lambda e: e.tensor_tensor(out=sm[0], in0=sm[0], in1=sm[1], op=ALU.subtract), reads=[bsm[0], bsm[1]], writes=[bsm[0]])
        S.op("pool", lambda e: e.tensor_tensor(out=sm[2], in0=sm[2], in1=sm[3], op=ALU.add), reads=[bsm[2], bsm[3]], writes=[bsm[2]])
        S.op("dve", lambda e: e.tensor_tensor(out=oR, in0=sm[0], in1=eR, op=ALU.add), reads=[bsm[0]] + reads, writes=writes)
        S.op("pool", lambda e: e.tensor_tensor(out=oI, in0=sm[2], in1=eI, op=ALU.add), reads=[bsm[2]] + reads, writes=writes)

    def sweep(dc, cb0, nchk, use_init, contract):
        Xl = X[:, dc, cb0:cb0 + TCH * nchk].rearrange("p (c t) -> p c t", t=TCH)
        Yl = Y[:, dc, :].rearrange("p (c t) -> p c t", t=TCH)

        def xsv(tt):
            return tt[:, 0:2 * QN * SG * nchk].rearrange("p (r q s c) -> p r q s c", r=2, q=QN, s=SG)

        for t in sts:
            hb = t.gcount % 2
            if not use_init:
                S.op("dve", (lambda e, t=t, hb=hb: e.memset(t.HsR[hb][:, t.islot, :, 0:nchk], 0.0)), writes=[t.bHs[hb]])
                S.op("pool", (lambda e, t=t, hb=hb: e.memset(t.HsI[hb][:, t.islot, :, 0:nchk], 0.0)), writes=[t.bHs[hb]])
            else:
                S.op("dve", (lambda e, t=t, hb=hb: e.tensor_copy(out=t.HsR[hb][:, t.islot, :, 0:nchk], in_=t.HinR[:, :, 0:nchk])),
                     reads=[t.bHin], writes=[t.bHs[hb]])
                S.op("pool", (lambda e, t=t, hb=hb: e.tensor_copy(out=t.HsI[hb][:, t.islot, :, 0:nchk], in_=t.HinI[:, :, 0:nchk])),
                     reads=[t.bHin], writes=[t.bHs[hb]])
        for g in range(NG):
            ctx_ = []
            for t in sts:
                hb = t.gcount % 2
                xb = t.gcount % 2
                t.gcount += 1
                j_lo = g * SG if t.dd == 0 else TCH - (g + 1) * SG
                rhs = Xl[:, :, j_lo:j_lo + SG].transpose([0, 2, 1])
                for ri in range(2):
                    xp = t.XP[ri][:, 0:QN * SG * nchk].rearrange("p (q s c) -> p q s c", s=SG, c=nchk)
                    for q in range(QN):
                        S.op("pe", (lambda e, t=t, xp=xp, q=q, ri=ri, rhs=rhs: e.matmul(
                            xp[:, q, :, :], t.BpF[:, q, ri, :], rhs, start=True, stop=True)),
                            reads=[t.bBp] + bU, writes=[t.bXP[ri]])
                    S.op("act", (lambda e, t=t, xp=xp, ri=ri, xb=xb: e.activation(
                        out=xsv(t.XS[xb])[:, ri, :, :, :], in_=xp[:, :, :, :], func=AF.Identity)),
                        reads=[t.bXP[ri]], writes=[t.bXS[xb]])
                ctx_.append((t, hb, xb, j_lo))
            for s_ in range(SG):
                ops = []
                for (t, hb, xb, j_lo) in ctx_:
                    rs, ws = (s_, s_ + 1) if t.dd == 0 else (SG - s_, SG - 1 - s_)
                    xs = s_ if t.dd == 0 else SG - 1 - s_
                    o = Stream()
                    o.t, o.hb, o.xb = t, hb, xb
                    o.a_bc = t.a16.unsqueeze(2).broadcast_to([128, QN, nchk])
                    o.b_bc = t.b16.unsqueeze(2).broadcast_to([128, QN, nchk])
                    o.hr, o.hi = t.HsR[hb][:, rs, :, 0:nchk], t.HsI[hb][:, rs, :, 0:nchk]
                    o.nr, o.ni = t.HsR[hb][:, ws, :, 0:nchk], t.HsI[hb][:, ws, :, 0:nchk]
                    o.xre, o.xim = xsv(t.XSF[xb])[:, 0, :, xs, :], xsv(t.XSF[xb])[:, 1, :, xs, :]
                    o.rd = [t.bHs[hb], k.bconst]
                    ops.append(o)
                for o in ops:
                    S.op("dve", (lambda e, o=o: e.tensor_tensor(out=o.nr, in0=o.a_bc, in1=o.hr, op=ALU.mult)), reads=o.rd, writes=[o.t.bHs[o.hb]])
                    S.op("dve", (lambda e, o=o: e.tensor_tensor(out=o.t.t2[:, :, 0:nchk], in0=o.b_bc, in1=o.hi, op=ALU.mult)), reads=o.rd, writes=[o.t.bt2])
                    S.op("pool", (lambda e, o=o: e.tensor_tensor(out=o.ni, in0=o.b_bc, in1=o.hr, op=ALU.mult)), reads=o.rd, writes=[o.t.bHs[o.hb]])
                    S.op("pool", (lambda e, o=o: e.tensor_tensor(out=o.t.t4[:, :, 0:nchk], in0=o.a_bc, in1=o.hi, op=ALU.mult)), reads=o.rd, writes=[o.t.bt4])
                for o in ops:
                    S.op("dve", (lambda e, o=o: e.tensor_tensor(out=o.nr, in0=o.nr, in1=o.t.t2[:, :, 0:nchk], op=ALU.subtract)),
                         reads=[o.t.bt2, o.t.bHs[o.hb]], writes=[o.t.bHs[o.hb]])
                    S.op("pool", (lambda e, o=o: e.tensor_tensor(out=o.ni, in0=o.ni, in1=o.t.t4[:, :, 0:nchk], op=ALU.add)),
                         reads=[o.t.bt4, o.t.bHs[o.hb]], writes=[o.t.bHs[o.hb]])
                for o in ops:
                    S.op("dve", (lambda e, o=o: e.tensor_tensor(out=o.nr, in0=o.nr, in1=o.xre, op=ALU.add)),
                         reads=[o.t.bXS[o.xb], o.t.bHs[o.hb]], writes=[o.t.bHs[o.hb]])
                    S.op("pool", (lambda e, o=o: e.tensor_tensor(out=o.ni, in0=o.ni, in1=o.xim, op=ALU.add)),
                         reads=[o.t.bXS[o.xb], o.t.bHs[o.hb]], writes=[o.t.bHs[o.hb]])
            for (t, hb, xb, j_lo) in ctx_:
                if contract:
                    yps = k.ps[t.yps_i][:, 0:SG * 32].rearrange("p (s c) -> p s c", c=32)
                    for q in range(QN):
                        S.op("pe", (lambda e, t=t, q=q, yps=yps, hb=hb: e.matmul(
                            yps[:, :, 0:nchk], t.CpF[:, q, 0, :], t.HsR[hb][:, t.lo_s:t.lo_s + SG, q, 0:nchk],
                            start=(q == 0), stop=False)), reads=[t.bCp, t.bHs[hb]], writes=[t.bYps])
                        S.op("pe", (lambda e, t=t, q=q, yps=yps, hb=hb: e.matmul(
                            yps[:, :, 0:nchk], t.CpF[:, q, 1, :], t.HsI[hb][:, t.lo_s:t.lo_s + SG, q, 0:nchk],
                            start=False, stop=(q == QN - 1))), reads=[t.bCp, t.bHs[hb]], writes=[t.bYps])
                    yv = Yl[:, :, j_lo:j_lo + SG].transpose([0, 2, 1])
                    S.op("dve", (lambda e, yv=yv, yps=yps: e.tensor_tensor(out=yv, in0=yv.bitcast(F32), in1=yps[:, :, 0:nchk],
                                                                            op=ALU.add)),
                         reads=[t.bYps, bY], writes=[bY])
                if g < NG - 1:
                    nb_ = t.gcount % 2
                    S.op("dve", (lambda e, t=t, hb=hb, nb_=nb_: e.tensor_copy(out=t.HsR[nb_][:, t.islot, :, 0:nchk], in_=t.HsR[hb][:, t.eslot, :, 0:nchk])),
                         reads=[t.bHs[hb]], writes=[t.bHs[nb_]])
                    S.op("pool", (lambda e, t=t, hb=hb, nb_=nb_: e.tensor_copy(out=t.HsI[nb_][:, t.islot, :, 0:nchk], in_=t.HsI[hb][:, t.eslot, :, 0:nchk])),
                         reads=[t.bHs[hb]], writes=[t.bHs[nb_]])
        return [(t.gcount - 1) % 2 for t in sts]

    for dc in range(8):
        for t in sts:
            dd = t.dd
            S.op("pool", (lambda e, t=t, dd=dd, dc=dc: e.dma_start(out=t.BpT, in_=d.Bf[:, dd, dc * QN:dc * QN + QN])),
                 writes=[t.bBp], dsem=k.dsem_w[2 * dd])
            S.op("pool", (lambda e, t=t, dd=dd, dc=dc: e.dma_start(out=t.Cp, in_=d.Cf[:, dd, dc * QN:dc * QN + QN])),
                 writes=[t.bCp], dsem=k.dsem_w[2 * dd + 1])
            f0 = 2 * dc * QN + dd
            sl = slice(f0, f0 + 2 * QN - 1, 2)
            t.a16, t.b16 = st[:, 0, sl], st[:, 1, sl]
            t.aT16, t.bT16 = st[:, 2, sl], st[:, 3, sl]
            fre = st[:, 4, sl].unsqueeze(2).broadcast_to([128, QN, 128])
            fim = st[:, 5, sl].unsqueeze(2).broadcast_to([128, QN, 128])
            cre, cim = t.CpF[:, :, 0, :], t.CpF[:, :, 1, :]
            Cp, bCp = t.Cp, t.bCp
            S.op("dve", (lambda e, fim=fim, cim=cim: e.tensor_tensor(out=tC, in0=cim, in1=fim, op=ALU.mult)), reads=[bCp, k.bconst], writes=[btC])
            S.op("dve", (lambda e, fre=fre, cre=cre: e.tensor_tensor(out=tD, in0=cre, in1=fre, op=ALU.mult)), reads=[bCp, k.bconst], writes=[btD])
            S.op("dve", lambda e: e.tensor_tensor(out=tD, in0=tD, in1=tC, op=ALU.subtract), reads=[btC, btD], writes=[btD])
            S.op("dve", (lambda e, fim=fim, cre=cre: e.tensor_tensor(out=tC, in0=cre, in1=fim, op=ALU.mult)), reads=[bCp, k.bconst, btD], writes=[btC])
            S.op("dve", (lambda e, fre=fre, cim=cim, Cp=Cp: e.tensor_tensor(out=Cp[:, :, 1, :], in0=cim, in1=fre, op=ALU.mult)), reads=[bCp, k.bconst], writes=[bCp])
            S.op("dve", (lambda e, cim=cim, Cp=Cp: e.scalar_tensor_tensor(out=Cp[:, :, 1, :], in0=cim, scalar=-1.0, in1=tC,
                                                                        op0=ALU.mult, op1=ALU.subtract)), reads=[btC, bCp], writes=[bCp])
            S.op("dve", (lambda e, Cp=Cp: e.tensor_copy(out=Cp[:, :, 0, :], in_=tD)), reads=[btD, bCp], writes=[bCp])
        hbs = sweep(dc, 0, LC // TCH, False, False)
        for t, hb in zip(sts, hbs):
            S.op("dve", (lambda e, t=t: e.memset(t.h0R, 0.0)), writes=[t.bh0])
            S.op("dve", (lambda e, t=t: e.memset(t.h0I, 0.0)), writes=[t.bh0])
        ncx = LC // TCH
        for ci in range(ncx):
            for t, hb in zip(sts, hbs):
                c = ci if t.dd == 0 else ncx - 1 - ci
                cstep_small(t, t.hAR, t.hAI, t.h0R, t.h0I, t.HsR[hb][:, t.eslot, :, c], t.HsI[hb][:, t.eslot, :, c],
                            [t.bh0, t.bHs[hb], k.bconst], [t.bhA])
            for t, hb in zip(sts, hbs):
                S.op("dve", (lambda e, t=t: e.tensor_copy(out=t.h0R, in_=t.hAR)), reads=[t.bhA], writes=[t.bh0])
                S.op("pool", (lambda e, t=t: e.tensor_copy(out=t.h0I, in_=t.hAI)), reads=[t.bhA], writes=[t.bh0])
        nl = LL // TCH
        hbs = sweep(dc, LC, nl, False, False)
        for t, hb in zip(sts, hbs):
            c_first = 0 if t.dd == 0 else nl - 1
            S.op("dve", (lambda e, t=t, c_first=c_first: e.tensor_copy(out=t.HinR[:, :, c_first], in_=t.h0R)), reads=[t.bh0], writes=[t.bHin])
            S.op("pool", (lambda e, t=t, c_first=c_first: e.tensor_copy(out=t.HinI[:, :, c_first], in_=t.h0I)), reads=[t.bh0], writes=[t.bHin])
        for ci in range(nl - 1):
            for t, hb in zip(sts, hbs):
                c, cn = (ci, ci + 1) if t.dd == 0 else (nl - 1 - ci, nl - 2 - ci)
                cstep_small(t, t.HinR[:, :, cn], t.HinI[:, :, cn], t.HinR[:, :, c], t.HinI[:, :, c],
                            t.HsR[hb][:, t.eslot, :, c], t.HsI[hb][:, t.eslot, :, c], [t.bHin, t.bHs[hb], k.bconst], [t.bHin])
        sweep(dc, LC, nl, True, True)
    S.barrier()
    for kk in range(8):
        S.op("act", (lambda e, kk=kk: e.activation(out=Y[:, kk, :], in_=Y[:, kk, :].bitcast(F32), func=AF.Gelu_apprx_tanh)),
             reads=[bY], writes=[bY])
    S.barrier()
    for kk in range(8):
        S.op("sp", (lambda e, kk=kk: e.dma_start(out=X[:, kk, LC:NT], in_=d.xsp[:, kk, :])),
             reads=[bsp], writes=bX(LC, NT), dsem=k.dsem_x)
    AR.o = o_ar
    AFa.reset()
    WG = 512
    wa = [(AR.take([8, 128]), Buf(f"wa{i}"), k.dsem_w[i]) for i in range(2)]
    wb = [(AR.take([8, 128]), Buf(f"wb{i}"), k.dsem_w[2 + i]) for i in range(2)]
    OST = AFa.take([8, WG]); bOST = Buf("OSTg")
    sg = [(AFa.take([WG]), Buf(f"sg{i}")) for i in range(2)]
    scr = make_scr(k, WG, 7)
    it = 0
    for w in range(LL // WG):
        c0 = w * WG
        for dcn in range(8):
            wa_, bwa, dsa = wa[it % 2]
            wb_, bwb, dsb = wb[it % 2]
            sg_, bsg = sg[it % 2]
            pa, pb = (0, 1) if it % 2 == 0 else (2, 3)
            it += 1
            S.op("pool", (lambda e, wa_=wa_, dcn=dcn: e.dma_start(out=wa_, in_=d.wglu[0, dcn])), writes=[bwa], dsem=dsa)
            S.op("pool", (lambda e, wb_=wb_, dcn=dcn: e.dma_start(out=wb_, in_=d.wglu[1, dcn])), writes=[bwb], dsem=dsb)
            for (pi, wt_, bw_) in ((pa, wa_, bwa), (pb, wb_, bwb)):
                for kk in range(8):
                    S.op("pe", (lambda e, pi=pi, wt_=wt_, kk=kk, c0=c0: e.matmul(
                        k.ps[pi][:, 0:WG], wt_[:, kk, :], Y[:, kk, c0:c0 + WG], start=(kk == 0), stop=(kk == 7))),
                        reads=[bw_, bY], writes=[k.bps[pi]])
            S.op("act", (lambda e, pb=pb, sg_=sg_: e.activation(out=sg_, in_=k.ps[pb][:, 0:WG], func=AF.Sigmoid)),
                 reads=[k.bps[pb]], writes=[bsg])
            S.op("dve", (lambda e, pa=pa, sg_=sg_, dcn=dcn: e.tensor_tensor(out=OST[:, dcn, :], in0=k.ps[pa][:, 0:WG], in1=sg_,
                                                                            op=ALU.mult)),
                 reads=[k.bps[pa], bsg], writes=[bOST])
        postnorm_update(k, X, bX, LC + c0, LC + c0 + WG, l, 0, b, OST, bOST, scr)


N_CORES = 8


def kernel(**inputs):
    inp = {k_: np.asarray(v) for k_, v in inputs.items()}
    sh = shared_inputs(inp)
    nc = build(stage=99, nb=2)
    in_maps = []
    for c in range(N_CORES):
        m = dict(sh)
        m.update(core_inputs(inp, c))
        in_maps.append(m)
    res = run_bass_kernel_spmd(nc, in_maps, core_ids=list(range(N_CORES)))
    out = np.empty((2 * N_CORES, LL, D), np.float32)
    for c in range(N_CORES):
        o = np.asarray(res.results[c]["outT"])
        out[2 * c] = o[0].T
        out[2 * c + 1] = o[1].T
    return out
```

```python
import contextlib
import math
import numpy as np
import concourse.bass as bass
import concourse.mybir as mybir
from concourse.bass_utils import run_bass_kernel_spmd

F32 = mybir.dt.float32
F32R = mybir.dt.float32r
AF = mybir.ActivationFunctionType
ALU = mybir.AluOpType

D = 1024
NT = 2304
LC = 256
LL = 2048
DFF = 2816
NFC = 22
EPS = 1e-6


class Buf:
    __slots__ = ("name", "w", "r")

    def __init__(self, name=""):
        self.name = name
        self.w = None
        self.r = {}


class DmaSem:
    def __init__(self, sched, name):
        self.key = ("dma", name)
        self.count = 0
        sched.dma_sems.append(self)


class Sched:
    COMPUTE = ("pe", "act", "dve", "pool")
    QUEUES = ("pe", "act", "dve", "pool", "sp")
    EPOCH = 12000

    def __init__(self):
        self.q = {e: [] for e in self.QUEUES}
        self.known = {e: {} for e in self.QUEUES}
        self.dma_sems = []
        self.pending = {e: {} for e in self.QUEUES}
        self.ec = {e: 0 for e in self.QUEUES}
        self.eng_keys = set()

    def _ekey(self, eng, count):
        ep = (count - 1) // self.EPOCH
        return ("eng", eng, ep), (count - 1) % self.EPOCH + 1

    def barrier(self):
        snap = {}
        for e in self.COMPUTE:
            if self.ec[e] > 0:
                key, val = self._ekey(e, self.ec[e])
                snap[key] = val
        for d in self.dma_sems:
            snap[d.key] = d.count
        for e in self.QUEUES:
            p = self.pending[e]
            for k, v in snap.items():
                if v > p.get(k, 0):
                    p[k] = v

    def op(self, eng, fn, reads=(), writes=(), dsem=None):
        q = self.q[eng]
        idx = len(q) + 1
        deps = dict(self.pending[eng])
        self.pending[eng] = {}
        for key in [kk for kk in deps if kk[0] == "eng" and kk[1] == eng]:
            del deps[key]

        def add(tok):
            if tok is None:
                return
            key, val, teng, tidx = tok
            if teng == eng and key[0] == "eng":
                if eng == "pe" or idx - tidx > 1:
                    return
            if deps.get(key, 0) < val:
                deps[key] = val

        for b in reads:
            add(b.w)
        for b in writes:
            add(b.w)
            for t in b.r.values():
                add(t)
        kn = self.known[eng]
        waits = []
        for key, val in deps.items():
            if val <= 0:
                continue
            if kn.get(key, 0) >= val:
                continue
            kn[key] = val
            waits.append((key, val))
        if dsem is not None:
            dsem.count += 16
            tok = (dsem.key, dsem.count, eng, idx)
            inc = (dsem.key, 16)
        else:
            self.ec[eng] += 1
            key, val = self._ekey(eng, self.ec[eng])
            self.eng_keys.add(key)
            tok = (key, val, eng, idx)
            inc = (key, 1)
        q.append((waits, fn, inc))
        for b in reads:
            b.r[(tok[0][0], tok[0][1])] = tok
        for b in writes:
            b.w = tok
            b.r = {}
        return tok

    def emit(self, nc, final_wait_tokens=()):
        with contextlib.ExitStack() as st:
            sems = {}
            for key in sorted(self.eng_keys):
                sems[key] = st.enter_context(nc.semaphore("s_%s_%d" % (key[1], key[2])))
            for d in self.dma_sems:
                sems[d.key] = st.enter_context(nc.semaphore("d_" + str(d.key[1])))
            block = st.enter_context(nc.Block())
            q = self.q

            def replay(eng_obj, name, extra_waits=()):
                for waits, fn, inc in q[name]:
                    for key, val in waits:
                        eng_obj.wait_ge(sems[key], val)
                    ins = fn(eng_obj)
                    ins.then_inc(sems[inc[0]], inc[1])
                for key, val in extra_waits:
                    eng_obj.wait_ge(sems[key], val)

            fw = [(t[0], t[1]) for t in final_wait_tokens]

            @block.tensor
            def _(e):
                replay(e, "pe")

            @block.scalar
            def _(e):
                replay(e, "act")

            @block.vector
            def _(e):
                replay(e, "dve")

            @block.gpsimd
            def _(e):
                replay(e, "pool")

            @block.sync
            def _(e):
                replay(e, "sp", fw)


def rope_tables():
    grid_w, head_dim, theta = 64, 128, 10000.0
    rows = LL // grid_w
    row = np.repeat(np.arange(rows, dtype=np.float32), grid_w)
    col = np.tile(np.arange(grid_w, dtype=np.float32), rows)
    ppa = head_dim // 4
    freq = (np.float32(theta) ** (-np.arange(ppa, dtype=np.float32) / np.float32(ppa))).astype(np.float32)
    ang = np.concatenate([row[:, None] * freq, col[:, None] * freq], axis=-1).astype(np.float32)
    cos = np.cos(ang).astype(np.float32).T
    sin = np.sin(ang).astype(np.float32).T
    return np.ascontiguousarray(np.concatenate([cos, cos], 0)), np.ascontiguousarray(np.concatenate([sin, sin], 0))


def pk(v):
    v = np.asarray(v, np.float32)
    lead = v.shape[:-1]
    m = v.shape[-1] // 128
    r = v.reshape(lead + (m, 128))
    r = np.moveaxis(r, -1, 0)
    return np.ascontiguousarray(r)


def shared_inputs(inp):
    s = {}
    s["w_mod"] = np.ascontiguousarray(inp["w_mod"], np.float32)
    s["bmT"] = pk(inp["b_mod"])
    g = np.stack([inp["g_pre_mix"], inp["g_post_mix"], inp["g_pre_ffn"], inp["g_post_ffn"]], 1)
    s["gT"] = pk(g)
    wqkv = np.asarray(inp["w_qkv"][0], np.float32)
    s["wqk"] = np.ascontiguousarray(wqkv[:, :1280].reshape(8, 128, 10, 128).transpose(2, 1, 0, 3))
    s["wv"] = np.ascontiguousarray(wqkv[:, 1280:].reshape(8, 128, 256).transpose(1, 0, 2))
    wo = np.asarray(inp["w_o"][0], np.float32)
    s["wo"] = np.ascontiguousarray(wo.reshape(8, 128, 8, 128).transpose(2, 1, 0, 3))
    s["qkg"] = np.ascontiguousarray(np.stack([inp["q_norm_g"][0], inp["k_norm_g"][0]], 1), np.float32)
    cos, sin = rope_tables()
    s["cos"], s["sin"] = cos, sin
    rot = np.zeros((128, 128), np.float32)
    for m in range(64):
        rot[m + 64, m] = -1.0
        rot[m, m + 64] = 1.0
    s["rotm"] = rot
    s["ones"] = np.ones((128, 128), np.float32)
    wup = np.asarray(inp["w_up"], np.float32)
    wu = wup.reshape(2, 8, 128, 2, NFC, 128)
    s["wup"] = np.ascontiguousarray(wu.transpose(0, 4, 2, 1, 3, 5)).reshape(2, NFC, 128, 8, 256)
    wdn = np.asarray(inp["w_down"], np.float32)
    s["wdn"] = np.ascontiguousarray(wdn.reshape(2, NFC, 128, 8, 128).transpose(0, 3, 2, 1, 4))
    s["cw"] = pk(inp["conv_w"])
    s["cb"] = pk(inp["conv_b"])
    lre = np.asarray(inp["ssm_lambda_re"][0], np.float32)
    lim = np.asarray(inp["ssm_lambda_im"][0], np.float32)
    lst = np.asarray(inp["ssm_log_step"][0], np.float32)

    def en_pd(a):
        return a.reshape(2, 32, 2, 64).transpose(2, 3, 1, 0).reshape(128, 64)

    lam = np.stack([en_pd(lre), en_pd(lim), en_pd(np.broadcast_to(lst[:, :, None], (2, 64, 64)))], 1)
    s["lam"] = np.ascontiguousarray(lam, np.float32)
    Bf = np.zeros((4, 2, 16, 2, 32, 2, 2, 64), np.float32)
    Cf = np.zeros((2, 64, 2, 32, 2, 4, 2, 16), np.float32)
    for ri, (bsrc, csrc) in enumerate(((inp["ssm_b_re"], inp["ssm_c_re"]), (inp["ssm_b_im"], inp["ssm_c_im"]))):
        bsrc = np.asarray(bsrc[0], np.float32)
        csrc = np.asarray(csrc[0], np.float32)
        for p_ in range(32):
            r_ = p_ % 4
            for e_ in range(2):
                g_ = 2 * p_ + e_
                Bf[r_, e_, :, :, p_, ri, e_, :] = bsrc[:, g_, :, :].transpose(2, 0, 1)
                Cf[e_, :, :, p_, ri, r_, e_, :] = csrc[:, g_, :, :].transpose(2, 0, 1)
    s["Bf"] = np.ascontiguousarray(Bf.reshape(128, 2, 32, 2, 128))
    s["Cf"] = np.ascontiguousarray(Cf.reshape(128, 2, 32, 2, 128))
    s["dsk"] = pk(inp["ssm_d"][0])
    wg = np.stack([inp["w_glu_a"][0], inp["w_glu_b"][0]]).astype(np.float32)
    s["wglu"] = np.ascontiguousarray(wg.reshape(2, 8, 128, 8, 128).transpose(0, 3, 2, 1, 4))
    return s


def core_inputs(inp, core):
    b0 = 2 * core
    m = {}
    xs = []
    for b in (b0, b0 + 1):
        xs.append(np.concatenate([inp["ctx"][b], inp["x"][b]], 0).T)
    m["xT"] = np.ascontiguousarray(np.stack(xs), np.float32)
    c3 = np.stack([inp["c"][b0], inp["c"][b0 + 1], inp["c_ctx"]], -1)
    m["cT"] = np.ascontiguousarray(c3.reshape(8, 128, 3).transpose(1, 0, 2), np.float32)
    return m


NAR = 24576
NAF = 8256
TCH = 64
TCX = 8
TWO_PI = 2.0 * math.pi
import os as _os
IMENG = _os.environ.get('IMENG', 'pool')


class K:
    pass


class XB:
    def __init__(self):
        self.b = [Buf(f"X{i}") for i in range(NT // 256)]

    def __call__(self, c0, c1):
        return self.b[c0 // 256:(c1 + 255) // 256]


class Arena:
    def __init__(self, t, n, dt):
        self.t, self.n, self.dt, self.o = t, n, dt, 0

    def reset(self):
        self.o = 0

    def take(self, shape):
        n = int(np.prod(shape))
        v = self.t[:, self.o:self.o + n]
        self.o += n
        assert self.o <= self.n, (self.o, self.n)
        if len(shape) == 2:
            v = v.rearrange("p (a b) -> p a b", b=shape[1])
        elif len(shape) == 3:
            v = v.rearrange("p (a b c) -> p a b c", b=shape[1], c=shape[2])
        elif len(shape) == 4:
            v = v.rearrange("p (a b c d) -> p a b c d", b=shape[1], c=shape[2], d=shape[3])
        return v


def build(stage=99, nb=2):
    nc = bass.Bass("TRN2", target_bir_lowering=False)
    S = Sched()
    k = K()
    k.nc, k.S = nc, S

    def din(name, shape, dt=F32):
        return nc.dram_tensor(name, list(shape), dt, kind="ExternalInput").ap()

    d = K()
    d.xT = din("xT", [2, D, NT])
    d.cT = din("cT", [128, 8, 3])
    d.w_mod = din("w_mod", [2, D, 6 * D])
    d.bmT = din("bmT", [128, 2, 48])
    d.gT = din("gT", [128, 2, 4, 8])
    d.wqk = din("wqk", [10, 128, 8, 128])
    d.wv = din("wv", [128, 8, 256])
    d.wo = din("wo", [8, 128, 8, 128])
    d.qkg = din("qkg", [128, 2])
    d.cos = din("cos", [128, LL])
    d.sin = din("sin", [128, LL])
    d.rotm = din("rotm", [128, 128])
    d.ones = din("ones", [128, 128])
    d.wup = din("wup", [2, NFC, 128, 8, 256])
    d.wdn = din("wdn", [2, 8, 128, NFC, 128])
    d.cw = din("cw", [128, 2, 3, 44])
    d.cb = din("cb", [128, 2, 44])
    d.lam = din("lam", [128, 3, 64])
    d.Bf = din("Bf", [128, 2, 32, 2, 128])
    d.Cf = din("Cf", [128, 2, 32, 2, 128])
    d.dsk = din("dsk", [128, 8])
    d.wglu = din("wglu", [2, 8, 128, 8, 128])
    d.out = nc.dram_tensor("outT", [2, D, LL], F32, kind="ExternalOutput").ap()
    if stage != 99:
        d.dbg = nc.dram_tensor("dbg", [128, 8, NT], F32, kind="ExternalOutput").ap()
    d.xsp = nc.dram_tensor("xspill", [128, 8, LL], F32, kind="Internal").ap()
    k.d = d

    with contextlib.ExitStack() as st:
        def sb(name, shape, dt=F32):
            return st.enter_context(nc.sbuf_tensor("sb_" + name, list(shape), dt))

        k.Xt = sb("X", [128, 8 * NT])
        k.ARt = sb("AR", [128, NAR], F32R)
        k.AFt = sb("AF", [128, NAF])
        k.AR = Arena(k.ARt, NAR, F32R)
        k.AF = Arena(k.AFt, NAF, F32)
        k.ones = sb("ones", [128, 128], F32R)
        k.rotm = sb("rotm", [128, 128], F32R)
        k.modv = sb("modv", [128, 2, 48, 3])
        k.gT = sb("gT", [128, 2, 4, 8])
        k.sA = sb("sA", [128, 2, 2, 3, 8])
        k.sG = sb("sG", [128, 2, 2, 3, 8])
        k.qkg = sb("qkg", [128, 2])
        k.cw = sb("cw", [128, 2, 3, 44])
        k.cb = sb("cb", [128, 2, 44])
        k.epsD = sb("epsD", [128, 1])
        k.eps128 = sb("eps128", [128, 1])
        k.bmT = sb("bmT", [128, 2, 48])
        k.dsk = sb("dsk", [128, 8])
        k.stab = sb("stab", [128, 8, 64])
        k.ps01 = st.enter_context(nc.psum_tensor("ps01", [128, 1024], F32))
        k.ps23 = st.enter_context(nc.psum_tensor("ps23", [128, 1024], F32))
        k.ps = [k.ps01[:, 0:512], k.ps01[:, 512:1024], k.ps23[:, 0:512], k.ps23[:, 512:1024]]
        k.ps += [st.enter_context(nc.psum_tensor(f"ps{i}", [128, 512], F32)) for i in range(4, 8)]
        k.bps = [Buf(f"ps{i}") for i in range(8)]
        k.bconst = Buf("const")
        k.dsem_c = DmaSem(S, "const")
        k.dsem_o = DmaSem(S, "out")
        k.dsem_x = DmaSem(S, "x")
        k.dsem_w = [DmaSem(S, f"w{i}") for i in range(8)]
        k.dsem_m = [DmaSem(S, f"m{i}") for i in range(2)]
        k.dsem_cs = DmaSem(S, "const_sw")

        prologue(k)
        fin = []
        for b in range(nb):
            fin += batch_element(k, b, stage)
        S.emit(nc, fin)
    k.ninstr = {e: len(q) for e, q in S.q.items()}
    print("instr counts", k.ninstr)
    return nc


def prologue(k):
    nc, S, d = k.nc, k.S, k.d
    bc = k.bconst
    dc = k.dsem_c
    S.op("pool", lambda e: e.dma_start(out=k.ones[:], in_=d.ones), writes=[bc], dsem=k.dsem_cs)
    S.op("pool", lambda e: e.dma_start(out=k.rotm[:], in_=d.rotm), writes=[bc], dsem=k.dsem_cs)
    for t, src in ((k.gT, d.gT), (k.qkg, d.qkg), (k.cw, d.cw), (k.cb, d.cb), (k.bmT, d.bmT), (k.dsk, d.dsk)):
        S.op("sp", (lambda e, t=t, src=src: e.dma_start(out=t[:], in_=src)), writes=[bc], dsem=dc)
    S.op("dve", lambda e: e.memset(k.epsD[:], EPS), writes=[bc])
    S.op("dve", lambda e: e.memset(k.eps128[:], EPS), writes=[bc])
    AFa = k.AF
    AFa.reset()
    cT = AFa.take([8, 3])
    sc = AFa.take([8, 3])
    bcT, bsc = Buf("cT"), Buf("sc")
    S.op("sp", lambda e: e.dma_start(out=cT, in_=d.cT), writes=[bcT], dsem=dc)
    S.op("act", lambda e: e.activation(out=sc, in_=cT, func=AF.Silu), reads=[bcT], writes=[bsc])
    import os
    PS = int(os.environ.get("PRO_STOP", "9"))
    if PS <= 1:
        S.barrier()
        return
    NP = 384
    wt = [(AFa.take([8, NP]), Buf(f"wm{i}")) for i in range(2)]
    bmod = Buf("modv")
    it = 0
    for l in range(2):
        for j in range(6 * D // NP):
            wv_, wb = wt[it % 2]
            ds = k.dsem_m[it % 2]
            src = d.w_mod[l, :, j * NP:(j + 1) * NP].rearrange("(kk p) n -> p kk n", p=128)
            S.op("sp", (lambda e, wv_=wv_, src=src: e.dma_start(out=wv_, in_=src)), writes=[wb], dsem=ds)
            for c in range(NP // 128):
                oc = j * (NP // 128) + c
                pst = k.ps[l][:, oc * 3:oc * 3 + 3]
                for kk in range(8):
                    S.op("pe", (lambda e, pst=pst, wv_=wv_, c=c, kk=kk: e.matmul(
                        pst, wv_[:, kk, c * 128:(c + 1) * 128], sc[:, kk, :], start=(kk == 0), stop=(kk == 7))),
                        reads=[wb, bsc], writes=[k.bps[l]])
            it += 1
        S.op("dve", (lambda e, l=l: e.tensor_tensor(
            out=k.modv[:, l, :, :], in0=k.ps[l][:, 0:144].rearrange("p (a b) -> p a b", b=3),
            in1=k.bmT[:, l, :].unsqueeze(2).broadcast_to([128, 48, 3]), op=ALU.add)),
            reads=[k.bps[l], bc], writes=[bmod])
    if PS <= 2:
        S.barrier()
        return
    for l in range(2):
        for j, (gi, si) in enumerate(((0, 1), (2, 4))):
            for r in range(3):
                S.op("dve", (lambda e, l=l, j=j, gi=gi, si=si, r=r: e.scalar_tensor_tensor(
                    out=k.sA[:, l, j, r, :], in0=k.modv[:, l, si * 8:(si + 1) * 8, r], scalar=1.0,
                    in1=k.gT[:, l, gi, :], op0=ALU.add, op1=ALU.mult)), reads=[bmod, bc], writes=[bc])
        for j, (gi, si) in enumerate(((1, 2), (3, 5))):
            for r in range(3):
                S.op("dve", (lambda e, l=l, j=j, gi=gi, si=si, r=r: e.tensor_tensor(
                    out=k.sG[:, l, j, r, :], in0=k.modv[:, l, si * 8:(si + 1) * 8, r],
                    in1=k.gT[:, l, gi, :], op=ALU.mult)), reads=[bmod, bc], writes=[bc])
    S.barrier()
    if PS <= 3:
        return
    s5_tables(k)
    S.barrier()


def s5_tables(k):
    S, d = k.S, k.d
    AFa = k.AF
    AFa.reset()
    lam = AFa.take([3, 64])
    w = [AFa.take([64]) for _ in range(10)]
    bt = Buf("s5t")
    S.op("sp", lambda e: e.dma_start(out=lam, in_=d.lam), writes=[bt], dsem=k.dsem_c)
    lre, lim, lst = lam[:, 0, :], lam[:, 1, :], lam[:, 2, :]
    dt_, xr, xi, mag, tq, tr, cs, sn, den, t2 = w
    st = k.stab

    def dve(fn):
        S.op("dve", fn, reads=[bt], writes=[bt])

    def act(fn):
        S.op("act", fn, reads=[bt], writes=[bt])

    act(lambda e: e.activation(out=dt_, in_=lst, func=AF.Exp))
    dve(lambda e: e.tensor_tensor(out=xr, in0=lre, in1=dt_, op=ALU.mult))
    dve(lambda e: e.tensor_tensor(out=xi, in0=lim, in1=dt_, op=ALU.mult))
    MAGIC = 12582912.0

    def sincos(mult, out_c, out_s):
        for off, dst in ((0.0, out_s), (math.pi / 2, out_c)):
            dve(lambda e, off=off: e.tensor_scalar(out=tq, in0=xi, scalar1=float(mult), scalar2=float(off),
                                                    op0=ALU.mult, op1=ALU.add))
            dve(lambda e: e.tensor_scalar(out=tr, in0=tq, scalar1=1.0 / TWO_PI, scalar2=MAGIC,
                                          op0=ALU.mult, op1=ALU.add))
            dve(lambda e: e.tensor_scalar(out=tr, in0=tr, scalar1=MAGIC, scalar2=None, op0=ALU.subtract))
            dve(lambda e: e.scalar_tensor_tensor(out=tq, in0=tr, scalar=-TWO_PI, in1=tq, op0=ALU.mult, op1=ALU.add))
            dve(lambda e: e.tensor_scalar(out=tq, in0=tq, scalar1=3.1415925, scalar2=-3.1415925,
                                          op0=ALU.min, op1=ALU.max))
            act(lambda e, dst=dst: e.activation(out=dst, in_=tq, func=AF.Sin))

    for mult, ia, ib in ((1.0, 0, 1), (float(TCH), 2, 3), (float(TCX), 6, 7)):
        sincos(mult, cs, sn)
        act(lambda e, mult=mult: e.activation(out=mag, in_=xr, func=AF.Exp, scale=float(mult)))
        dve(lambda e, ia=ia: e.tensor_tensor(out=st[:, ia, :], in0=mag, in1=cs, op=ALU.mult))
        dve(lambda e, ib=ib: e.tensor_tensor(out=st[:, ib, :], in0=mag, in1=sn, op=ALU.mult))
    dve(lambda e: e.tensor_tensor(out=den, in0=lre, in1=lre, op=ALU.mult))
    dve(lambda e: e.tensor_tensor(out=t2, in0=lim, in1=lim, op=ALU.mult))
    dve(lambda e: e.tensor_tensor(out=den, in0=den, in1=t2, op=ALU.add))
    dve(lambda e: e.reciprocal(out=den, in_=den))
    dve(lambda e: e.tensor_scalar(out=tq, in0=st[:, 0, :], scalar1=-1.0, scalar2=None, op0=ALU.add))
    dve(lambda e: e.tensor_tensor(out=tr, in0=tq, in1=lre, op=ALU.mult))
    dve(lambda e: e.tensor_tensor(out=t2, in0=st[:, 1, :], in1=lim, op=ALU.mult))
    dve(lambda e: e.tensor_tensor(out=tr, in0=tr, in1=t2, op=ALU.add))
    dve(lambda e: e.tensor_tensor(out=st[:, 4, :], in0=tr, in1=den, op=ALU.mult))
    dve(lambda e: e.tensor_tensor(out=tr, in0=st[:, 1, :], in1=lre, op=ALU.mult))
    dve(lambda e: e.tensor_tensor(out=t2, in0=tq, in1=lim, op=ALU.mult))
    dve(lambda e: e.tensor_tensor(out=tr, in0=tr, in1=t2, op=ALU.subtract))
    dve(lambda e: e.tensor_tensor(out=st[:, 5, :], in0=tr, in1=den, op=ALU.mult))
    S.op("dve", lambda e: e.tensor_copy(out=cs, in_=st[:, 5, :]), reads=[bt], writes=[bt, k.bconst])


def segs_rows(c0, c1, b):
    out = []
    if c0 < LC:
        out.append((c0, min(c1, LC), 2))
    if c1 > LC:
        out.append((max(c0, LC), c1, b))
    return out


def norm_stats(k, src_fn, reads, W, scr):
    S = k.S
    pi = scr["ps"]
    pst = k.ps[pi][:, 0:W]
    for kk in range(8):
        sq, bsq = scr["sq"][kk % 2]
        S.op("act", (lambda e, sq=sq, kk=kk: e.activation(out=sq[:, 0:W], in_=src_fn(kk), func=AF.Square)),
             reads=reads, writes=[bsq])
        S.op("pe", (lambda e, sq=sq, kk=kk: e.matmul(pst, k.ones[:], sq[:, 0:W], start=(kk == 0), stop=(kk == 7))),
             reads=[bsq, k.bconst], writes=[k.bps[pi]])
    rstd, brs = scr["rstd"]
    S.op("act", lambda e: e.activation(out=rstd[:, 0:W], in_=pst, func=AF.Sqrt, scale=1.0 / D, bias=k.epsD[:]),
         reads=[k.bps[pi], k.bconst], writes=[brs])
    S.op("dve", lambda e: e.reciprocal(out=rstd[:, 0:W], in_=rstd[:, 0:W]), reads=[brs], writes=[brs])
    return rstd, brs


def modnorm(k, X, bX, c0, c1, l, j, b, hout, bh, scr, ho=0):
    S = k.S
    W = c1 - c0
    Ws = W + (W % 2)
    assert c0 + Ws <= NT
    rstd, brs = norm_stats(k, lambda kk: X[:, kk, c0:c0 + Ws], bX(c0, c0 + Ws), Ws, scr)
    shift_i = 0 if j == 0 else 3
    for kk in range(8):
        tmp, btmp = scr["tmp"][kk % 2]
        S.op("dve", (lambda e, tmp=tmp, kk=kk: e.tensor_tensor(out=tmp[:, 0:W], in0=X[:, kk, c0:c1], in1=rstd[:, 0:W],
                                                               op=ALU.mult)), reads=bX(c0, c1) + [brs], writes=[btmp])
        for (lo, hi, r) in segs_rows(c0, c1, b):
            S.op("act", (lambda e, tmp=tmp, kk=kk, lo=lo, hi=hi, r=r: e.activation(
                out=hout[:, kk, ho + lo - c0:ho + hi - c0], in_=tmp[:, lo - c0:hi - c0], func=AF.Identity,
                scale=k.sA[:, l, j, r, kk:kk + 1], bias=k.modv[:, l, shift_i * 8 + kk, r:r + 1])),
                reads=[btmp, k.bconst], writes=(bh if isinstance(bh, list) else [bh]))


def postnorm_update(k, X, bX, c0, c1, l, j, b, ost, bost, scr):
    S = k.S
    W = c1 - c0
    rstd, brs = norm_stats(k, lambda kk: ost[:, kk, 0:W], [bost], W, scr)
    for kk in range(8):
        tmp, btmp = scr["tmp"][kk % 2]
        S.op("dve", (lambda e, tmp=tmp, kk=kk: e.tensor_tensor(out=tmp[:, 0:W], in0=ost[:, kk, 0:W], in1=rstd[:, 0:W],
                                                               op=ALU.mult)), reads=[bost, brs], writes=[btmp])
        for (lo, hi, r) in segs_rows(c0, c1, b):
            S.op("dve", (lambda e, tmp=tmp, kk=kk, lo=lo, hi=hi, r=r: e.scalar_tensor_tensor(
                out=X[:, kk, lo:hi], in0=tmp[:, lo - c0:hi - c0], scalar=k.sG[:, l, j, r, kk:kk + 1],
                in1=X[:, kk, lo:hi], op0=ALU.mult, op1=ALU.add)), reads=[btmp, k.bconst] + bX(lo, hi), writes=bX(lo, hi))


def dump_x(k, X, bX, fin):
    S, d = k.S, k.d
    for kk in range(8):
        fin.append(S.op("sp", (lambda e, kk=kk: e.dma_start(out=d.dbg[:, kk, :], in_=X[:, kk, :])),
                        reads=bX(0, NT), dsem=k.dsem_o))


def batch_element(k, b, stage):
    nc, S, d = k.nc, k.S, k.d
    X = k.Xt[:, :].rearrange("p (a t) -> p a t", t=NT)
    bX = XB()
    for kk in range(8):
        S.op("sp", (lambda e, kk=kk: e.dma_start(out=X[:, kk, :], in_=d.xT[b, kk * 128:(kk + 1) * 128, :])),
             writes=bX(0, NT), dsem=k.dsem_x)
    fin = []
    if stage == 0:
        dump_x(k, X, bX, fin)
        S.barrier()
        return fin
    import os
    if os.environ.get("S5_ONLY") == "1":
        s5_layer(k, b, X, bX)
        S.barrier()
        dump_x(k, X, bX, fin)
        S.barrier()
        return fin
    attention_layer(k, b, X, bX)
    S.barrier()
    if stage == 1:
        if b == 0:
            dump_x(k, X, bX, fin)
        S.barrier()
        return fin
    ffn_layer(k, b, X, bX, 0, 0, NT)
    S.barrier()
    if stage == 2:
        if b == 0:
            dump_x(k, X, bX, fin)
        S.barrier()
        return fin
    s5_layer(k, b, X, bX)
    S.barrier()
    if stage == 3:
        if b == 0:
            dump_x(k, X, bX, fin)
        S.barrier()
        return fin
    ffn_layer(k, b, X, bX, 1, LC, NT)
    S.barrier()
    for kk in range(8):
        fin.append(S.op("sp", (lambda e, kk=kk: e.dma_start(out=d.out[b, kk * 128:(kk + 1) * 128, :], in_=X[:, kk, LC:NT])),
                        reads=bX(LC, NT), dsem=k.dsem_o))
    S.barrier()
    return fin


def make_scr(k, W, ps):
    AR, AFa = k.AR, k.AF
    scr = {"ps": ps, "sq": [], "tmp": []}
    for i in range(2):
        scr["sq"].append((AR.take([W]), Buf(f"sq{i}")))
    for i in range(2):
        scr["tmp"].append((AFa.take([W]), Buf(f"tmp{i}")))
    scr["rstd"] = (AFa.take([W]), Buf("rstd"))
    return scr


def attention_layer(k, b, X, bX):
    nc, S, d = k.nc, k.S, k.d
    l = 0
    WQ = 256
    AR, AFa = k.AR, k.AF
    AR.reset()
    AFa.reset()
    KT = AR.take([2, NT])
    V = AR.take([18, 256])
    AT = AR.take([8, WQ])
    HW = AR.take([8, WQ])
    QT = AR.take([8, WQ])
    bKT, bV, bAT, bHW, bQT = Buf("KT"), Buf("V"), Buf("AT"), Buf("HW"), Buf("QT")
    NRB = 4
    ring = [(AR.take([8, 128]), Buf(f"wr{i}"), k.dsem_w[i]) for i in range(NRB)]
    loads = []
    for w_ in range(NT // WQ):
        loads += [("qk", 8), ("qk", 9)]
    for w_ in range(NT // WQ):
        loads += [("qk", h_) for h_ in range(8)] + [("wo", dc_) for dc_ in range(8)]
    issued = [0]
    cons = [0]

    def issue(upto):
        while issued[0] < min(upto, len(loads)):
            kind, idx = loads[issued[0]]
            v, bw, ds = ring[issued[0] % NRB]
            src = d.wqk[idx] if kind == "qk" else d.wo[idx]
            S.op("pool", (lambda e, v=v, src=src: e.dma_start(out=v, in_=src)), writes=[bw], dsem=ds)
            issued[0] += 1

    def next_w(kind, idx):
        c = cons[0]
        assert loads[c] == (kind, idx), (loads[c], kind, idx)
        issue(c + NRB)
        cons[0] += 1
        v, bw, ds = ring[c % NRB]
        return v, bw
    WV = AR.take([8, 256]); bWV = Buf("WV")
    PT = [(AR.take([WQ]), Buf(f"PT{i}")) for i in range(4)]
    KNs = [(AR.take([WQ]), Buf(f"KN{i}")) for i in range(2)]
    SQ2s = [(AR.take([WQ]), Buf(f"SQ2{i}")) for i in range(2)]
    COS = AFa.take([LL])
    SIN = AFa.take([LL])
    OST = AFa.take([8, WQ]); bOST = Buf("OST")
    RS2s = [(AFa.take([WQ]), Buf(f"RS2{i}")) for i in range(2)]
    T2s = [(AFa.take([WQ]), Buf(f"T2{i}")) for i in range(2)]
    hcount = [0]
    RINV = AFa.take([WQ]); bRINV = Buf("RINV")
    scr = make_scr(k, WQ, 7)
    brope = Buf("rope")
    S.op("sp", lambda e: e.dma_start(out=COS, in_=d.cos), writes=[brope], dsem=k.dsem_c)
    S.op("sp", lambda e: e.dma_start(out=SIN, in_=d.sin), writes=[brope], dsem=k.dsem_c)
    S.op("pool", lambda e: e.dma_start(out=WV, in_=d.wv), writes=[bWV], dsem=k.dsem_w[5])
    issue(NRB - 1)
    bS = {(4, 0): Buf("S40"), (4, 1): Buf("S41"), (5, 0): Buf("S50"), (5, 1): Buf("S51")}
    def qk_head(ci, gcol, c0, lat, dst, bdst):
        wv_, bw = next_w("qk", ci)
        par = hcount[0] % 2
        hcount[0] += 1
        pi = 6 + par
        KN, bKN = KNs[par]
        SQ2, bSQ2 = SQ2s[par]
        RS2, bRS2 = RS2s[par]
        T2, bT2 = T2s[par]
        pst = k.ps[pi][:, 0:WQ]
        for kk in range(8):
            S.op("pe", (lambda e, kk=kk: e.matmul(pst, wv_[:, kk, :], HW[:, kk, :], start=(kk == 0), stop=(kk == 7))),
                 reads=[bw, bHW], writes=[k.bps[pi]])
        S.op("act", lambda e: e.activation(out=SQ2, in_=pst, func=AF.Square), reads=[k.bps[pi]], writes=[bSQ2])
        p2 = k.ps[pi][:, 256:256 + WQ]
        S.op("pe", lambda e: e.matmul(p2, k.ones[:], SQ2, start=True, stop=True), reads=[bSQ2, k.bconst],
             writes=[k.bps[pi]])
        S.op("act", lambda e: e.activation(out=RS2, in_=p2, func=AF.Sqrt, scale=1.0 / 128, bias=k.eps128[:]),
             reads=[k.bps[pi], k.bconst], writes=[bRS2])
        S.op("dve", lambda e: e.reciprocal(out=RS2, in_=RS2), reads=[bRS2], writes=[bRS2])
        if not lat:
            S.op("dve", lambda e: e.scalar_tensor_tensor(out=dst, in0=pst, scalar=k.qkg[:, gcol:gcol + 1], in1=RS2,
                                                         op0=ALU.mult, op1=ALU.mult),
                 reads=[k.bps[pi], bRS2, k.bconst], writes=[bdst])
            return
        S.op("dve", lambda e: e.scalar_tensor_tensor(out=KN, in0=pst, scalar=k.qkg[:, gcol:gcol + 1], in1=RS2,
                                                     op0=ALU.mult, op1=ALU.mult),
             reads=[k.bps[pi], bRS2, k.bconst], writes=[bKN])
        S.op("pe", lambda e: e.matmul(p2, k.rotm[:], KN, start=True, stop=True), reads=[bKN, k.bconst, bRS2],
             writes=[k.bps[pi]])
        t0 = c0 - LC
        S.op("dve", lambda e: e.tensor_tensor(out=T2, in0=p2, in1=SIN[:, t0:t0 + WQ], op=ALU.mult),
             reads=[k.bps[pi], brope], writes=[bT2])
        S.op("dve", lambda e: e.tensor_tensor(out=RS2, in0=KN.bitcast(F32), in1=COS[:, t0:t0 + WQ], op=ALU.mult),
             reads=[bKN, brope], writes=[bRS2])
        S.op("dve", lambda e: e.tensor_tensor(out=dst, in0=RS2, in1=T2, op=ALU.add), reads=[bRS2, bT2], writes=[bdst])

    for w in range(NT // WQ):
        c0 = w * WQ
        lat = c0 >= LC
        modnorm(k, X, bX, c0, c0 + WQ, l, 0, b, HW, bHW, scr)
        for jk in range(2):
            qk_head(8 + jk, 1, c0, lat, KT[:, jk, c0:c0 + WQ], bKT)
        for sub in range(2):
            pi = 5
            pst = k.ps[pi][:, sub * 256:(sub + 1) * 256]
            for kk in range(8):
                S.op("pe", (lambda e, kk=kk, sub=sub, pst=pst: e.matmul(
                    pst, HW[:, kk, sub * 128:(sub + 1) * 128], WV[:, kk, :], start=(kk == 0), stop=(kk == 7))),
                    reads=[bHW, bWV], writes=[k.bps[pi]])
            S.op("act", (lambda e, sub=sub, pst=pst, w=w: e.activation(out=V[:, 2 * w + sub, :], in_=pst, func=AF.Identity)),
                 reads=[k.bps[pi]], writes=[bV])
    scale = 1.0 / math.sqrt(128.0)
    wo_it = [0]
    pt_it = [0]
    for w in range(NT // WQ):
        c0 = w * WQ
        lat = c0 >= LC
        nkt = 18 if lat else 2
        modnorm(k, X, bX, c0, c0 + WQ, l, 0, b, HW, bHW, scr)
        for h in range(8):
            qk_head(h, 0, c0, lat, QT[:, h, :], bQT)
        for h in range(8):
            jk = h // 4
            po, pr = (0, 1) if h % 2 == 0 else (2, 3)
            for kt in range(nkt):
                sp_i = 4 if (kt % 2 == 0) else 5
                half = (kt // 2) % 2
                pss = k.ps[sp_i][:, half * 256:half * 256 + WQ]
                S.op("pe", (lambda e, pss=pss, jk=jk, kt=kt, h=h: e.matmul(
                    pss, KT[:, jk, kt * 128:(kt + 1) * 128], QT[:, h, :], start=True, stop=True)),
                    reads=[bKT, bQT], writes=[bS[(sp_i, half)]])
                pt, bpt = PT[pt_it[0] % 4]
                pt_it[0] += 1
                S.op("act", (lambda e, pt=pt, pss=pss: e.activation(out=pt, in_=pss, func=AF.Exp, scale=scale)),
                     reads=[bS[(sp_i, half)]], writes=[bpt])
                S.op("pe", (lambda e, pt=pt, jk=jk, kt=kt, po=po, nkt=nkt: e.matmul(
                    k.ps[po][:, 0:WQ], V[:, kt, jk * 128:(jk + 1) * 128], pt, start=(kt == 0), stop=(kt == nkt - 1))),
                    reads=[bV, bpt], writes=[k.bps[po]])
                S.op("pe", (lambda e, pt=pt, kt=kt, pr=pr, nkt=nkt: e.matmul(
                    k.ps[pr][:, 0:WQ], k.ones[:], pt, start=(kt == 0), stop=(kt == nkt - 1))),
                    reads=[bpt, k.bconst], writes=[k.bps[pr]])
            S.op("dve", (lambda e, pr=pr: e.reciprocal(out=RINV, in_=k.ps[pr][:, 0:WQ])), reads=[k.bps[pr]], writes=[bRINV])
            S.op("dve", (lambda e, po=po, h=h: e.tensor_tensor(out=AT[:, h, :], in0=k.ps[po][:, 0:WQ], in1=RINV, op=ALU.mult)),
                 reads=[k.bps[po], bRINV], writes=[bAT])
        for dcn in range(8):
            v, bw = next_w("wo", dcn)
            pi = 6 + dcn % 2
            pst = k.ps[pi][:, 0:WQ]
            for h in range(8):
                S.op("pe", (lambda e, v=v, h=h, pst=pst: e.matmul(pst, v[:, h, :], AT[:, h, :], start=(h == 0), stop=(h == 7))),
                     reads=[bw, bAT], writes=[k.bps[pi]])
            S.op("act", (lambda e, dcn=dcn, pst=pst: e.activation(out=OST[:, dcn, :], in_=pst, func=AF.Identity)),
                 reads=[k.bps[pi]], writes=[bOST])
        postnorm_update(k, X, bX, c0, c0 + WQ, l, 0, b, OST, bOST, scr)


def excl_segs(lo, hi, bad):
    if hi <= lo:
        return []
    if bad is None or bad < lo or bad >= hi:
        return [(lo, hi)]
    out = []
    if bad > lo:
        out.append((lo, bad))
    if bad + 1 < hi:
        out.append((bad + 1, hi))
    return out


def ffn_windows(cs, ce):
    n = ce - cs
    if n == NT:
        sizes = [460, 462, 460, 462, 460]
    else:
        sizes = [410, 410, 408, 410, 410]
    assert sum(sizes) == n
    out = []
    s = cs
    for w in sizes:
        out.append((s, s + w))
        s += w
    return out


def ffn_layer(k, b, X, bX, l, cs, ce):
    nc, S, d = k.nc, k.S, k.d
    AR, AFa = k.AR, k.AF
    AR.reset()
    AFa.reset()
    WM = 464
    HW = AR.take([8, WM]); bHW = Buf("HWf")
    G = AR.take([NFC, 462]); bG = [Buf(f"G{i}") for i in range(NFC)]
    NWU, NWD = 4, 3
    WB = AR.take([NWD * NFC * 128])
    bwu = [Buf(f"wu{i}") for i in range(NWU)]
    bwd = [Buf(f"wd{i}") for i in range(NWD)]
    wu = [(WB[:, i * 2048:(i + 1) * 2048].rearrange("p (a b) -> p a b", b=256), bwu[i], k.dsem_w[i]) for i in range(NWU)]
    wd = [(WB[:, i * 2816:(i + 1) * 2816].rearrange("p (a b) -> p a b", b=128), bwd[i], k.dsem_w[NWU + i]) for i in range(NWD)]

    def ovl(a0, a1, b0, b1):
        return a0 < b1 and b0 < a1
    wu_ov = [[bwd[j] for j in range(NWD) if ovl(i * 2048, (i + 1) * 2048, j * 2816, (j + 1) * 2816)] for i in range(NWU)]
    wd_ov = [[bwu[i] for i in range(NWU) if ovl(i * 2048, (i + 1) * 2048, j * 2816, (j + 1) * 2816)] for j in range(NWD)]
    HALO = AR.take([8, 2]); bHALO = Buf("halo")
    OST = AFa.take([8, 462]); bOST = Buf("OSTf")
    yv = [(AFa.take([WM]), Buf(f"yv{i}")) for i in range(2)]
    yg = [(AFa.take([WM]), Buf(f"yg{i}")) for i in range(2)]
    scr = make_scr(k, WM, 7)
    wins = ffn_windows(cs, ce)
    loads = []
    for wi_ in range(len(wins)):
        loads += [("u", fp_) for fp_ in range(NFC)] + [("d", dc_) for dc_ in range(8)]
    issued = [0]
    slots = {}
    cnt = {"u": 0, "d": 0}
    occ = {"u": [-1] * NWU, "d": [-1] * NWD}
    cons = [0]
    ovi_u = [[j for j in range(NWD) if ovl(i * 2048, (i + 1) * 2048, j * 2816, (j + 1) * 2816)] for i in range(NWU)]
    ovi_d = [[i for i in range(NWU) if ovl(i * 2048, (i + 1) * 2048, j * 2816, (j + 1) * 2816)] for j in range(NWD)]

    def issue(upto):
        while issued[0] < min(upto, len(loads)):
            kind, idx = loads[issued[0]]
            if kind == "u":
                i = cnt["u"] % NWU
                conf = [occ["u"][i]] + [occ["d"][j] for j in ovi_u[i]]
            else:
                i = cnt["d"] % NWD
                conf = [occ["d"][i]] + [occ["u"][j] for j in ovi_d[i]]
            if any(c_ >= cons[0] for c_ in conf):
                break
            cnt[kind] += 1
            occ[kind][i] = issued[0]
            if kind == "u":
                wv_, bw, ds = wu[i]
                S.op("pool", (lambda e, wv_=wv_, idx=idx: e.dma_start(out=wv_, in_=d.wup[l, idx])), writes=[bw] + wu_ov[i], dsem=ds)
            else:
                wv_, bw, ds = wd[i]
                S.op("pool", (lambda e, wv_=wv_, idx=idx: e.dma_start(out=wv_, in_=d.wdn[l, idx])), writes=[bw] + wd_ov[i], dsem=ds)
            slots[issued[0]] = (wv_, bw)
            issued[0] += 1
    PF = 4

    def next_w():
        issue(cons[0] + 1 + PF)
        assert cons[0] in slots, (cons[0], issued[0])
        r = slots.pop(cons[0])
        cons[0] += 1
        return r
    issue(PF)
    for wi, (s, e_) in enumerate(wins):
        W = e_ - s
        ms, me = max(s - 1, cs), min(e_ + 1, ce)
        if (me - ms) % 2 == 1:
            if me < ce:
                me += 1
            else:
                ms -= 1
        Wm = me - ms
        nleft = s - ms
        modnorm(k, X, bX, s, me, l, 1, b, HW, bHW, scr, ho=nleft)
        if nleft > 0:
            S.op("dve", (lambda e, nleft=nleft, s=s, e_=e_, ms=ms, W=W, Wm=Wm: e.tensor_copy(out=HW[:, :, 0:nleft], in_=HALO[:, :, 2 - nleft:2])),
                 reads=[bHALO], writes=[bHW])
        S.op("dve", (lambda e, s=s, e_=e_, ms=ms: e.tensor_copy(out=HALO[:, :, :], in_=HW[:, :, e_ - 2 - ms:e_ - ms])),
             reads=[bHW], writes=[bHALO])
        badl = LC if l == 0 else None
        badr = LC - 1 if l == 0 else None
        for fp in range(NFC):
            wv_, bw = next_w()
            par = fp % 2
            pv, pg = (0, 1) if par == 0 else (2, 3)
            for (pi, co) in ((pv, 0), (pg, 128)):
                for kk in range(8):
                    S.op("pe", (lambda e, pi=pi, co=co, kk=kk, wv_=wv_, s=s, e_=e_, ms=ms, W=W, Wm=Wm: e.matmul(
                        k.ps[pi][:, 0:Wm], wv_[:, kk, co:co + 128], HW[:, kk, 0:Wm], start=(kk == 0), stop=(kk == 7))),
                        reads=[bw, bHW], writes=[k.bps[pi]])
            for (pi, ci, (yt, byt)) in ((pv, fp, yv[par]), (pg, NFC + fp, yg[par])):
                psr = k.ps[pi]
                S.op("act", (lambda e, psr=psr, ci=ci, yt=yt, s=s, e_=e_, ms=ms, W=W, Wm=Wm: e.activation(
                    out=yt[:, 0:W], in_=psr[:, s - ms:e_ - ms], func=AF.Identity,
                    scale=k.cw[:, l, 1, ci:ci + 1], bias=k.cb[:, l, ci:ci + 1])),
                    reads=[k.bps[pi], k.bconst], writes=[byt])
                for (lo, hi) in excl_segs(max(s, cs + 1), e_, badl):
                    S.op("dve", (lambda e, psr=psr, ci=ci, yt=yt, lo=lo, hi=hi, s=s, e_=e_, ms=ms, W=W, Wm=Wm: e.scalar_tensor_tensor(
                        out=yt[:, lo - s:hi - s], in0=psr[:, lo - 1 - ms:hi - 1 - ms], scalar=k.cw[:, l, 0, ci:ci + 1],
                        in1=yt[:, lo - s:hi - s], op0=ALU.mult, op1=ALU.add)),
                        reads=[k.bps[pi], k.bconst, byt], writes=[byt])
                for (lo, hi) in excl_segs(s, min(e_, ce - 1), badr):
                    S.op("dve", (lambda e, psr=psr, ci=ci, yt=yt, lo=lo, hi=hi, s=s, e_=e_, ms=ms, W=W, Wm=Wm: e.scalar_tensor_tensor(
                        out=yt[:, lo - s:hi - s], in0=psr[:, lo + 1 - ms:hi + 1 - ms], scalar=k.cw[:, l, 2, ci:ci + 1],
                        in1=yt[:, lo - s:hi - s], op0=ALU.mult, op1=ALU.add)),
                        reads=[k.bps[pi], k.bconst, byt], writes=[byt])
            ygt, bygt = yg[par]
            yvt, byvt = yv[par]
            S.op("act", (lambda e, ygt=ygt, s=s, e_=e_, ms=ms, W=W, Wm=Wm: e.activation(out=ygt[:, 0:W], in_=ygt[:, 0:W], func=AF.Silu)),
                 reads=[bygt], writes=[bygt])
            S.op("pool", (lambda e, ygt=ygt, yvt=yvt, fp=fp, s=s, e_=e_, ms=ms, W=W, Wm=Wm: e.tensor_tensor(out=G[:, fp, 0:W], in0=yvt[:, 0:W], in1=ygt[:, 0:W],
                                                                                op=ALU.mult)),
                 reads=[bygt, byvt], writes=[bG[fp]])
        for dcn in range(8):
            wv_, bw = next_w()
            pi = 4 + dcn % 2
            for fc in range(NFC):
                S.op("pe", (lambda e, pi=pi, fc=fc, wv_=wv_, s=s, e_=e_, ms=ms, W=W, Wm=Wm: e.matmul(
                    k.ps[pi][:, 0:W], wv_[:, fc, :], G[:, fc, 0:W], start=(fc == 0), stop=(fc == NFC - 1))),
                    reads=[bw, bG[fc]], writes=[k.bps[pi]])
            S.op("act", (lambda e, pi=pi, dcn=dcn, s=s, e_=e_, ms=ms, W=W, Wm=Wm: e.activation(out=OST[:, dcn, 0:W], in_=k.ps[pi][:, 0:W], func=AF.Identity)),
                 reads=[k.bps[pi]], writes=[bOST])
        postnorm_update(k, X, bX, s, e_, l, 1, b, OST, bOST, scr)


def s5_layer(k, b, X, bX):
    nc, S, d = k.nc, k.S, k.d
    l = 1
    AR, AFa = k.AR, k.AF
    AR.reset()
    AFa.reset()
    st = k.stab
    bsp = Buf("xsp")
    for kk in range(8):
        S.op("sp", (lambda e, kk=kk: e.dma_start(out=d.xsp[:, kk, :], in_=X[:, kk, LC:NT])),
             reads=bX(LC, NT), writes=[bsp], dsem=k.dsem_x)
    Y = AR.take([8, LL]); bY = Buf("Y")
    o_ar = AR.o
    scr = make_scr(k, 256, 7)
    for w in range(NT // 256):
        c0 = w * 256
        modnorm(k, X, bX, c0, c0 + 256, l, 0, b, X, bX(c0, c0 + 256), scr, ho=c0)
    S.barrier()
    AR.o = o_ar
    AFa.reset()
    QN, SG = 4, 4
    NG = TCH // SG
    NS = 2
    bU = bX(0, NT)
    for kk in range(8):
        S.op("dve", (lambda e, kk=kk: e.tensor_scalar(out=Y[:, kk, :], in0=X[:, kk, LC:NT], scalar1=k.dsk[:, kk:kk + 1], scalar2=None,
                                                     op0=ALU.mult)), reads=bU + [k.bconst], writes=[bY])

    class Stream:
        pass

    sts = []
    for dd in range(NS):
        t = Stream()
        t.dd = dd
        t.BpT = AR.take([QN, 2, 128]); t.bBp = Buf(f"Bp{dd}")
        t.Cp = AR.take([QN, 2, 128]); t.bCp = Buf(f"Cp{dd}")
        t.BpF, t.CpF = t.BpT.bitcast(F32), t.Cp.bitcast(F32)
        t.XS = [AR.take([2 * QN * SG * 32]) for _ in range(2)]
        t.bXS = [Buf(f"XS{dd}0"), Buf(f"XS{dd}1")]
        t.XSF = [x.bitcast(F32) for x in t.XS]
        t.HsR = [AFa.take([SG + 1, QN, 32]) for _ in range(2)]
        t.HsI = [AFa.take([SG + 1, QN, 32]) for _ in range(2)]
        t.t2 = AFa.take([QN, 32]); t.bt2 = Buf(f"t2{dd}")
        t.t4 = AFa.take([QN, 32]); t.bt4 = Buf(f"t4{dd}")
        t.bHs = [Buf(f"Hs{dd}0"), Buf(f"Hs{dd}1")]
        t.HinR = AFa.take([QN, 32]); t.HinI = AFa.take([QN, 32]); t.bHin = Buf(f"Hin{dd}")
        t.h0R = AFa.take([QN]); t.h0I = AFa.take([QN]); t.bh0 = Buf(f"h0{dd}")
        t.hAR = AFa.take([QN]); t.hAI = AFa.take([QN]); t.bhA = Buf(f"hA{dd}")
        t.sm = [AFa.take([QN]) for _ in range(4)]; t.bsm = [Buf(f"sm{dd}{i}") for i in range(4)]
        t.XP = [k.ps[2 * dd], k.ps[2 * dd + 1]]
        t.bXP = [Buf(f"XPre{dd}"), Buf(f"XPim{dd}")]
        t.yps_i = 4 + dd
        t.bYps = Buf(f"Yps{dd}")
        t.gcount = 0
        t.islot = 0 if dd == 0 else SG
        t.eslot = SG if dd == 0 else 0
        t.lo_s = 1 if dd == 0 else 0
        sts.append(t)
    tC = AFa.take([QN, 128]); btC = Buf("tC")
    tD = AFa.take([QN, 128]); btD = Buf("tD")

    def cstep_small(t, oR, oI, hR, hI, eR, eI, reads, writes, ctxp=False):
        sm, bsm = t.sm, t.bsm
        aT, bT = (t.aX16, t.bX16) if ctxp else (t.aT16, t.bT16)
        S.op("dve", lambda e: e.tensor_tensor(out=sm[0], in0=aT, in1=hR, op=ALU.mult), reads=reads, writes=[bsm[0]])
        S.op("dve", lambda e: e.tensor_tensor(out=sm[1], in0=bT, in1=hI, op=ALU.mult), reads=reads, writes=[bsm[1]])
        S.op("pool", lambda e: e.tensor_tensor(out=sm[2], in0=bT, in1=hR, op=ALU.mult), reads=reads, writes=[bsm[2]])
        S.op("pool", lambda e: e.tensor_tensor(out=sm[3], in0=aT, in1=hI, op=ALU.mult), reads=reads, writes=[bsm[3]])
        S.op("dve", lambda e: e.tensor_tensor(out=sm[0], in0=sm[0], in1=sm[1], op=ALU.subtract), reads=[bsm[0], bsm[1]], writes=[bsm[0]])
        S.op("pool", lambda e: e.tensor_tensor(out=sm[2], in0=sm[2], in1=sm[3], op=ALU.add), reads=[bsm[2], bsm[3]], writes=[bsm[2]])
        S.op("dve", lambda e: e.tensor_tensor(out=oR, in0=sm[0], in1=eR, op=ALU.add), reads=[bsm[0]] + reads, writes=writes)
        S.op("pool", lambda e: e.tensor_tensor(out=oI, in0=sm[2], in1=eI, op=ALU.add), reads=[bsm[2]] + reads, writes=writes)

    def sweep(dc, cb0, nchk, use_init, contract, tch=TCH):
        Xl = X[:, dc, cb0:cb0 + tch * nchk].rearrange("p (c t) -> p c t", t=tch)
        Yl = Y[:, dc, :].rearrange("p (c t) -> p c t", t=TCH)

        def xsv(tt):
            return tt[:, 0:2 * QN * SG * nchk].rearrange("p (r q s c) -> p r q s c", r=2, q=QN, s=SG)

        for t in sts:
            hb = t.gcount % 2
            if not use_init:
                S.op("dve", (lambda e, t=t, hb=hb: e.memset(t.HsR[hb][:, t.islot, :, 0:nchk], 0.0)), writes=[t.bHs[hb]])
                S.op("pool", (lambda e, t=t, hb=hb: e.memset(t.HsI[hb][:, t.islot, :, 0:nchk], 0.0)), writes=[t.bHs[hb]])
            else:
                S.op("dve", (lambda e, t=t, hb=hb: e.tensor_copy(out=t.HsR[hb][:, t.islot, :, 0:nchk], in_=t.HinR[:, :, 0:nchk])),
                     reads=[t.bHin], writes=[t.bHs[hb]])
                S.op("pool", (lambda e, t=t, hb=hb: e.tensor_copy(out=t.HsI[hb][:, t.islot, :, 0:nchk], in_=t.HinI[:, :, 0:nchk])),
                     reads=[t.bHin], writes=[t.bHs[hb]])
        for g in range(tch // SG):
            ctx_ = []
            for t in sts:
                hb = t.gcount % 2
                xb = t.gcount % 2
                t.gcount += 1
                j_lo = g * SG if t.dd == 0 else tch - (g + 1) * SG
                rhs = Xl[:, :, j_lo:j_lo + SG].transpose([0, 2, 1])
                for ri in range(2):
                    xp = t.XP[ri][:, 0:QN * SG * nchk].rearrange("p (q s c) -> p q s c", s=SG, c=nchk)
                    for q in range(QN):
                        S.op("pe", (lambda e, t=t, xp=xp, q=q, ri=ri, rhs=rhs: e.matmul(
                            xp[:, q, :, :], t.BpF[:, q, ri, :], rhs, start=True, stop=True)),
                            reads=[t.bBp] + bU, writes=[t.bXP[ri]])
                    S.op("act", (lambda e, t=t, xp=xp, ri=ri, xb=xb: e.activation(
                        out=xsv(t.XS[xb])[:, ri, :, :, :], in_=xp[:, :, :, :], func=AF.Identity)),
                        reads=[t.bXP[ri]], writes=[t.bXS[xb]])
                ctx_.append((t, hb, xb, j_lo))
            for s_ in range(SG):
                ops = []
                for (t, hb, xb, j_lo) in ctx_:
                    rs, ws = (s_, s_ + 1) if t.dd == 0 else (SG - s_, SG - 1 - s_)
                    xs = s_ if t.dd == 0 else SG - 1 - s_
                    o = Stream()
                    o.t, o.hb, o.xb = t, hb, xb
                    o.a_bc = t.a16.unsqueeze(2).broadcast_to([128, QN, nchk])
                    o.b_bc = t.b16.unsqueeze(2).broadcast_to([128, QN, nchk])
                    o.hr, o.hi = t.HsR[hb][:, rs, :, 0:nchk], t.HsI[hb][:, rs, :, 0:nchk]
                    o.nr, o.ni = t.HsR[hb][:, ws, :, 0:nchk], t.HsI[hb][:, ws, :, 0:nchk]
                    o.xre, o.xim = xsv(t.XSF[xb])[:, 0, :, xs, :], xsv(t.XSF[xb])[:, 1, :, xs, :]
                    o.rd = [t.bHs[hb], k.bconst]
                    ops.append(o)
                for o in ops:
                    S.op("dve", (lambda e, o=o: e.tensor_tensor(out=o.nr, in0=o.a_bc, in1=o.hr, op=ALU.mult)), reads=o.rd, writes=[o.t.bHs[o.hb]])
                    S.op("dve", (lambda e, o=o: e.tensor_tensor(out=o.t.t2[:, :, 0:nchk], in0=o.b_bc, in1=o.hi, op=ALU.mult)), reads=o.rd, writes=[o.t.bt2])
                    S.op("pool", (lambda e, o=o: e.tensor_tensor(out=o.ni, in0=o.b_bc, in1=o.hr, op=ALU.mult)), reads=o.rd, writes=[o.t.bHs[o.hb]])
                    S.op("pool", (lambda e, o=o: e.tensor_tensor(out=o.t.t4[:, :, 0:nchk], in0=o.a_bc, in1=o.hi, op=ALU.mult)), reads=o.rd, writes=[o.t.bt4])
                for o in ops:
                    S.op("dve", (lambda e, o=o: e.tensor_tensor(out=o.nr, in0=o.nr, in1=o.t.t2[:, :, 0:nchk], op=ALU.subtract)),
                         reads=[o.t.bt2, o.t.bHs[o.hb]], writes=[o.t.bHs[o.hb]])
                    S.op("pool", (lambda e, o=o: e.tensor_tensor(out=o.ni, in0=o.ni, in1=o.t.t4[:, :, 0:nchk], op=ALU.add)),
                         reads=[o.t.bt4, o.t.bHs[o.hb]], writes=[o.t.bHs[o.hb]])
                for o in ops:
                    S.op("dve", (lambda e, o=o: e.tensor_tensor(out=o.nr, in0=o.nr, in1=o.xre, op=ALU.add)),
                         reads=[o.t.bXS[o.xb], o.t.bHs[o.hb]], writes=[o.t.bHs[o.hb]])
                    S.op("pool", (lambda e, o=o: e.tensor_tensor(out=o.ni, in0=o.ni, in1=o.xim, op=ALU.add)),
                         reads=[o.t.bXS[o.xb], o.t.bHs[o.hb]], writes=[o.t.bHs[o.hb]])
            for (t, hb, xb, j_lo) in ctx_:
                if contract:
                    yps = k.ps[t.yps_i][:, 0:SG * 32].rearrange("p (s c) -> p s c", c=32)
                    for q in range(QN):
                        S.op("pe", (lambda e, t=t, q=q, yps=yps, hb=hb: e.matmul(
                            yps[:, :, 0:nchk], t.CpF[:, q, 0, :], t.HsR[hb][:, t.lo_s:t.lo_s + SG, q, 0:nchk],
                            start=(q == 0), stop=False)), reads=[t.bCp, t.bHs[hb]], writes=[t.bYps])
                        S.op("pe", (lambda e, t=t, q=q, yps=yps, hb=hb: e.matmul(
                            yps[:, :, 0:nchk], t.CpF[:, q, 1, :], t.HsI[hb][:, t.lo_s:t.lo_s + SG, q, 0:nchk],
                            start=False, stop=(q == QN - 1))), reads=[t.bCp, t.bHs[hb]], writes=[t.bYps])
                    yv = Yl[:, :, j_lo:j_lo + SG].transpose([0, 2, 1])
                    S.op("dve", (lambda e, yv=yv, yps=yps: e.tensor_tensor(out=yv, in0=yv.bitcast(F32), in1=yps[:, :, 0:nchk],
                                                                            op=ALU.add)),
                         reads=[t.bYps, bY], writes=[bY])
                if g < tch // SG - 1:
                    nb_ = t.gcount % 2
                    S.op("dve", (lambda e, t=t, hb=hb, nb_=nb_: e.tensor_copy(out=t.HsR[nb_][:, t.islot, :, 0:nchk], in_=t.HsR[hb][:, t.eslot, :, 0:nchk])),
                         reads=[t.bHs[hb]], writes=[t.bHs[nb_]])
                    S.op("pool", (lambda e, t=t, hb=hb, nb_=nb_: e.tensor_copy(out=t.HsI[nb_][:, t.islot, :, 0:nchk], in_=t.HsI[hb][:, t.eslot, :, 0:nchk])),
                         reads=[t.bHs[hb]], writes=[t.bHs[nb_]])
        return [(t.gcount - 1) % 2 for t in sts]

    for dc in range(8):
        for t in sts:
            dd = t.dd
            S.op("pool", (lambda e, t=t, dd=dd, dc=dc: e.dma_start(out=t.BpT, in_=d.Bf[:, dd, dc * QN:dc * QN + QN])),
                 writes=[t.bBp], dsem=k.dsem_w[2 * dd])
            S.op("pool", (lambda e, t=t, dd=dd, dc=dc: e.dma_start(out=t.Cp, in_=d.Cf[:, dd, dc * QN:dc * QN + QN])),
                 writes=[t.bCp], dsem=k.dsem_w[2 * dd + 1])
            f0 = 2 * dc * QN + dd
            sl = slice(f0, f0 + 2 * QN - 1, 2)
            t.a16, t.b16 = st[:, 0, sl], st[:, 1, sl]
            t.aT16, t.bT16 = st[:, 2, sl], st[:, 3, sl]
            t.aX16, t.bX16 = st[:, 6, sl], st[:, 7, sl]
            fre = st[:, 4, sl].unsqueeze(2).broadcast_to([128, QN, 128])
            fim = st[:, 5, sl].unsqueeze(2).broadcast_to([128, QN, 128])
            cre, cim = t.CpF[:, :, 0, :], t.CpF[:, :, 1, :]
            Cp, bCp = t.Cp, t.bCp
            S.op("dve", (lambda e, fim=fim, cim=cim: e.tensor_tensor(out=tC, in0=cim, in1=fim, op=ALU.mult)), reads=[bCp, k.bconst], writes=[btC])
            S.op("dve", (lambda e, fre=fre, cre=cre: e.tensor_tensor(out=tD, in0=cre, in1=fre, op=ALU.mult)), reads=[bCp, k.bconst], writes=[btD])
            S.op("dve", lambda e: e.tensor_tensor(out=tD, in0=tD, in1=tC, op=ALU.subtract), reads=[btC, btD], writes=[btD])
            S.op("dve", (lambda e, fim=fim, cre=cre: e.tensor_tensor(out=tC, in0=cre, in1=fim, op=ALU.mult)), reads=[bCp, k.bconst, btD], writes=[btC])
            S.op("dve", (lambda e, fre=fre, cim=cim, Cp=Cp: e.tensor_tensor(out=Cp[:, :, 1, :], in0=cim, in1=fre, op=ALU.mult)), reads=[bCp, k.bconst], writes=[bCp])
            S.op("dve", (lambda e, cim=cim, Cp=Cp: e.scalar_tensor_tensor(out=Cp[:, :, 1, :], in0=cim, scalar=-1.0, in1=tC,
                                                                        op0=ALU.mult, op1=ALU.subtract)), reads=[btC, bCp], writes=[bCp])
            S.op("dve", (lambda e, Cp=Cp: e.tensor_copy(out=Cp[:, :, 0, :], in_=tD)), reads=[btD, bCp], writes=[bCp])
        hbs = sweep(dc, 0, LC // TCX, False, False, tch=TCX)
        for t, hb in zip(sts, hbs):
            S.op("dve", (lambda e, t=t: e.memset(t.h0R, 0.0)), writes=[t.bh0])
            S.op("dve", (lambda e, t=t: e.memset(t.h0I, 0.0)), writes=[t.bh0])
        ncx = LC // TCX
        for ci in range(ncx):
            for t, hb in zip(sts, hbs):
                c = ci if t.dd == 0 else ncx - 1 - ci
                cstep_small(t, t.hAR, t.hAI, t.h0R, t.h0I, t.HsR[hb][:, t.eslot, :, c], t.HsI[hb][:, t.eslot, :, c],
                            [t.bh0, t.bHs[hb], k.bconst], [t.bhA], ctxp=True)
            for t, hb in zip(sts, hbs):
                S.op("dve", (lambda e, t=t: e.tensor_copy(out=t.h0R, in_=t.hAR)), reads=[t.bhA], writes=[t.bh0])
                S.op("pool", (lambda e, t=t: e.tensor_copy(out=t.h0I, in_=t.hAI)), reads=[t.bhA], writes=[t.bh0])
        nl = LL // TCH
        hbs = sweep(dc, LC, nl, False, False)
        for t, hb in zip(sts, hbs):
            c_first = 0 if t.dd == 0 else nl - 1
            S.op("dve", (lambda e, t=t, c_first=c_first: e.tensor_copy(out=t.HinR[:, :, c_first], in_=t.h0R)), reads=[t.bh0], writes=[t.bHin])
            S.op("pool", (lambda e, t=t, c_first=c_first: e.tensor_copy(out=t.HinI[:, :, c_first], in_=t.h0I)), reads=[t.bh0], writes=[t.bHin])
        for ci in range(nl - 1):
            for t, hb in zip(sts, hbs):
                c, cn = (ci, ci + 1) if t.dd == 0 else (nl - 1 - ci, nl - 2 - ci)
                cstep_small(t, t.HinR[:, :, cn], t.HinI[:, :, cn], t.HinR[:, :, c], t.HinI[:, :, c],
                            t.HsR[hb][:, t.eslot, :, c], t.HsI[hb][:, t.eslot, :, c], [t.bHin, t.bHs[hb], k.bconst], [t.bHin])
        sweep(dc, LC, nl, True, True)
    S.barrier()
    for kk in range(8):
        S.op("act", (lambda e, kk=kk: e.activation(out=Y[:, kk, :], in_=Y[:, kk, :].bitcast(F32), func=AF.Gelu_apprx_tanh)),
             reads=[bY], writes=[bY])
    S.barrier()
    for kk in range(8):
        S.op("sp", (lambda e, kk=kk: e.dma_start(out=X[:, kk, LC:NT], in_=d.xsp[:, kk, :])),
             reads=[bsp], writes=bX(LC, NT), dsem=k.dsem_x)
    AR.o = o_ar
    AFa.reset()
    WG = 512
    wa = [(AR.take([8, 128]), Buf(f"wa{i}"), k.dsem_w[i]) for i in range(2)]
    wb = [(AR.take([8, 128]), Buf(f"wb{i}"), k.dsem_w[2 + i]) for i in range(2)]
    OST = AFa.take([8, WG]); bOST = Buf("OSTg")
    sg = [(AFa.take([WG]), Buf(f"sg{i}")) for i in range(2)]
    scr = make_scr(k, WG, 7)
    it = 0
    for w in range(LL // WG):
        c0 = w * WG
        for dcn in range(8):
            wa_, bwa, dsa = wa[it % 2]
            wb_, bwb, dsb = wb[it % 2]
            sg_, bsg = sg[it % 2]
            pa, pb = (0, 1) if it % 2 == 0 else (2, 3)
            it += 1
            S.op("pool", (lambda e, wa_=wa_, dcn=dcn: e.dma_start(out=wa_, in_=d.wglu[0, dcn])), writes=[bwa], dsem=dsa)
            S.op("pool", (lambda e, wb_=wb_, dcn=dcn: e.dma_start(out=wb_, in_=d.wglu[1, dcn])), writes=[bwb], dsem=dsb)
            for (pi, wt_, bw_) in ((pa, wa_, bwa), (pb, wb_, bwb)):
                for kk in range(8):
                    S.op("pe", (lambda e, pi=pi, wt_=wt_, kk=kk, c0=c0: e.matmul(
                        k.ps[pi][:, 0:WG], wt_[:, kk, :], Y[:, kk, c0:c0 + WG], start=(kk == 0), stop=(kk == 7))),
                        reads=[bw_, bY], writes=[k.bps[pi]])
            S.op("act", (lambda e, pb=pb, sg_=sg_: e.activation(out=sg_, in_=k.ps[pb][:, 0:WG], func=AF.Sigmoid)),
                 reads=[k.bps[pb]], writes=[bsg])
            S.op("dve", (lambda e, pa=pa, sg_=sg_, dcn=dcn: e.tensor_tensor(out=OST[:, dcn, :], in0=k.ps[pa][:, 0:WG], in1=sg_,
                                                                            op=ALU.mult)),
                 reads=[k.bps[pa], bsg], writes=[bOST])
        postnorm_update(k, X, bX, LC + c0, LC + c0 + WG, l, 0, b, OST, bOST, scr)


N_CORES = 8


def kernel(**inputs):
    inp = {k_: np.asarray(v) for k_, v in inputs.items()}
    sh = shared_inputs(inp)
    nc = build(stage=99, nb=2)
    in_maps = []
    for c in range(N_CORES):
        m = dict(sh)
        m.update(core_inputs(inp, c))
        in_maps.append(m)
    res = run_bass_kernel_spmd(nc, in_maps, core_ids=list(range(N_CORES)))
    out = np.empty((2 * N_CORES, LL, D), np.float32)
    for c in range(N_CORES):
        o = np.asarray(res.results[c]["outT"])
        out[2 * c] = o[0].T
        out[2 * c + 1] = o[1].T
    return out
```

```python
import contextlib
import math
import numpy as np
import concourse.bass as bass
import concourse.mybir as mybir
from concourse.bass_utils import run_bass_kernel_spmd

F32 = mybir.dt.float32
F32R = mybir.dt.float32r
AF = mybir.ActivationFunctionType
ALU = mybir.AluOpType

D = 1024
NT = 2304
LC = 256
LL = 2048
DFF = 2816
NFC = 22
EPS = 1e-6


class Buf:
    __slots__ = ("name", "w", "r")

    def __init__(self, name=""):
        self.name = name
        self.w = None
        self.r = {}


class DmaSem:
    def __init__(self, sched, name):
        self.key = ("dma", name)
        self.count = 0
        sched.dma_sems.append(self)


class Sched:
    COMPUTE = ("pe", "act", "dve", "pool")
    QUEUES = ("pe", "act", "dve", "pool", "sp")
    EPOCH = 12000

    def __init__(self):
        self.q = {e: [] for e in self.QUEUES}
        self.known = {e: {} for e in self.QUEUES}
        self.dma_sems = []
        self.pending = {e: {} for e in self.QUEUES}
        self.ec = {e: 0 for e in self.QUEUES}
        self.eng_keys = set()

    def _ekey(self, eng, count):
        ep = (count - 1) // self.EPOCH
        return ("eng", eng, ep), (count - 1) % self.EPOCH + 1

    def barrier(self):
        snap = {}
        for e in self.COMPUTE:
            if self.ec[e] > 0:
                key, val = self._ekey(e, self.ec[e])
                snap[key] = val
        for d in self.dma_sems:
            snap[d.key] = d.count
        for e in self.QUEUES:
            p = self.pending[e]
            for k, v in snap.items():
                if v > p.get(k, 0):
                    p[k] = v

    def op(self, eng, fn, reads=(), writes=(), dsem=None):
        q = self.q[eng]
        idx = len(q) + 1
        deps = dict(self.pending[eng])
        self.pending[eng] = {}
        for key in [kk for kk in deps if kk[0] == "eng" and kk[1] == eng]:
            del deps[key]

        def add(tok):
            if tok is None:
                return
            key, val, teng, tidx = tok
            if teng == eng and key[0] == "eng":
                if eng == "pe" or idx - tidx > 1:
                    return
            if deps.get(key, 0) < val:
                deps[key] = val

        for b in reads:
            add(b.w)
        for b in writes:
            add(b.w)
            for t in b.r.values():
                add(t)
        kn = self.known[eng]
        waits = []
        for key, val in deps.items():
            if val <= 0:
                continue
            if kn.get(key, 0) >= val:
                continue
            kn[key] = val
            waits.append((key, val))
        if dsem is not None:
            dsem.count += 16
            tok = (dsem.key, dsem.count, eng, idx)
            inc = (dsem.key, 16)
        else:
            self.ec[eng] += 1
            key, val = self._ekey(eng, self.ec[eng])
            self.eng_keys.add(key)
            tok = (key, val, eng, idx)
            inc = (key, 1)
        q.append((waits, fn, inc))
        for b in reads:
            b.r[(tok[0][0], tok[0][1])] = tok
        for b in writes:
            b.w = tok
            b.r = {}
        return tok

    def emit(self, nc, final_wait_tokens=()):
        with contextlib.ExitStack() as st:
            sems = {}
            for key in sorted(self.eng_keys):
                sems[key] = st.enter_context(nc.semaphore("s_%s_%d" % (key[1], key[2])))
            for d in self.dma_sems:
                sems[d.key] = st.enter_context(nc.semaphore("d_" + str(d.key[1])))
            block = st.enter_context(nc.Block())
            q = self.q

            def replay(eng_obj, name, extra_waits=()):
                for waits, fn, inc in q[name]:
                    for key, val in waits:
                        eng_obj.wait_ge(sems[key], val)
                    ins = fn(eng_obj)
                    ins.then_inc(sems[inc[0]], inc[1])
                for key, val in extra_waits:
                    eng_obj.wait_ge(sems[key], val)

            fw = [(t[0], t[1]) for t in final_wait_tokens]

            @block.tensor
            def _(e):
                replay(e, "pe")

            @block.scalar
            def _(e):
                replay(e, "act")

            @block.vector
            def _(e):
                replay(e, "dve")

            @block.gpsimd
            def _(e):
                replay(e, "pool")

            @block.sync
            def _(e):
                replay(e, "sp", fw)


def rope_tables():
    grid_w, head_dim, theta = 64, 128, 10000.0
    rows = LL // grid_w
    row = np.repeat(np.arange(rows, dtype=np.float32), grid_w)
    col = np.tile(np.arange(grid_w, dtype=np.float32), rows)
    ppa = head_dim // 4
    freq = (np.float32(theta) ** (-np.arange(ppa, dtype=np.float32) / np.float32(ppa))).astype(np.float32)
    ang = np.concatenate([row[:, None] * freq, col[:, None] * freq], axis=-1).astype(np.float32)
    cos = np.cos(ang).astype(np.float32).T
    sin = np.sin(ang).astype(np.float32).T
    return np.ascontiguousarray(np.concatenate([cos, cos], 0)), np.ascontiguousarray(np.concatenate([sin, sin], 0))


def pk(v):
    v = np.asarray(v, np.float32)
    lead = v.shape[:-1]
    m = v.shape[-1] // 128
    r = v.reshape(lead + (m, 128))
    r = np.moveaxis(r, -1, 0)
    return np.ascontiguousarray(r)


def shared_inputs(inp):
    s = {}
    s["w_mod"] = np.ascontiguousarray(inp["w_mod"], np.float32)
    s["bmT"] = pk(inp["b_mod"])
    g = np.stack([inp["g_pre_mix"], inp["g_post_mix"], inp["g_pre_ffn"], inp["g_post_ffn"]], 1)
    s["gT"] = pk(g)
    wqkv = np.asarray(inp["w_qkv"][0], np.float32)
    s["wqk"] = np.ascontiguousarray(wqkv[:, :1280].reshape(8, 128, 10, 128).transpose(2, 1, 0, 3))
    s["wv"] = np.ascontiguousarray(wqkv[:, 1280:].reshape(8, 128, 256).transpose(1, 0, 2))
    wo = np.asarray(inp["w_o"][0], np.float32)
    s["wo"] = np.ascontiguousarray(wo.reshape(8, 128, 8, 128).transpose(2, 1, 0, 3))
    s["qkg"] = np.ascontiguousarray(np.stack([inp["q_norm_g"][0], inp["k_norm_g"][0]], 1), np.float32)
    cos, sin = rope_tables()
    s["cos"], s["sin"] = cos, sin
    rot = np.zeros((128, 128), np.float32)
    for m in range(64):
        rot[m + 64, m] = -1.0
        rot[m, m + 64] = 1.0
    s["rotm"] = rot
    s["ones"] = np.ones((128, 128), np.float32)
    wup = np.asarray(inp["w_up"], np.float32)
    wu = wup.reshape(2, 8, 128, 2, NFC, 128)
    s["wup"] = np.ascontiguousarray(wu.transpose(0, 4, 2, 1, 3, 5)).reshape(2, NFC, 128, 8, 256)
    wdn = np.asarray(inp["w_down"], np.float32)
    s["wdn"] = np.ascontiguousarray(wdn.reshape(2, NFC, 128, 8, 128).transpose(0, 3, 2, 1, 4))
    s["cw"] = pk(inp["conv_w"])
    s["cb"] = pk(inp["conv_b"])
    lre = np.asarray(inp["ssm_lambda_re"][0], np.float32)
    lim = np.asarray(inp["ssm_lambda_im"][0], np.float32)
    lst = np.asarray(inp["ssm_log_step"][0], np.float32)

    def en_pd(a):
        return a.reshape(2, 32, 2, 64).transpose(2, 3, 1, 0).reshape(128, 64)

    lam = np.stack([en_pd(lre), en_pd(lim), en_pd(np.broadcast_to(lst[:, :, None], (2, 64, 64)))], 1)
    s["lam"] = np.ascontiguousarray(lam, np.float32)
    Bf = np.zeros((4, 2, 16, 2, 32, 2, 2, 64), np.float32)
    Cf = np.zeros((2, 64, 2, 32, 2, 4, 2, 16), np.float32)
    for ri, (bsrc, csrc) in enumerate(((inp["ssm_b_re"], inp["ssm_c_re"]), (inp["ssm_b_im"], inp["ssm_c_im"]))):
        bsrc = np.asarray(bsrc[0], np.float32)
        csrc = np.asarray(csrc[0], np.float32)
        for p_ in range(32):
            r_ = p_ % 4
            for e_ in range(2):
                g_ = 2 * p_ + e_
                Bf[r_, e_, :, :, p_, ri, e_, :] = bsrc[:, g_, :, :].transpose(2, 0, 1)
                Cf[e_, :, :, p_, ri, r_, e_, :] = csrc[:, g_, :, :].transpose(2, 0, 1)
    s["Bf"] = np.ascontiguousarray(Bf.reshape(128, 2, 32, 2, 128))
    s["Cf"] = np.ascontiguousarray(Cf.reshape(128, 2, 32, 2, 128))
    s["dsk"] = pk(inp["ssm_d"][0])
    wg = np.stack([inp["w_glu_a"][0], inp["w_glu_b"][0]]).astype(np.float32)
    s["wglu"] = np.ascontiguousarray(wg.reshape(2, 8, 128, 8, 128).transpose(0, 3, 2, 1, 4))
    return s


def core_inputs(inp, core):
    b0 = 2 * core
    m = {}
    xs = []
    for b in (b0, b0 + 1):
        xs.append(np.concatenate([inp["ctx"][b], inp["x"][b]], 0).T)
    m["xT"] = np.ascontiguousarray(np.stack(xs), np.float32)
    c3 = np.stack([inp["c"][b0], inp["c"][b0 + 1], inp["c_ctx"]], -1)
    m["cT"] = np.ascontiguousarray(c3.reshape(8, 128, 3).transpose(1, 0, 2), np.float32)
    return m


NAR = 24576
NAF = 8256
TCH = 64
TCX = 8
TWO_PI = 2.0 * math.pi
import os as _os
IMENG = _os.environ.get('IMENG', 'pool')


class K:
    pass


class XB:
    def __init__(self):
        self.b = [Buf(f"X{i}") for i in range(NT // 256)]

    def __call__(self, c0, c1):
        return self.b[c0 // 256:(c1 + 255) // 256]


class Arena:
    def __init__(self, t, n, dt):
        self.t, self.n, self.dt, self.o = t, n, dt, 0

    def reset(self):
        self.o = 0

    def take(self, shape):
        n = int(np.prod(shape))
        v = self.t[:, self.o:self.o + n]
        self.o += n
        assert self.o <= self.n, (self.o, self.n)
        if len(shape) == 2:
            v = v.rearrange("p (a b) -> p a b", b=shape[1])
        elif len(shape) == 3:
            v = v.rearrange("p (a b c) -> p a b c", b=shape[1], c=shape[2])
        elif len(shape) == 4:
            v = v.rearrange("p (a b c d) -> p a b c d", b=shape[1], c=shape[2], d=shape[3])
        return v


def build(stage=99, nb=2):
    nc = bass.Bass("TRN2", target_bir_lowering=False)
    S = Sched()
    k = K()
    k.nc, k.S = nc, S

    def din(name, shape, dt=F32):
        return nc.dram_tensor(name, list(shape), dt, kind="ExternalInput").ap()

    d = K()
    d.xT = din("xT", [2, D, NT])
    d.cT = din("cT", [128, 8, 3])
    d.w_mod = din("w_mod", [2, D, 6 * D])
    d.bmT = din("bmT", [128, 2, 48])
    d.gT = din("gT", [128, 2, 4, 8])
    d.wqk = din("wqk", [10, 128, 8, 128])
    d.wv = din("wv", [128, 8, 256])
    d.wo = din("wo", [8, 128, 8, 128])
    d.qkg = din("qkg", [128, 2])
    d.cos = din("cos", [128, LL])
    d.sin = din("sin", [128, LL])
    d.rotm = din("rotm", [128, 128])
    d.ones = din("ones", [128, 128])
    d.wup = din("wup", [2, NFC, 128, 8, 256])
    d.wdn = din("wdn", [2, 8, 128, NFC, 128])
    d.cw = din("cw", [128, 2, 3, 44])
    d.cb = din("cb", [128, 2, 44])
    d.lam = din("lam", [128, 3, 64])
    d.Bf = din("Bf", [128, 2, 32, 2, 128])
    d.Cf = din("Cf", [128, 2, 32, 2, 128])
    d.dsk = din("dsk", [128, 8])
    d.wglu = din("wglu", [2, 8, 128, 8, 128])
    d.out = nc.dram_tensor("outT", [2, D, LL], F32, kind="ExternalOutput").ap()
    if stage != 99:
        d.dbg = nc.dram_tensor("dbg", [128, 8, NT], F32, kind="ExternalOutput").ap()
    d.xsp = nc.dram_tensor("xspill", [128, 8, LL], F32, kind="Internal").ap()
    k.d = d

    with contextlib.ExitStack() as st:
        def sb(name, shape, dt=F32):
            return st.enter_context(nc.sbuf_tensor("sb_" + name, list(shape), dt))

        k.Xt = sb("X", [128, 8 * NT])
        k.ARt = sb("AR", [128, NAR], F32R)
        k.AFt = sb("AF", [128, NAF])
        k.AR = Arena(k.ARt, NAR, F32R)
        k.AF = Arena(k.AFt, NAF, F32)
        k.ones = sb("ones", [128, 128], F32R)
        k.rotm = sb("rotm", [128, 128], F32R)
        k.modv = sb("modv", [128, 2, 48, 3])
        k.gT = sb("gT", [128, 2, 4, 8])
        k.sA = sb("sA", [128, 2, 2, 3, 8])
        k.sG = sb("sG", [128, 2, 2, 3, 8])
        k.qkg = sb("qkg", [128, 2])
        k.cw = sb("cw", [128, 2, 3, 44])
        k.cb = sb("cb", [128, 2, 44])
        k.epsD = sb("epsD", [128, 1])
        k.eps128 = sb("eps128", [128, 1])
        k.bmT = sb("bmT", [128, 2, 48])
        k.dsk = sb("dsk", [128, 8])
        k.stab = sb("stab", [128, 8, 64])
        k.ps01 = st.enter_context(nc.psum_tensor("ps01", [128, 1024], F32))
        k.ps23 = st.enter_context(nc.psum_tensor("ps23", [128, 1024], F32))
        k.ps = [k.ps01[:, 0:512], k.ps01[:, 512:1024], k.ps23[:, 0:512], k.ps23[:, 512:1024]]
        k.ps += [st.enter_context(nc.psum_tensor(f"ps{i}", [128, 512], F32)) for i in range(4, 8)]
        k.bps = [Buf(f"ps{i}") for i in range(8)]
        k.bconst = Buf("const")
        k.dsem_c = DmaSem(S, "const")
        k.dsem_o = DmaSem(S, "out")
        k.dsem_x = DmaSem(S, "x")
        k.dsem_w = [DmaSem(S, f"w{i}") for i in range(8)]
        k.dsem_m = [DmaSem(S, f"m{i}") for i in range(2)]
        k.dsem_cs = DmaSem(S, "const_sw")

        prologue(k)
        fin = []
        for b in range(nb):
            fin += batch_element(k, b, stage)
        S.emit(nc, fin)
    k.ninstr = {e: len(q) for e, q in S.q.items()}
    print("instr counts", k.ninstr)
    return nc


def prologue(k):
    nc, S, d = k.nc, k.S, k.d
    bc = k.bconst
    dc = k.dsem_c
    S.op("pool", lambda e: e.dma_start(out=k.ones[:], in_=d.ones), writes=[bc], dsem=k.dsem_cs)
    S.op("pool", lambda e: e.dma_start(out=k.rotm[:], in_=d.rotm), writes=[bc], dsem=k.dsem_cs)
    for t, src in ((k.gT, d.gT), (k.qkg, d.qkg), (k.cw, d.cw), (k.cb, d.cb), (k.bmT, d.bmT), (k.dsk, d.dsk)):
        S.op("sp", (lambda e, t=t, src=src: e.dma_start(out=t[:], in_=src)), writes=[bc], dsem=dc)
    S.op("dve", lambda e: e.memset(k.epsD[:], EPS), writes=[bc])
    S.op("dve", lambda e: e.memset(k.eps128[:], EPS), writes=[bc])
    AFa = k.AF
    AFa.reset()
    cT = AFa.take([8, 3])
    sc = AFa.take([8, 3])
    bcT, bsc = Buf("cT"), Buf("sc")
    S.op("sp", lambda e: e.dma_start(out=cT, in_=d.cT), writes=[bcT], dsem=dc)
    S.op("act", lambda e: e.activation(out=sc, in_=cT, func=AF.Silu), reads=[bcT], writes=[bsc])
    import os
    PS = int(os.environ.get("PRO_STOP", "9"))
    if PS <= 1:
        S.barrier()
        return
    NP = 384
    wt = [(AFa.take([8, NP]), Buf(f"wm{i}")) for i in range(2)]
    bmod = Buf("modv")
    it = 0
    for l in range(2):
        for j in range(6 * D // NP):
            wv_, wb = wt[it % 2]
            ds = k.dsem_m[it % 2]
            src = d.w_mod[l, :, j * NP:(j + 1) * NP].rearrange("(kk p) n -> p kk n", p=128)
            S.op("sp", (lambda e, wv_=wv_, src=src: e.dma_start(out=wv_, in_=src)), writes=[wb], dsem=ds)
            for c in range(NP // 128):
                oc = j * (NP // 128) + c
                pst = k.ps[l][:, oc * 3:oc * 3 + 3]
                for kk in range(8):
                    S.op("pe", (lambda e, pst=pst, wv_=wv_, c=c, kk=kk: e.matmul(
                        pst, wv_[:, kk, c * 128:(c + 1) * 128], sc[:, kk, :], start=(kk == 0), stop=(kk == 7))),
                        reads=[wb, bsc], writes=[k.bps[l]])
            it += 1
        S.op("dve", (lambda e, l=l: e.tensor_tensor(
            out=k.modv[:, l, :, :], in0=k.ps[l][:, 0:144].rearrange("p (a b) -> p a b", b=3),
            in1=k.bmT[:, l, :].unsqueeze(2).broadcast_to([128, 48, 3]), op=ALU.add)),
            reads=[k.bps[l], bc], writes=[bmod])
    if PS <= 2:
        S.barrier()
        return
    for l in range(2):
        for j, (gi, si) in enumerate(((0, 1), (2, 4))):
            for r in range(3):
                S.op("dve", (lambda e, l=l, j=j, gi=gi, si=si, r=r: e.scalar_tensor_tensor(
                    out=k.sA[:, l, j, r, :], in0=k.modv[:, l, si * 8:(si + 1) * 8, r], scalar=1.0,
                    in1=k.gT[:, l, gi, :], op0=ALU.add, op1=ALU.mult)), reads=[bmod, bc], writes=[bc])
        for j, (gi, si) in enumerate(((1, 2), (3, 5))):
            for r in range(3):
                S.op("dve", (lambda e, l=l, j=j, gi=gi, si=si, r=r: e.tensor_tensor(
                    out=k.sG[:, l, j, r, :], in0=k.modv[:, l, si * 8:(si + 1) * 8, r],
                    in1=k.gT[:, l, gi, :], op=ALU.mult)), reads=[bmod, bc], writes=[bc])
    S.barrier()
    if PS <= 3:
        return
    s5_tables(k)
    S.barrier()


def s5_tables(k):
    S, d = k.S, k.d
    AFa = k.AF
    AFa.reset()
    lam = AFa.take([3, 64])
    w = [AFa.take([64]) for _ in range(10)]
    bt = Buf("s5t")
    S.op("sp", lambda e: e.dma_start(out=lam, in_=d.lam), writes=[bt], dsem=k.dsem_c)
    lre, lim, lst = lam[:, 0, :], lam[:, 1, :], lam[:, 2, :]
    dt_, xr, xi, mag, tq, tr, cs, sn, den, t2 = w
    st = k.stab

    def dve(fn):
        S.op("dve", fn, reads=[bt], writes=[bt])

    def act(fn):
        S.op("act", fn, reads=[bt], writes=[bt])

    act(lambda e: e.activation(out=dt_, in_=lst, func=AF.Exp))
    dve(lambda e: e.tensor_tensor(out=xr, in0=lre, in1=dt_, op=ALU.mult))
    dve(lambda e: e.tensor_tensor(out=xi, in0=lim, in1=dt_, op=ALU.mult))
    MAGIC = 12582912.0

    def sincos(mult, out_c, out_s):
        for off, dst in ((0.0, out_s), (math.pi / 2, out_c)):
            dve(lambda e, off=off: e.tensor_scalar(out=tq, in0=xi, scalar1=float(mult), scalar2=float(off),
                                                    op0=ALU.mult, op1=ALU.add))
            dve(lambda e: e.tensor_scalar(out=tr, in0=tq, scalar1=1.0 / TWO_PI, scalar2=MAGIC,
                                          op0=ALU.mult, op1=ALU.add))
            dve(lambda e: e.tensor_scalar(out=tr, in0=tr, scalar1=MAGIC, scalar2=None, op0=ALU.subtract))
            dve(lambda e: e.scalar_tensor_tensor(out=tq, in0=tr, scalar=-TWO_PI, in1=tq, op0=ALU.mult, op1=ALU.add))
            dve(lambda e: e.tensor_scalar(out=tq, in0=tq, scalar1=3.1415925, scalar2=-3.1415925,
                                          op0=ALU.min, op1=ALU.max))
            act(lambda e, dst=dst: e.activation(out=dst, in_=tq, func=AF.Sin))

    for mult, ia, ib in ((1.0, 0, 1), (float(TCH), 2, 3), (float(TCX), 6, 7)):
        sincos(mult, cs, sn)
        act(lambda e, mult=mult: e.activation(out=mag, in_=xr, func=AF.Exp, scale=float(mult)))
        dve(lambda e, ia=ia: e.tensor_tensor(out=st[:, ia, :], in0=mag, in1=cs, op=ALU.mult))
        dve(lambda e, ib=ib: e.tensor_tensor(out=st[:, ib, :], in0=mag, in1=sn, op=ALU.mult))
    dve(lambda e: e.tensor_tensor(out=den, in0=lre, in1=lre, op=ALU.mult))
    dve(lambda e: e.tensor_tensor(out=t2, in0=lim, in1=lim, op=ALU.mult))
    dve(lambda e: e.tensor_tensor(out=den, in0=den, in1=t2, op=ALU.add))
    dve(lambda e: e.reciprocal(out=den, in_=den))
    dve(lambda e: e.tensor_scalar(out=tq, in0=st[:, 0, :], scalar1=-1.0, scalar2=None, op0=ALU.add))
    dve(lambda e: e.tensor_tensor(out=tr, in0=tq, in1=lre, op=ALU.mult))
    dve(lambda e: e.tensor_tensor(out=t2, in0=st[:, 1, :], in1=lim, op=ALU.mult))
    dve(lambda e: e.tensor_tensor(out=tr, in0=tr, in1=t2, op=ALU.add))
    dve(lambda e: e.tensor_tensor(out=st[:, 4, :], in0=tr, in1=den, op=ALU.mult))
    dve(lambda e: e.tensor_tensor(out=tr, in0=st[:, 1, :], in1=lre, op=ALU.mult))
    dve(lambda e: e.tensor_tensor(out=t2, in0=tq, in1=lim, op=ALU.mult))
    dve(lambda e: e.tensor_tensor(out=tr, in0=tr, in1=t2, op=ALU.subtract))
    dve(lambda e: e.tensor_tensor(out=st[:, 5, :], in0=tr, in1=den, op=ALU.mult))
    S.op("dve", lambda e: e.tensor_copy(out=cs, in_=st[:, 5, :]), reads=[bt], writes=[bt, k.bconst])


def segs_rows(c0, c1, b):
    out = []
    if c0 < LC:
        out.append((c0, min(c1, LC), 2))
    if c1 > LC:
        out.append((max(c0, LC), c1, b))
    return out


def norm_stats(k, src_fn, reads, W, scr):
    S = k.S
    pi = scr["ps"]
    pst = k.ps[pi][:, 0:W]
    for kk in range(8):
        sq, bsq = scr["sq"][kk % 2]
        S.op("act", (lambda e, sq=sq, kk=kk: e.activation(out=sq[:, 0:W], in_=src_fn(kk), func=AF.Square)),
             reads=reads, writes=[bsq])
        S.op("pe", (lambda e, sq=sq, kk=kk: e.matmul(pst, k.ones[:], sq[:, 0:W], start=(kk == 0), stop=(kk == 7))),
             reads=[bsq, k.bconst], writes=[k.bps[pi]])
    rstd, brs = scr["rstd"]
    S.op("act", lambda e: e.activation(out=rstd[:, 0:W], in_=pst, func=AF.Sqrt, scale=1.0 / D, bias=k.epsD[:]),
         reads=[k.bps[pi], k.bconst], writes=[brs])
    S.op("dve", lambda e: e.reciprocal(out=rstd[:, 0:W], in_=rstd[:, 0:W]), reads=[brs], writes=[brs])
    return rstd, brs


def modnorm(k, X, bX, c0, c1, l, j, b, hout, bh, scr, ho=0):
    S = k.S
    W = c1 - c0
    Ws = W + (W % 2)
    assert c0 + Ws <= NT
    rstd, brs = norm_stats(k, lambda kk: X[:, kk, c0:c0 + Ws], bX(c0, c0 + Ws), Ws, scr)
    shift_i = 0 if j == 0 else 3
    for kk in range(8):
        tmp, btmp = scr["tmp"][kk % 2]
        S.op("dve", (lambda e, tmp=tmp, kk=kk: e.tensor_tensor(out=tmp[:, 0:W], in0=X[:, kk, c0:c1], in1=rstd[:, 0:W],
                                                               op=ALU.mult)), reads=bX(c0, c1) + [brs], writes=[btmp])
        for (lo, hi, r) in segs_rows(c0, c1, b):
            S.op("act", (lambda e, tmp=tmp, kk=kk, lo=lo, hi=hi, r=r: e.activation(
                out=hout[:, kk, ho + lo - c0:ho + hi - c0], in_=tmp[:, lo - c0:hi - c0], func=AF.Identity,
                scale=k.sA[:, l, j, r, kk:kk + 1], bias=k.modv[:, l, shift_i * 8 + kk, r:r + 1])),
                reads=[btmp, k.bconst], writes=(bh if isinstance(bh, list) else [bh]))


def postnorm_update(k, X, bX, c0, c1, l, j, b, ost, bost, scr):
    S = k.S
    W = c1 - c0
    rstd, brs = norm_stats(k, lambda kk: ost[:, kk, 0:W], [bost], W, scr)
    for kk in range(8):
        tmp, btmp = scr["tmp"][kk % 2]
        S.op("dve", (lambda e, tmp=tmp, kk=kk: e.tensor_tensor(out=tmp[:, 0:W], in0=ost[:, kk, 0:W], in1=rstd[:, 0:W],
                                                               op=ALU.mult)), reads=[bost, brs], writes=[btmp])
        for (lo, hi, r) in segs_rows(c0, c1, b):
            S.op("dve", (lambda e, tmp=tmp, kk=kk, lo=lo, hi=hi, r=r: e.scalar_tensor_tensor(
                out=X[:, kk, lo:hi], in0=tmp[:, lo - c0:hi - c0], scalar=k.sG[:, l, j, r, kk:kk + 1],
                in1=X[:, kk, lo:hi], op0=ALU.mult, op1=ALU.add)), reads=[btmp, k.bconst] + bX(lo, hi), writes=bX(lo, hi))


def dump_x(k, X, bX, fin):
    S, d = k.S, k.d
    for kk in range(8):
        fin.append(S.op("sp", (lambda e, kk=kk: e.dma_start(out=d.dbg[:, kk, :], in_=X[:, kk, :])),
                        reads=bX(0, NT), dsem=k.dsem_o))


def batch_element(k, b, stage):
    nc, S, d = k.nc, k.S, k.d
    X = k.Xt[:, :].rearrange("p (a t) -> p a t", t=NT)
    bX = XB()
    for kk in range(8):
        S.op("sp", (lambda e, kk=kk: e.dma_start(out=X[:, kk, :], in_=d.xT[b, kk * 128:(kk + 1) * 128, :])),
             writes=bX(0, NT), dsem=k.dsem_x)
    fin = []
    if stage == 0:
        dump_x(k, X, bX, fin)
        S.barrier()
        return fin
    import os
    if os.environ.get("S5_ONLY") == "1":
        s5_layer(k, b, X, bX)
        S.barrier()
        dump_x(k, X, bX, fin)
        S.barrier()
        return fin
    attention_layer(k, b, X, bX)
    S.barrier()
    if stage == 1:
        if b == 0:
            dump_x(k, X, bX, fin)
        S.barrier()
        return fin
    ffn_layer(k, b, X, bX, 0, 0, NT)
    S.barrier()
    if stage == 2:
        if b == 0:
            dump_x(k, X, bX, fin)
        S.barrier()
        return fin
    s5_layer(k, b, X, bX)
    S.barrier()
    if stage == 3:
        if b == 0:
            dump_x(k, X, bX, fin)
        S.barrier()
        return fin
    ffn_layer(k, b, X, bX, 1, LC, NT)
    S.barrier()
    for kk in range(8):
        fin.append(S.op("sp", (lambda e, kk=kk: e.dma_start(out=d.out[b, kk * 128:(kk + 1) * 128, :], in_=X[:, kk, LC:NT])),
                        reads=bX(LC, NT), dsem=k.dsem_o))
    S.barrier()
    return fin


def make_scr(k, W, ps):
    AR, AFa = k.AR, k.AF
    scr = {"ps": ps, "sq": [], "tmp": []}
    for i in range(2):
        scr["sq"].append((AR.take([W]), Buf(f"sq{i}")))
    for i in range(2):
        scr["tmp"].append((AFa.take([W]), Buf(f"tmp{i}")))
    scr["rstd"] = (AFa.take([W]), Buf("rstd"))
    return scr


def attention_layer(k, b, X, bX):
    nc, S, d = k.nc, k.S, k.d
    l = 0
    WQ = 256
    AR, AFa = k.AR, k.AF
    AR.reset()
    AFa.reset()
    KT = AR.take([2, NT])
    V = AR.take([18, 256])
    AT = AR.take([8, WQ])
    HW = AR.take([8, WQ])
    QT = AR.take([8, WQ])
    bKT, bV, bAT, bHW, bQT = Buf("KT"), Buf("V"), Buf("AT"), Buf("HW"), Buf("QT")
    NRB = 4
    ring = [(AR.take([8, 128]), Buf(f"wr{i}"), k.dsem_w[i]) for i in range(NRB)]
    loads = []
    for w_ in range(NT // WQ):
        loads += [("qk", 8), ("qk", 9)]
    for w_ in range(NT // WQ):
        loads += [("qk", h_) for h_ in range(8)] + [("wo", dc_) for dc_ in range(8)]
    issued = [0]
    cons = [0]

    def issue(upto):
        while issued[0] < min(upto, len(loads)):
            kind, idx = loads[issued[0]]
            v, bw, ds = ring[issued[0] % NRB]
            src = d.wqk[idx] if kind == "qk" else d.wo[idx]
            S.op("pool", (lambda e, v=v, src=src: e.dma_start(out=v, in_=src)), writes=[bw], dsem=ds)
            issued[0] += 1

    def next_w(kind, idx):
        c = cons[0]
        assert loads[c] == (kind, idx), (loads[c], kind, idx)
        issue(c + NRB)
        cons[0] += 1
        v, bw, ds = ring[c % NRB]
        return v, bw
    WV = AR.take([8, 256]); bWV = Buf("WV")
    PT = [(AR.take([WQ]), Buf(f"PT{i}")) for i in range(4)]
    KNs = [(AR.take([WQ]), Buf(f"KN{i}")) for i in range(2)]
    SQ2s = [(AR.take([WQ]), Buf(f"SQ2{i}")) for i in range(2)]
    COS = AFa.take([LL])
    SIN = AFa.take([LL])
    OST = AFa.take([8, WQ]); bOST = Buf("OST")
    RS2s = [(AFa.take([WQ]), Buf(f"RS2{i}")) for i in range(2)]
    T2s = [(AFa.take([WQ]), Buf(f"T2{i}")) for i in range(2)]
    hcount = [0]
    RINV = AFa.take([WQ]); bRINV = Buf("RINV")
    scr = make_scr(k, WQ, 7)
    brope = Buf("rope")
    S.op("sp", lambda e: e.dma_start(out=COS, in_=d.cos), writes=[brope], dsem=k.dsem_c)
    S.op("sp", lambda e: e.dma_start(out=SIN, in_=d.sin), writes=[brope], dsem=k.dsem_c)
    S.op("pool", lambda e: e.dma_start(out=WV, in_=d.wv), writes=[bWV], dsem=k.dsem_w[5])
    issue(NRB - 1)
    bS = {(4, 0): Buf("S40"), (4, 1): Buf("S41"), (5, 0): Buf("S50"), (5, 1): Buf("S51")}
    def qk_head(ci, gcol, c0, lat, dst, bdst):
        wv_, bw = next_w("qk", ci)
        par = hcount[0] % 2
        hcount[0] += 1
        pi = 6 + par
        KN, bKN = KNs[par]
        SQ2, bSQ2 = SQ2s[par]
        RS2, bRS2 = RS2s[par]
        T2, bT2 = T2s[par]
        pst = k.ps[pi][:, 0:WQ]
        for kk in range(8):
            S.op("pe", (lambda e, kk=kk: e.matmul(pst, wv_[:, kk, :], HW[:, kk, :], start=(kk == 0), stop=(kk == 7))),
                 reads=[bw, bHW], writes=[k.bps[pi]])
        S.op("act", lambda e: e.activation(out=SQ2, in_=pst, func=AF.Square), reads=[k.bps[pi]], writes=[bSQ2])
        p2 = k.ps[pi][:, 256:256 + WQ]
        S.op("pe", lambda e: e.matmul(p2, k.ones[:], SQ2, start=True, stop=True), reads=[bSQ2, k.bconst],
             writes=[k.bps[pi]])
        S.op("act", lambda e: e.activation(out=RS2, in_=p2, func=AF.Sqrt, scale=1.0 / 128, bias=k.eps128[:]),
             reads=[k.bps[pi], k.bconst], writes=[bRS2])
        S.op("dve", lambda e: e.reciprocal(out=RS2, in_=RS2), reads=[bRS2], writes=[bRS2])
        if not lat:
            S.op("dve", lambda e: e.scalar_tensor_tensor(out=dst, in0=pst, scalar=k.qkg[:, gcol:gcol + 1], in1=RS2,
                                                         op0=ALU.mult, op1=ALU.mult),
                 reads=[k.bps[pi], bRS2, k.bconst], writes=[bdst])
            return
        S.op("dve", lambda e: e.scalar_tensor_tensor(out=KN, in0=pst, scalar=k.qkg[:, gcol:gcol + 1], in1=RS2,
                                                     op0=ALU.mult, op1=ALU.mult),
             reads=[k.bps[pi], bRS2, k.bconst], writes=[bKN])
        S.op("pe", lambda e: e.matmul(p2, k.rotm[:], KN, start=True, stop=True), reads=[bKN, k.bconst, bRS2],
             writes=[k.bps[pi]])
        t0 = c0 - LC
        S.op("dve", lambda e: e.tensor_tensor(out=T2, in0=p2, in1=SIN[:, t0:t0 + WQ], op=ALU.mult),
             reads=[k.bps[pi], brope], writes=[bT2])
        S.op("dve", lambda e: e.tensor_tensor(out=RS2, in0=KN.bitcast(F32), in1=COS[:, t0:t0 + WQ], op=ALU.mult),
             reads=[bKN, brope], writes=[bRS2])
        S.op("dve", lambda e: e.tensor_tensor(out=dst, in0=RS2, in1=T2, op=ALU.add), reads=[bRS2, bT2], writes=[bdst])

    for w in range(NT // WQ):
        c0 = w * WQ
        lat = c0 >= LC
        modnorm(k, X, bX, c0, c0 + WQ, l, 0, b, HW, bHW, scr)
        for jk in range(2):
            qk_head(8 + jk, 1, c0, lat, KT[:, jk, c0:c0 + WQ], bKT)
        for sub in range(2):
            pi = 5
            pst = k.ps[pi][:, sub * 256:(sub + 1) * 256]
            for kk in range(8):
                S.op("pe", (lambda e, kk=kk, sub=sub, pst=pst: e.matmul(
                    pst, HW[:, kk, sub * 128:(sub + 1) * 128], WV[:, kk, :], start=(kk == 0), stop=(kk == 7))),
                    reads=[bHW, bWV], writes=[k.bps[pi]])
            S.op("act", (lambda e, sub=sub, pst=pst, w=w: e.activation(out=V[:, 2 * w + sub, :], in_=pst, func=AF.Identity)),
                 reads=[k.bps[pi]], writes=[bV])
    scale = 1.0 / math.sqrt(128.0)
    wo_it = [0]
    pt_it = [0]
    for w in range(NT // WQ):
        c0 = w * WQ
        lat = c0 >= LC
        nkt = 18 if lat else 2
        modnorm(k, X, bX, c0, c0 + WQ, l, 0, b, HW, bHW, scr)
        for h in range(8):
            qk_head(h, 0, c0, lat, QT[:, h, :], bQT)
        for h in range(8):
            jk = h // 4
            po, pr = (0, 1) if h % 2 == 0 else (2, 3)
            for kt in range(nkt):
                sp_i = 4 if (kt % 2 == 0) else 5
                half = (kt // 2) % 2
                pss = k.ps[sp_i][:, half * 256:half * 256 + WQ]
                S.op("pe", (lambda e, pss=pss, jk=jk, kt=kt, h=h: e.matmul(
                    pss, KT[:, jk, kt * 128:(kt + 1) * 128], QT[:, h, :], start=True, stop=True)),
                    reads=[bKT, bQT], writes=[bS[(sp_i, half)]])
                pt, bpt = PT[pt_it[0] % 4]
                pt_it[0] += 1
                S.op("act", (lambda e, pt=pt, pss=pss: e.activation(out=pt, in_=pss, func=AF.Exp, scale=scale)),
                     reads=[bS[(sp_i, half)]], writes=[bpt])
                S.op("pe", (lambda e, pt=pt, jk=jk, kt=kt, po=po, nkt=nkt: e.matmul(
                    k.ps[po][:, 0:WQ], V[:, kt, jk * 128:(jk + 1) * 128], pt, start=(kt == 0), stop=(kt == nkt - 1))),
                    reads=[bV, bpt], writes=[k.bps[po]])
                S.op("pe", (lambda e, pt=pt, kt=kt, pr=pr, nkt=nkt: e.matmul(
                    k.ps[pr][:, 0:WQ], k.ones[:], pt, start=(kt == 0), stop=(kt == nkt - 1))),
                    reads=[bpt, k.bconst], writes=[k.bps[pr]])
            S.op("dve", (lambda e, pr=pr: e.reciprocal(out=RINV, in_=k.ps[pr][:, 0:WQ])), reads=[k.bps[pr]], writes=[bRINV])
            S.op("dve", (lambda e, po=po, h=h: e.tensor_tensor(out=AT[:, h, :], in0=k.ps[po][:, 0:WQ], in1=RINV, op=ALU.mult)),
                 reads=[k.bps[po], bRINV], writes=[bAT])
        for dcn in range(8):
            v, bw = next_w("wo", dcn)
            pi = 6 + dcn % 2
            pst = k.ps[pi][:, 0:WQ]
            for h in range(8):
                S.op("pe", (lambda e, v=v, h=h, pst=pst: e.matmul(pst, v[:, h, :], AT[:, h, :], start=(h == 0), stop=(h == 7))),
                     reads=[bw, bAT], writes=[k.bps[pi]])
            S.op("act", (lambda e, dcn=dcn, pst=pst: e.activation(out=OST[:, dcn, :], in_=pst, func=AF.Identity)),
                 reads=[k.bps[pi]], writes=[bOST])
        postnorm_update(k, X, bX, c0, c0 + WQ, l, 0, b, OST, bOST, scr)


def excl_segs(lo, hi, bad):
    if hi <= lo:
        return []
    if bad is None or bad < lo or bad >= hi:
        return [(lo, hi)]
    out = []
    if bad > lo:
        out.append((lo, bad))
    if bad + 1 < hi:
        out.append((bad + 1, hi))
    return out


def ffn_windows(cs, ce):
    n = ce - cs
    if n == NT:
        sizes = [460, 462, 460, 462, 460]
    else:
        sizes = [410, 410, 408, 410, 410]
    assert sum(sizes) == n
    out = []
    s = cs
    for w in sizes:
        out.append((s, s + w))
        s += w
    return out


def ffn_layer(k, b, X, bX, l, cs, ce):
    nc, S, d = k.nc, k.S, k.d
    AR, AFa = k.AR, k.AF
    AR.reset()
    AFa.reset()
    WM = 464
    HW = AR.take([8, WM]); bHW = Buf("HWf")
    G = AR.take([NFC, 462]); bG = [Buf(f"G{i}") for i in range(NFC)]
    NWU, NWD = 4, 3
    WB = AR.take([NWD * NFC * 128])
    bwu = [Buf(f"wu{i}") for i in range(NWU)]
    bwd = [Buf(f"wd{i}") for i in range(NWD)]
    wu = [(WB[:, i * 2048:(i + 1) * 2048].rearrange("p (a b) -> p a b", b=256), bwu[i], k.dsem_w[i]) for i in range(NWU)]
    wd = [(WB[:, i * 2816:(i + 1) * 2816].rearrange("p (a b) -> p a b", b=128), bwd[i], k.dsem_w[NWU + i]) for i in range(NWD)]

    def ovl(a0, a1, b0, b1):
        return a0 < b1 and b0 < a1
    wu_ov = [[bwd[j] for j in range(NWD) if ovl(i * 2048, (i + 1) * 2048, j * 2816, (j + 1) * 2816)] for i in range(NWU)]
    wd_ov = [[bwu[i] for i in range(NWU) if ovl(i * 2048, (i + 1) * 2048, j * 2816, (j + 1) * 2816)] for j in range(NWD)]
    HALO = AR.take([8, 2]); bHALO = Buf("halo")
    OST = AFa.take([8, 462]); bOST = Buf("OSTf")
    yv = [(AFa.take([WM]), Buf(f"yv{i}")) for i in range(2)]
    yg = [(AFa.take([WM]), Buf(f"yg{i}")) for i in range(2)]
    scr = make_scr(k, WM, 7)
    wins = ffn_windows(cs, ce)
    loads = []
    for wi_ in range(len(wins)):
        loads += [("u", fp_) for fp_ in range(NFC)] + [("d", dc_) for dc_ in range(8)]
    issued = [0]
    slots = {}
    cnt = {"u": 0, "d": 0}
    occ = {"u": [-1] * NWU, "d": [-1] * NWD}
    cons = [0]
    ovi_u = [[j for j in range(NWD) if ovl(i * 2048, (i + 1) * 2048, j * 2816, (j + 1) * 2816)] for i in range(NWU)]
    ovi_d = [[i for i in range(NWU) if ovl(i * 2048, (i + 1) * 2048, j * 2816, (j + 1) * 2816)] for j in range(NWD)]

    def issue(upto):
        while issued[0] < min(upto, len(loads)):
            kind, idx = loads[issued[0]]
            if kind == "u":
                i = cnt["u"] % NWU
                conf = [occ["u"][i]] + [occ["d"][j] for j in ovi_u[i]]
            else:
                i = cnt["d"] % NWD
                conf = [occ["d"][i]] + [occ["u"][j] for j in ovi_d[i]]
            if any(c_ >= cons[0] for c_ in conf):
                break
            cnt[kind] += 1
            occ[kind][i] = issued[0]
            if kind == "u":
                wv_, bw, ds = wu[i]
                S.op("pool", (lambda e, wv_=wv_, idx=idx: e.dma_start(out=wv_, in_=d.wup[l, idx])), writes=[bw] + wu_ov[i], dsem=ds)
            else:
                wv_, bw, ds = wd[i]
                S.op("pool", (lambda e, wv_=wv_, idx=idx: e.dma_start(out=wv_, in_=d.wdn[l, idx])), writes=[bw] + wd_ov[i], dsem=ds)
            slots[issued[0]] = (wv_, bw)
            issued[0] += 1
    PF = 4

    def next_w():
        issue(cons[0] + 1 + PF)
        assert cons[0] in slots, (cons[0], issued[0])
        r = slots.pop(cons[0])
        cons[0] += 1
        return r
    issue(PF)
    for wi, (s, e_) in enumerate(wins):
        W = e_ - s
        ms, me = max(s - 1, cs), min(e_ + 1, ce)
        if (me - ms) % 2 == 1:
            if me < ce:
                me += 1
            else:
                ms -= 1
        Wm = me - ms
        nleft = s - ms
        modnorm(k, X, bX, s, me, l, 1, b, HW, bHW, scr, ho=nleft)
        if nleft > 0:
            S.op("dve", (lambda e, nleft=nleft, s=s, e_=e_, ms=ms, W=W, Wm=Wm: e.tensor_copy(out=HW[:, :, 0:nleft], in_=HALO[:, :, 2 - nleft:2])),
                 reads=[bHALO], writes=[bHW])
        S.op("dve", (lambda e, s=s, e_=e_, ms=ms: e.tensor_copy(out=HALO[:, :, :], in_=HW[:, :, e_ - 2 - ms:e_ - ms])),
             reads=[bHW], writes=[bHALO])
        badl = LC if l == 0 else None
        badr = LC - 1 if l == 0 else None
        for fp in range(NFC):
            wv_, bw = next_w()
            par = fp % 2
            pv, pg = (0, 1) if par == 0 else (2, 3)
            for (pi, co) in ((pv, 0), (pg, 128)):
                for kk in range(8):
                    S.op("pe", (lambda e, pi=pi, co=co, kk=kk, wv_=wv_, s=s, e_=e_, ms=ms, W=W, Wm=Wm: e.matmul(
                        k.ps[pi][:, 0:Wm], wv_[:, kk, co:co + 128], HW[:, kk, 0:Wm], start=(kk == 0), stop=(kk == 7))),
                        reads=[bw, bHW], writes=[k.bps[pi]])
            for (pi, ci, (yt, byt)) in ((pv, fp, yv[par]), (pg, NFC + fp, yg[par])):
                psr = k.ps[pi]
                S.op("act", (lambda e, psr=psr, ci=ci, yt=yt, s=s, e_=e_, ms=ms, W=W, Wm=Wm: e.activation(
                    out=yt[:, 0:W], in_=psr[:, s - ms:e_ - ms], func=AF.Identity,
                    scale=k.cw[:, l, 1, ci:ci + 1], bias=k.cb[:, l, ci:ci + 1])),
                    reads=[k.bps[pi], k.bconst], writes=[byt])
                for (lo, hi) in excl_segs(max(s, cs + 1), e_, badl):
                    S.op("dve", (lambda e, psr=psr, ci=ci, yt=yt, lo=lo, hi=hi, s=s, e_=e_, ms=ms, W=W, Wm=Wm: e.scalar_tensor_tensor(
                        out=yt[:, lo - s:hi - s], in0=psr[:, lo - 1 - ms:hi - 1 - ms], scalar=k.cw[:, l, 0, ci:ci + 1],
                        in1=yt[:, lo - s:hi - s], op0=ALU.mult, op1=ALU.add)),
                        reads=[k.bps[pi], k.bconst, byt], writes=[byt])
                for (lo, hi) in excl_segs(s, min(e_, ce - 1), badr):
                    S.op("dve", (lambda e, psr=psr, ci=ci, yt=yt, lo=lo, hi=hi, s=s, e_=e_, ms=ms, W=W, Wm=Wm: e.scalar_tensor_tensor(
                        out=yt[:, lo - s:hi - s], in0=psr[:, lo + 1 - ms:hi + 1 - ms], scalar=k.cw[:, l, 2, ci:ci + 1],
                        in1=yt[:, lo - s:hi - s], op0=ALU.mult, op1=ALU.add)),
                        reads=[k.bps[pi], k.bconst, byt], writes=[byt])
            ygt, bygt = yg[par]
            yvt, byvt = yv[par]
            S.op("act", (lambda e, ygt=ygt, s=s, e_=e_, ms=ms, W=W, Wm=Wm: e.activation(out=ygt[:, 0:W], in_=ygt[:, 0:W], func=AF.Silu)),
                 reads=[bygt], writes=[bygt])
            S.op("pool", (lambda e, ygt=ygt, yvt=yvt, fp=fp, s=s, e_=e_, ms=ms, W=W, Wm=Wm: e.tensor_tensor(out=G[:, fp, 0:W], in0=yvt[:, 0:W], in1=ygt[:, 0:W],
                                                                                op=ALU.mult)),
                 reads=[bygt, byvt], writes=[bG[fp]])
        for dcn in range(8):
            wv_, bw = next_w()
            pi = 4 + dcn % 2
            for fc in range(NFC):
                S.op("pe", (lambda e, pi=pi, fc=fc, wv_=wv_, s=s, e_=e_, ms=ms, W=W, Wm=Wm: e.matmul(
                    k.ps[pi][:, 0:W], wv_[:, fc, :], G[:, fc, 0:W], start=(fc == 0), stop=(fc == NFC - 1))),
                    reads=[bw, bG[fc]], writes=[k.bps[pi]])
            S.op("act", (lambda e, pi=pi, dcn=dcn, s=s, e_=e_, ms=ms, W=W, Wm=Wm: e.activation(out=OST[:, dcn, 0:W], in_=k.ps[pi][:, 0:W], func=AF.Identity)),
                 reads=[k.bps[pi]], writes=[bOST])
        postnorm_update(k, X, bX, s, e_, l, 1, b, OST, bOST, scr)


def s5_layer(k, b, X, bX):
    nc, S, d = k.nc, k.S, k.d
    l = 1
    AR, AFa = k.AR, k.AF
    AR.reset()
    AFa.reset()
    st = k.stab
    bsp = Buf("xsp")
    for kk in range(8):
        S.op("sp", (lambda e, kk=kk: e.dma_start(out=d.xsp[:, kk, :], in_=X[:, kk, LC:NT])),
             reads=bX(LC, NT), writes=[bsp], dsem=k.dsem_x)
    Y = AR.take([8, LL]); bY = Buf("Y")
    o_ar = AR.o
    scr = make_scr(k, 256, 7)
    for w in range(NT // 256):
        c0 = w * 256
        modnorm(k, X, bX, c0, c0 + 256, l, 0, b, X, bX(c0, c0 + 256), scr, ho=c0)
    S.barrier()
    AR.o = o_ar
    AFa.reset()
    QN, SG = 4, 4
    NG = TCH // SG
    NS = 2
    bU = bX(0, NT)
    for kk in range(8):
        S.op("dve", (lambda e, kk=kk: e.tensor_scalar(out=Y[:, kk, :], in0=X[:, kk, LC:NT], scalar1=k.dsk[:, kk:kk + 1], scalar2=None,
                                                     op0=ALU.mult)), reads=bU + [k.bconst], writes=[bY])

    class Stream:
        pass

    sts = []
    for dd in range(NS):
        t = Stream()
        t.dd = dd
        t.BpT = AR.take([QN, 2, 128]); t.bBp = Buf(f"Bp{dd}")
        t.Cp = AR.take([QN, 2, 128]); t.bCp = Buf(f"Cp{dd}")
        t.BpF, t.CpF = t.BpT.bitcast(F32), t.Cp.bitcast(F32)
        t.XS = [AR.take([2 * QN * SG * 32]) for _ in range(2)]
        t.bXS = [Buf(f"XS{dd}0"), Buf(f"XS{dd}1")]
        t.XSF = [x.bitcast(F32) for x in t.XS]
        t.HsR = [AFa.take([SG + 1, QN, 32]) for _ in range(2)]
        t.HsI = [AFa.take([SG + 1, QN, 32]) for _ in range(2)]
        t.t2 = AFa.take([QN, 32]); t.bt2 = Buf(f"t2{dd}")
        t.t4 = AFa.take([QN, 32]); t.bt4 = Buf(f"t4{dd}")
        t.bHs = [Buf(f"Hs{dd}0"), Buf(f"Hs{dd}1")]
        t.HinR = AFa.take([QN, 32]); t.HinI = AFa.take([QN, 32]); t.bHin = Buf(f"Hin{dd}")
        t.h0R = AFa.take([QN]); t.h0I = AFa.take([QN]); t.bh0 = Buf(f"h0{dd}")
        t.hAR = AFa.take([QN]); t.hAI = AFa.take([QN]); t.bhA = Buf(f"hA{dd}")
        t.sm = [AFa.take([QN]) for _ in range(4)]; t.bsm = [Buf(f"sm{dd}{i}") for i in range(4)]
        t.SCR = AFa.take([QN, 32]); t.SCI = AFa.take([QN, 32]); t.bSC = Buf(f"SC{dd}")
        t.ptab = [(AFa.take([QN]), AFa.take([QN])) for _ in range(4)]; t.bpt = Buf(f"pt{dd}")
        t.XP = [k.ps[2 * dd], k.ps[2 * dd + 1]]
        t.bXP = [Buf(f"XPre{dd}"), Buf(f"XPim{dd}")]
        t.yps_i = 4 + dd
        t.bYps = Buf(f"Yps{dd}")
        t.gcount = 0
        t.islot = 0 if dd == 0 else SG
        t.eslot = SG if dd == 0 else 0
        t.lo_s = 1 if dd == 0 else 0
        sts.append(t)
    U2 = AFa.take([QN, 32]); bU2 = Buf("U2")
    V2 = AFa.take([QN, 32]); bV2 = Buf("V2")
    tC = AFa.take([QN, 128]); btC = Buf("tC")
    tD = AFa.take([QN, 128]); btD = Buf("tD")

    def cstep_small(t, oR, oI, hR, hI, eR, eI, reads, writes, ctxp=False):
        sm, bsm = t.sm, t.bsm
        aT, bT = (t.aX16, t.bX16) if ctxp else (t.aT16, t.bT16)
        S.op("dve", lambda e: e.tensor_tensor(out=sm[0], in0=aT, in1=hR, op=ALU.mult), reads=reads, writes=[bsm[0]])
        S.op("dve", lambda e: e.tensor_tensor(out=sm[1], in0=bT, in1=hI, op=ALU.mult), reads=reads, writes=[bsm[1]])
        S.op("pool", lambda e: e.tensor_tensor(out=sm[2], in0=bT, in1=hR, op=ALU.mult), reads=reads, writes=[bsm[2]])
        S.op("pool", lambda e: e.tensor_tensor(out=sm[3], in0=aT, in1=hI, op=ALU.mult), reads=reads, writes=[bsm[3]])
        S.op("dve", lambda e: e.tensor_tensor(out=sm[0], in0=sm[0], in1=sm[1], op=ALU.subtract), reads=[bsm[0], bsm[1]], writes=[bsm[0]])
        S.op("pool", lambda e: e.tensor_tensor(out=sm[2], in0=sm[2], in1=sm[3], op=ALU.add), reads=[bsm[2], bsm[3]], writes=[bsm[2]])
        S.op("dve", lambda e: e.tensor_tensor(out=oR, in0=sm[0], in1=eR, op=ALU.add), reads=[bsm[0]] + reads, writes=writes)
        S.op("pool", lambda e: e.tensor_tensor(out=oI, in0=sm[2], in1=eI, op=ALU.add), reads=[bsm[2]] + reads, writes=writes)

    def chunk_scan(t, ER, EI, n, aB, bB, reads):
        rev = t.dd == 1
        src, bsrc = (ER, EI), reads
        pa, pb = aB, bB
        bufs = [((t.SCR, t.SCI), t.bSC), ((t.HinR, t.HinI), t.bHin)]
        for kk in range(5):
            sh = 1 << kk
            (dR, dI), bd = bufs[kk % 2]
            w = n - sh
            if not rev:
                ld, ls, kp = slice(sh, n), slice(0, w), slice(0, sh)
            else:
                ld, ls, kp = slice(0, w), slice(sh, n), slice(w, n)
            pa_bc = pa.unsqueeze(2).broadcast_to([128, QN, w])
            pb_bc = pb.unsqueeze(2).broadcast_to([128, QN, w])
            sR, sI = src
            rd = bsrc + [t.bpt, k.bconst]
            S.op("dve", (lambda e, pa_bc=pa_bc, sR=sR, ls=ls, w=w: e.tensor_tensor(out=t.t2[:, :, 0:w], in0=pa_bc, in1=sR[:, :, ls], op=ALU.mult)), reads=rd, writes=[t.bt2])
            S.op("dve", (lambda e, pb_bc=pb_bc, sI=sI, ls=ls, w=w: e.tensor_tensor(out=U2[:, :, 0:w], in0=pb_bc, in1=sI[:, :, ls], op=ALU.mult)), reads=rd, writes=[bU2])
            S.op("pool", (lambda e, pb_bc=pb_bc, sR=sR, ls=ls, w=w: e.tensor_tensor(out=t.t4[:, :, 0:w], in0=pb_bc, in1=sR[:, :, ls], op=ALU.mult)), reads=rd, writes=[t.bt4])
            S.op("pool", (lambda e, pa_bc=pa_bc, sI=sI, ls=ls, w=w: e.tensor_tensor(out=V2[:, :, 0:w], in0=pa_bc, in1=sI[:, :, ls], op=ALU.mult)), reads=rd, writes=[bV2])
            S.op("dve", (lambda e, w=w: e.tensor_tensor(out=t.t2[:, :, 0:w], in0=t.t2[:, :, 0:w], in1=U2[:, :, 0:w], op=ALU.subtract)), reads=[t.bt2, bU2], writes=[t.bt2])
            S.op("pool", (lambda e, w=w: e.tensor_tensor(out=t.t4[:, :, 0:w], in0=t.t4[:, :, 0:w], in1=V2[:, :, 0:w], op=ALU.add)), reads=[t.bt4, bV2], writes=[t.bt4])
            S.op("dve", (lambda e, dR=dR, sR=sR, ld=ld, w=w: e.tensor_tensor(out=dR[:, :, ld], in0=t.t2[:, :, 0:w], in1=sR[:, :, ld], op=ALU.add)), reads=[t.bt2] + bsrc, writes=[bd])
            S.op("pool", (lambda e, dI=dI, sI=sI, ld=ld, w=w: e.tensor_tensor(out=dI[:, :, ld], in0=t.t4[:, :, 0:w], in1=sI[:, :, ld], op=ALU.add)), reads=[t.bt4] + bsrc, writes=[bd])
            S.op("dve", (lambda e, dR=dR, sR=sR, kp=kp: e.tensor_copy(out=dR[:, :, kp], in_=sR[:, :, kp])), reads=bsrc, writes=[bd])
            S.op("pool", (lambda e, dI=dI, sI=sI, kp=kp: e.tensor_copy(out=dI[:, :, kp], in_=sI[:, :, kp])), reads=bsrc, writes=[bd])
            src, bsrc = (dR, dI), [bd]
            if kk < 4:
                na, nb2 = t.ptab[kk]
                S.op("dve", (lambda e, pa=pa: e.tensor_tensor(out=t.sm[0], in0=pa, in1=pa, op=ALU.mult)), reads=[t.bpt, k.bconst], writes=[t.bsm[0]])
                S.op("dve", (lambda e, pb=pb: e.tensor_tensor(out=t.sm[1], in0=pb, in1=pb, op=ALU.mult)), reads=[t.bpt, k.bconst], writes=[t.bsm[1]])
                S.op("dve", (lambda e, pa=pa, pb=pb, nb2=nb2: e.scalar_tensor_tensor(out=nb2, in0=pa, scalar=2.0, in1=pb, op0=ALU.mult, op1=ALU.mult)),
                     reads=[t.bpt, k.bconst], writes=[t.bpt])
                S.op("dve", (lambda e, na=na: e.tensor_tensor(out=na, in0=t.sm[0], in1=t.sm[1], op=ALU.subtract)), reads=[t.bsm[0], t.bsm[1]], writes=[t.bpt])
                pa, pb = na, nb2
        return src

    def sweep(dc, cb0, nchk, use_init, contract, tch=TCH):
        Xl = X[:, dc, cb0:cb0 + tch * nchk].rearrange("p (c t) -> p c t", t=tch)
        Yl = Y[:, dc, :].rearrange("p (c t) -> p c t", t=TCH)

        def xsv(tt):
            return tt[:, 0:2 * QN * SG * nchk].rearrange("p (r q s c) -> p r q s c", r=2, q=QN, s=SG)

        for t in sts:
            hb = t.gcount % 2
            if not use_init:
                S.op("dve", (lambda e, t=t, hb=hb: e.memset(t.HsR[hb][:, t.islot, :, 0:nchk], 0.0)), writes=[t.bHs[hb]])
                S.op("pool", (lambda e, t=t, hb=hb: e.memset(t.HsI[hb][:, t.islot, :, 0:nchk], 0.0)), writes=[t.bHs[hb]])
            else:
                S.op("dve", (lambda e, t=t, hb=hb: e.tensor_copy(out=t.HsR[hb][:, t.islot, :, 0:nchk], in_=t.HinR[:, :, 0:nchk])),
                     reads=[t.bHin], writes=[t.bHs[hb]])
                S.op("pool", (lambda e, t=t, hb=hb: e.tensor_copy(out=t.HsI[hb][:, t.islot, :, 0:nchk], in_=t.HinI[:, :, 0:nchk])),
                     reads=[t.bHin], writes=[t.bHs[hb]])
        for g in range(tch // SG):
            ctx_ = []
            for t in sts:
                hb = t.gcount % 2
                xb = t.gcount % 2
                t.gcount += 1
                j_lo = g * SG if t.dd == 0 else tch - (g + 1) * SG
                rhs = Xl[:, :, j_lo:j_lo + SG].transpose([0, 2, 1])
                for ri in range(2):
                    xp = t.XP[ri][:, 0:QN * SG * nchk].rearrange("p (q s c) -> p q s c", s=SG, c=nchk)
                    for q in range(QN):
                        S.op("pe", (lambda e, t=t, xp=xp, q=q, ri=ri, rhs=rhs: e.matmul(
                            xp[:, q, :, :], t.BpF[:, q, ri, :], rhs, start=True, stop=True)),
                            reads=[t.bBp] + bU, writes=[t.bXP[ri]])
                    S.op("act", (lambda e, t=t, xp=xp, ri=ri, xb=xb: e.activation(
                        out=xsv(t.XS[xb])[:, ri, :, :, :], in_=xp[:, :, :, :], func=AF.Identity)),
                        reads=[t.bXP[ri]], writes=[t.bXS[xb]])
                ctx_.append((t, hb, xb, j_lo))
            for s_ in range(SG):
                ops = []
                for (t, hb, xb, j_lo) in ctx_:
                    rs, ws = (s_, s_ + 1) if t.dd == 0 else (SG - s_, SG - 1 - s_)
                    xs = s_ if t.dd == 0 else SG - 1 - s_
                    o = Stream()
                    o.t, o.hb, o.xb = t, hb, xb
                    o.a_bc = t.a16.unsqueeze(2).broadcast_to([128, QN, nchk])
                    o.b_bc = t.b16.unsqueeze(2).broadcast_to([128, QN, nchk])
                    o.hr, o.hi = t.HsR[hb][:, rs, :, 0:nchk], t.HsI[hb][:, rs, :, 0:nchk]
                    o.nr, o.ni = t.HsR[hb][:, ws, :, 0:nchk], t.HsI[hb][:, ws, :, 0:nchk]
                    o.xre, o.xim = xsv(t.XSF[xb])[:, 0, :, xs, :], xsv(t.XSF[xb])[:, 1, :, xs, :]
                    o.rd = [t.bHs[hb], k.bconst]
                    ops.append(o)
                for o in ops:
                    S.op("dve", (lambda e, o=o: e.tensor_tensor(out=o.nr, in0=o.a_bc, in1=o.hr, op=ALU.mult)), reads=o.rd, writes=[o.t.bHs[o.hb]])
                    S.op("dve", (lambda e, o=o: e.tensor_tensor(out=o.t.t2[:, :, 0:nchk], in0=o.b_bc, in1=o.hi, op=ALU.mult)), reads=o.rd, writes=[o.t.bt2])
                    S.op("pool", (lambda e, o=o: e.tensor_tensor(out=o.ni, in0=o.b_bc, in1=o.hr, op=ALU.mult)), reads=o.rd, writes=[o.t.bHs[o.hb]])
                    S.op("pool", (lambda e, o=o: e.tensor_tensor(out=o.t.t4[:, :, 0:nchk], in0=o.a_bc, in1=o.hi, op=ALU.mult)), reads=o.rd, writes=[o.t.bt4])
                for o in ops:
                    S.op("dve", (lambda e, o=o: e.tensor_tensor(out=o.nr, in0=o.nr, in1=o.t.t2[:, :, 0:nchk], op=ALU.subtract)),
                         reads=[o.t.bt2, o.t.bHs[o.hb]], writes=[o.t.bHs[o.hb]])
                    S.op("pool", (lambda e, o=o: e.tensor_tensor(out=o.ni, in0=o.ni, in1=o.t.t4[:, :, 0:nchk], op=ALU.add)),
                         reads=[o.t.bt4, o.t.bHs[o.hb]], writes=[o.t.bHs[o.hb]])
                for o in ops:
                    S.op("dve", (lambda e, o=o: e.tensor_tensor(out=o.nr, in0=o.nr, in1=o.xre, op=ALU.add)),
                         reads=[o.t.bXS[o.xb], o.t.bHs[o.hb]], writes=[o.t.bHs[o.hb]])
                    S.op("pool", (lambda e, o=o: e.tensor_tensor(out=o.ni, in0=o.ni, in1=o.xim, op=ALU.add)),
                         reads=[o.t.bXS[o.xb], o.t.bHs[o.hb]], writes=[o.t.bHs[o.hb]])
            for (t, hb, xb, j_lo) in ctx_:
                if contract:
                    yps = k.ps[t.yps_i][:, 0:SG * 32].rearrange("p (s c) -> p s c", c=32)
                    for q in range(QN):
                        S.op("pe", (lambda e, t=t, q=q, yps=yps, hb=hb: e.matmul(
                            yps[:, :, 0:nchk], t.CpF[:, q, 0, :], t.HsR[hb][:, t.lo_s:t.lo_s + SG, q, 0:nchk],
                            start=(q == 0), stop=False)), reads=[t.bCp, t.bHs[hb]], writes=[t.bYps])
                        S.op("pe", (lambda e, t=t, q=q, yps=yps, hb=hb: e.matmul(
                            yps[:, :, 0:nchk], t.CpF[:, q, 1, :], t.HsI[hb][:, t.lo_s:t.lo_s + SG, q, 0:nchk],
                            start=False, stop=(q == QN - 1))), reads=[t.bCp, t.bHs[hb]], writes=[t.bYps])
                    yv = Yl[:, :, j_lo:j_lo + SG].transpose([0, 2, 1])
                    S.op("dve", (lambda e, yv=yv, yps=yps: e.tensor_tensor(out=yv, in0=yv.bitcast(F32), in1=yps[:, :, 0:nchk],
                                                                            op=ALU.add)),
                         reads=[t.bYps, bY], writes=[bY])
                if g < tch // SG - 1:
                    nb_ = t.gcount % 2
                    S.op("dve", (lambda e, t=t, hb=hb, nb_=nb_: e.tensor_copy(out=t.HsR[nb_][:, t.islot, :, 0:nchk], in_=t.HsR[hb][:, t.eslot, :, 0:nchk])),
                         reads=[t.bHs[hb]], writes=[t.bHs[nb_]])
                    S.op("pool", (lambda e, t=t, hb=hb, nb_=nb_: e.tensor_copy(out=t.HsI[nb_][:, t.islot, :, 0:nchk], in_=t.HsI[hb][:, t.eslot, :, 0:nchk])),
                         reads=[t.bHs[hb]], writes=[t.bHs[nb_]])
        return [(t.gcount - 1) % 2 for t in sts]

    for dc in range(8):
        for t in sts:
            dd = t.dd
            S.op("pool", (lambda e, t=t, dd=dd, dc=dc: e.dma_start(out=t.BpT, in_=d.Bf[:, dd, dc * QN:dc * QN + QN])),
                 writes=[t.bBp], dsem=k.dsem_w[2 * dd])
            S.op("pool", (lambda e, t=t, dd=dd, dc=dc: e.dma_start(out=t.Cp, in_=d.Cf[:, dd, dc * QN:dc * QN + QN])),
                 writes=[t.bCp], dsem=k.dsem_w[2 * dd + 1])
            f0 = 2 * dc * QN + dd
            sl = slice(f0, f0 + 2 * QN - 1, 2)
            t.a16, t.b16 = st[:, 0, sl], st[:, 1, sl]
            t.aT16, t.bT16 = st[:, 2, sl], st[:, 3, sl]
            t.aX16, t.bX16 = st[:, 6, sl], st[:, 7, sl]
            fre = st[:, 4, sl].unsqueeze(2).broadcast_to([128, QN, 128])
            fim = st[:, 5, sl].unsqueeze(2).broadcast_to([128, QN, 128])
            cre, cim = t.CpF[:, :, 0, :], t.CpF[:, :, 1, :]
            Cp, bCp = t.Cp, t.bCp
            S.op("dve", (lambda e, fim=fim, cim=cim: e.tensor_tensor(out=tC, in0=cim, in1=fim, op=ALU.mult)), reads=[bCp, k.bconst], writes=[btC])
            S.op("dve", (lambda e, fre=fre, cre=cre: e.tensor_tensor(out=tD, in0=cre, in1=fre, op=ALU.mult)), reads=[bCp, k.bconst], writes=[btD])
            S.op("dve", lambda e: e.tensor_tensor(out=tD, in0=tD, in1=tC, op=ALU.subtract), reads=[btC, btD], writes=[btD])
            S.op("dve", (lambda e, fim=fim, cre=cre: e.tensor_tensor(out=tC, in0=cre, in1=fim, op=ALU.mult)), reads=[bCp, k.bconst, btD], writes=[btC])
            S.op("dve", (lambda e, fre=fre, cim=cim, Cp=Cp: e.tensor_tensor(out=Cp[:, :, 1, :], in0=cim, in1=fre, op=ALU.mult)), reads=[bCp, k.bconst], writes=[bCp])
            S.op("dve", (lambda e, cim=cim, Cp=Cp: e.scalar_tensor_tensor(out=Cp[:, :, 1, :], in0=cim, scalar=-1.0, in1=tC,
                                                                        op0=ALU.mult, op1=ALU.subtract)), reads=[btC, bCp], writes=[bCp])
            S.op("dve", (lambda e, Cp=Cp: e.tensor_copy(out=Cp[:, :, 0, :], in_=tD)), reads=[btD, bCp], writes=[bCp])
        hbs = sweep(dc, 0, LC // TCX, False, False, tch=TCX)
        ncx = LC // TCX
        for t, hb in zip(sts, hbs):
            GR, GI = chunk_scan(t, t.HsR[hb][:, t.eslot, :, 0:ncx], t.HsI[hb][:, t.eslot, :, 0:ncx], ncx, t.aX16, t.bX16, [t.bHs[hb]])
            cl = ncx - 1 if t.dd == 0 else 0
            S.op("dve", (lambda e, t=t, GR=GR, cl=cl: e.tensor_copy(out=t.h0R, in_=GR[:, :, cl])), reads=[t.bSC], writes=[t.bh0])
            S.op("pool", (lambda e, t=t, GI=GI, cl=cl: e.tensor_copy(out=t.h0I, in_=GI[:, :, cl])), reads=[t.bSC], writes=[t.bh0])
        nl = LL // TCH
        hbs = sweep(dc, LC, nl, False, False)
        for t, hb in zip(sts, hbs):
            cf = 0 if t.dd == 0 else nl - 1
            ER, EI = t.HsR[hb][:, t.eslot, :, 0:nl], t.HsI[hb][:, t.eslot, :, 0:nl]
            cstep_small(t, ER[:, :, cf], EI[:, :, cf], t.h0R, t.h0I, ER[:, :, cf], EI[:, :, cf],
                        [t.bh0, t.bHs[hb], k.bconst], [t.bHs[hb]])
            GR, GI = chunk_scan(t, ER, EI, nl, t.aT16, t.bT16, [t.bHs[hb]])
            if t.dd == 0:
                dsl, ssl = slice(1, nl), slice(0, nl - 1)
            else:
                dsl, ssl = slice(0, nl - 1), slice(1, nl)
            S.op("dve", (lambda e, t=t, GR=GR, dsl=dsl, ssl=ssl: e.tensor_copy(out=t.HinR[:, :, dsl], in_=GR[:, :, ssl])), reads=[t.bSC], writes=[t.bHin])
            S.op("pool", (lambda e, t=t, GI=GI, dsl=dsl, ssl=ssl: e.tensor_copy(out=t.HinI[:, :, dsl], in_=GI[:, :, ssl])), reads=[t.bSC], writes=[t.bHin])
            S.op("dve", (lambda e, t=t, cf=cf: e.tensor_copy(out=t.HinR[:, :, cf], in_=t.h0R)), reads=[t.bh0], writes=[t.bHin])
            S.op("pool", (lambda e, t=t, cf=cf: e.tensor_copy(out=t.HinI[:, :, cf], in_=t.h0I)), reads=[t.bh0], writes=[t.bHin])
        sweep(dc, LC, nl, True, True)
    S.barrier()
    for kk in range(8):
        S.op("act", (lambda e, kk=kk: e.activation(out=Y[:, kk, :], in_=Y[:, kk, :].bitcast(F32), func=AF.Gelu_apprx_tanh)),
             reads=[bY], writes=[bY])
    S.barrier()
    for kk in range(8):
        S.op("sp", (lambda e, kk=kk: e.dma_start(out=X[:, kk, LC:NT], in_=d.xsp[:, kk, :])),
             reads=[bsp], writes=bX(LC, NT), dsem=k.dsem_x)
    AR.o = o_ar
    AFa.reset()
    WG = 512
    wa = [(AR.take([8, 128]), Buf(f"wa{i}"), k.dsem_w[i]) for i in range(2)]
    wb = [(AR.take([8, 128]), Buf(f"wb{i}"), k.dsem_w[2 + i]) for i in range(2)]
    OST = AFa.take([8, WG]); bOST = Buf("OSTg")
    sg = [(AFa.take([WG]), Buf(f"sg{i}")) for i in range(2)]
    scr = make_scr(k, WG, 7)
    it = 0
    for w in range(LL // WG):
        c0 = w * WG
        for dcn in range(8):
            wa_, bwa, dsa = wa[it % 2]
            wb_, bwb, dsb = wb[it % 2]
            sg_, bsg = sg[it % 2]
            pa, pb = (0, 1) if it % 2 == 0 else (2, 3)
            it += 1
            S.op("pool", (lambda e, wa_=wa_, dcn=dcn: e.dma_start(out=wa_, in_=d.wglu[0, dcn])), writes=[bwa], dsem=dsa)
            S.op("pool", (lambda e, wb_=wb_, dcn=dcn: e.dma_start(out=wb_, in_=d.wglu[1, dcn])), writes=[bwb], dsem=dsb)
            for (pi, wt_, bw_) in ((pa, wa_, bwa), (pb, wb_, bwb)):
                for kk in range(8):
                    S.op("pe", (lambda e, pi=pi, wt_=wt_, kk=kk, c0=c0: e.matmul(
                        k.ps[pi][:, 0:WG], wt_[:, kk, :], Y[:, kk, c0:c0 + WG], start=(kk == 0), stop=(kk == 7))),
                        reads=[bw_, bY], writes=[k.bps[pi]])
            S.op("act", (lambda e, pb=pb, sg_=sg_: e.activation(out=sg_, in_=k.ps[pb][:, 0:WG], func=AF.Sigmoid)),
                 reads=[k.bps[pb]], writes=[bsg])
            S.op("dve", (lambda e, pa=pa, sg_=sg_, dcn=dcn: e.tensor_tensor(out=OST[:, dcn, :], in0=k.ps[pa][:, 0:WG], in1=sg_,
                                                                            op=ALU.mult)),
                 reads=[k.bps[pa], bsg], writes=[bOST])
        postnorm_update(k, X, bX, LC + c0, LC + c0 + WG, l, 0, b, OST, bOST, scr)


N_CORES = 8


def kernel(**inputs):
    inp = {k_: np.asarray(v) for k_, v in inputs.items()}
    sh = shared_inputs(inp)
    nc = build(stage=99, nb=2)
    in_maps = []
    for c in range(N_CORES):
        m = dict(sh)
        m.update(core_inputs(inp, c))
        in_maps.append(m)
    res = run_bass_kernel_spmd(nc, in_maps, core_ids=list(range(N_CORES)))
    out = np.empty((2 * N_CORES, LL, D), np.float32)
    for c in range(N_CORES):
        o = np.asarray(res.results[c]["outT"])
        out[2 * c] = o[0].T
        out[2 * c + 1] = o[1].T
    return out
```
